# Optimizing a Trainium2 kernel written in Bass

```python
import functools
import jax
import jax.numpy as jnp
from jax import lax
import numpy as np

D_MODEL = 1024
BATCH = 2
SEQ = 8192
DEPTH = 2
DEC_BATCH = 4
DEC_SEQ = 4096
PAST_LEN = 128

ATT_HD = 64
ATT_W = 3 * D_MODEL // 8
ATT_HEADS = ATT_W // ATT_HD
WINDOWS = (128, 512, 2048)
DILATIONS = (1, 4, 16)
CONV_W = D_MODEL // 4
CONV_K = 3
RW_HD = 64
RW = 3 * D_MODEL // 8
RW_HEADS = RW // RW_HD
DECAY_RANK = 64
A_RANK = 64
G_RANK = 128
RWKV_COLS = 3 * RW + 2 * DECAY_RANK + 2 * A_RANK + G_RANK
D_MIX = ATT_W + CONV_W + RW
D_IN_TOTAL = 3 * ATT_W + 3 * CONV_W + RWKV_COLS
D_FF = 2816
N_BUCKETS = 32
REL_MAX_DIST = 1024
ALPHA = (2 * DEPTH) ** 0.25
BETA = (8 * DEPTH) ** -0.25
LN_EPS = 1e-5
GN_EPS = 64e-5
NEG_INF = -1e9
F32 = jnp.float32

kernel_name = 'hymba_style_dilated_conv_rwkv7_encoder'


def _layer_norm(x, g, b, eps=LN_EPS):
    xf = x.astype(F32)
    mu = jnp.mean(xf, -1, keepdims=True)
    var = jnp.mean(jnp.square(xf - mu), -1, keepdims=True)
    return (xf - mu) * lax.rsqrt(var + eps) * g + b


def _modulate(x, mod, i):
    return x * (1 + mod[:, i, 1, None]) + mod[:, i, 0, None]


def _post_norm(x, delta, mod, i, g, b):
    return _layer_norm(ALPHA * x + (1 + mod[:, i, 2, None]) * delta, g, b).astype(x.dtype)


def _swiglu(h, w_in, w_out):
    gate, up = jnp.split(h @ w_in, 2, axis=-1)
    return (jax.nn.silu(gate) * up) @ w_out


def _t5_bucket(rel):
    half = N_BUCKETS // 2
    exact = half // 2
    n = np.abs(rel)
    large = exact + (np.log(np.maximum(n, exact) / exact) / np.log(REL_MAX_DIST / exact) * (half - exact)).astype(np.int64)
    large = np.minimum(large, half - 1)
    return (np.where(rel > 0, half, 0) + np.where(n < exact, n, large)).astype(np.int32)


def _dilated_branch(q, k, v, rel_bias, window, dilation):
    bt, s, h, dh = q.shape
    n_half = window // (2 * dilation)
    blk = n_half
    L = s // dilation
    nb = -(-L // blk)
    lp = nb * blk

    def to_sub(t):
        return t.reshape(bt, L, dilation, h, dh).transpose(0, 2, 1, 3, 4).reshape(bt * dilation, L, h, dh)

    qs, ks, vs = to_sub(q), to_sub(k), to_sub(v)
    qb = jnp.pad(qs, ((0, 0), (0, lp - L), (0, 0), (0, 0))).reshape(-1, nb, blk, h, dh)

    def neighbours(t):
        tp = jnp.pad(t, ((0, 0), (blk, lp - L + blk), (0, 0), (0, 0))).reshape(-1, nb + 2, blk, h, dh)
        return jnp.concatenate([tp[:, :-2], tp[:, 1:-1], tp[:, 2:]], axis=2)

    kb, vb = neighbours(ks), neighbours(vs)
    logits = jnp.einsum('nbqhd,nbkhd->nbhqk', qb, kb).astype(F32) * (dh ** -0.5)
    rel = np.arange(3 * blk)[None, :] - blk - np.arange(blk)[:, None]
    bias = jnp.take(rel_bias, jnp.asarray(_t5_bucket(rel * dilation)), axis=0)
    bias = jnp.transpose(bias, (2, 0, 1)).astype(F32)
    keypos = np.arange(nb)[:, None] * blk - blk + np.arange(3 * blk)[None, :]
    valid = (np.abs(rel) <= n_half)[None] & ((keypos >= 0) & (keypos < L))[:, None, :]
    logits = jnp.where(jnp.asarray(valid)[None, :, None], logits + bias, NEG_INF)
    m = jnp.max(logits, -1)
    p = jnp.exp(logits - m[..., None])
    l = jnp.sum(p, -1)
    o = jnp.einsum('nbhqk,nbkhd->nbqhd', p, vb.astype(F32)) / jnp.swapaxes(l, 2, 3)[..., None]

    def from_sub(t):
        t = t.reshape(bt, dilation, lp, *t.shape[3:])[:, :, :L]
        return jnp.swapaxes(t, 1, 2).reshape(bt, s, *t.shape[3:])

    return from_sub(jnp.swapaxes(m, 2, 3)), from_sub(jnp.swapaxes(l, 2, 3)), from_sub(o)


def _dilated_attention(q, k, v, rel_bias):
    bt, s = q.shape[:2]
    parts = [_dilated_branch(q, k, v, rel_bias, w, d) for w, d in zip(WINDOWS, DILATIONS)]
    m_max = functools.reduce(jnp.maximum, [m for m, _, _ in parts])
    wts = [l * jnp.exp(m - m_max) for m, l, _ in parts]
    num = sum(w[..., None] * o for w, (_, _, o) in zip(wts, parts))
    den = sum(wts)
    return (num / den[..., None]).reshape(bt, s, ATT_W)


def _short_conv(gb, gc, hin, conv_w):
    u = gc * hin
    up = jnp.pad(u, ((0, 0), (1, 1), (0, 0)))
    y = conv_w[0] * up[:, :-2] + conv_w[1] * up[:, 1:-1] + conv_w[2] * up[:, 2:]
    return gb * y


def _heads(t):
    return t.reshape(*t.shape[:-1], RW_HEADS, RW_HD)


def _rwkv7_bidir(z, mu, w0, w_up, a0, a_up, g_up, k_k, k_a, r_k, gn_g, gn_b):
    bt, s, _ = z.shape
    z = z.astype(F32)
    zp = jnp.pad(z, ((0, 0), (1, 1), (0, 0)))
    z = z + mu * (0.5 * (zp[:, :-2] + zp[:, 2:]) - z)
    r, k, v = z[..., :RW], z[..., RW:2 * RW], z[..., 2 * RW:3 * RW]
    o = 3 * RW
    wd = z[..., o:o + 2 * DECAY_RANK].reshape(bt, s, 2, DECAY_RANK)
    o += 2 * DECAY_RANK
    ad = z[..., o:o + 2 * A_RANK].reshape(bt, s, 2, A_RANK)
    o += 2 * A_RANK
    gd = z[..., o:o + G_RANK]
    w = w0 + jnp.einsum('bsjr,jrc->bsjc', jnp.tanh(wd), w_up)
    decay = jnp.exp(-jnp.exp(-jax.nn.softplus(-w) - 0.5))
    a = jax.nn.sigmoid(a0 + jnp.einsum('bsjr,jrc->bsjc', ad, a_up))
    g = jax.nn.sigmoid(gd) @ g_up
    kk = _heads(k * k_k)
    kk = kk / jnp.maximum(jnp.sqrt(jnp.sum(jnp.square(kk), -1, keepdims=True)), 1e-12)
    k_dir = _heads(k[:, :, None] * (1 + (a - 1) * k_a))
    a_h, decay_h = _heads(a), _heads(decay)
    r_h, k_h, v_h = _heads(r), _heads(k), _heads(v)

    def seq_major(t_f, t_b):
        return jnp.stack([t_f, t_b[:, ::-1]], axis=0).transpose(2, 0, 1, 3, 4)

    xs = (seq_major(r_h, r_h), seq_major(decay_h[:, :, 0], decay_h[:, :, 1]),
          seq_major(k_dir[:, :, 0], k_dir[:, :, 1]), seq_major(v_h, v_h),
          seq_major(kk, kk), seq_major(a_h[:, :, 0], a_h[:, :, 1]))

    def step(state, inp):
        r_t, w_t, k_t, v_t, kk_t, a_t = inp
        sa = jnp.einsum('dbhvk,dbhk->dbhv', state, kk_t)
        state = (state * w_t[..., None, :] - sa[..., :, None] * (kk_t * a_t)[..., None, :]
                 + v_t[..., :, None] * k_t[..., None, :])
        return state, jnp.einsum('dbhvk,dbhk->dbhv', state, r_t)

    state0 = jnp.zeros((2, bt, RW_HEADS, RW_HD, RW_HD), F32)
    _, ys = lax.scan(step, state0, xs)
    y = jnp.swapaxes(ys[:, 0] + ys[::-1, 1], 0, 1)
    ym = jnp.mean(y, -1, keepdims=True)
    yv = jnp.mean(jnp.square(y - ym), -1, keepdims=True)
    y = ((y - ym) * lax.rsqrt(yv + GN_EPS)).reshape(bt, s, RW) * gn_g + gn_b
    bonus = jnp.sum(r_h * k_h * r_k, -1, keepdims=True) * v_h
    return (y + bonus.reshape(bt, s, RW)) * g


def _mixer(h, w_in, w_out, rel_bias, conv_w, rw):
    bt, s, _ = h.shape
    u = h @ w_in
    q, k, v = [u[..., i * ATT_W:(i + 1) * ATT_W].reshape(bt, s, ATT_HEADS, ATT_HD) for i in range(3)]
    c0 = 3 * ATT_W
    gb, gc, hin = [u[..., c0 + i * CONV_W:c0 + (i + 1) * CONV_W] for i in range(3)]
    z = u[..., c0 + 3 * CONV_W:]
    att = _dilated_attention(q, k, v, rel_bias)
    conv = _short_conv(gb, gc, hin, conv_w).astype(F32)
    rwk = _rwkv7_bidir(z, *rw)
    return jnp.concatenate([att, conv, rwk], axis=-1) @ w_out


def _trunk(x, c, params):
    (w_ada, b_ada, ln_g, ln_b, ffn_w_in, ffn_w_out, w_mix_in, w_mix_out, rel_bias, conv_w,
     rwkv_mu, rwkv_w0, rwkv_w_up, rwkv_a0, rwkv_a_up, rwkv_g_up, rwkv_k_k, rwkv_k_a, rwkv_r_k,
     rwkv_gn_g, rwkv_gn_b) = params
    for l in range(DEPTH):
        mod = (c @ w_ada[l] + b_ada[l]).reshape(c.shape[0], 3, 3, D_MODEL)
        delta = 0.5 * _swiglu(_modulate(x, mod, 0), ffn_w_in[l, 0], ffn_w_out[l, 0])
        x = _post_norm(x, delta, mod, 0, ln_g[l, 0], ln_b[l, 0])
        rw = (rwkv_mu[l], rwkv_w0[l], rwkv_w_up[l], rwkv_a0[l], rwkv_a_up[l], rwkv_g_up[l],
              rwkv_k_k[l], rwkv_k_a[l], rwkv_r_k[l], rwkv_gn_g[l], rwkv_gn_b[l])
        delta = _mixer(_modulate(x, mod, 1), w_mix_in[l], w_mix_out[l], rel_bias, conv_w[l], rw)
        x = _post_norm(x, delta, mod, 1, ln_g[l, 1], ln_b[l, 1])
        delta = 0.5 * _swiglu(_modulate(x, mod, 2), ffn_w_in[l, 1], ffn_w_out[l, 1])
        x = _post_norm(x, delta, mod, 2, ln_g[l, 2], ln_b[l, 2])
    return x


def setup_inputs(seed: int = 0) -> dict:
    key = jax.random.key(seed)
    ks = jax.random.split(key, 25)
    nrm = jax.random.normal
    return {
        'x_prompt': nrm(ks[0], (BATCH, SEQ, D_MODEL), F32),
        'x_sample': nrm(ks[1], (DEC_BATCH, DEC_SEQ, D_MODEL), F32),
        'c_prompt': nrm(ks[2], (BATCH, D_MODEL), F32),
        'c_sample': nrm(ks[3], (DEC_BATCH, D_MODEL), F32),
        'w_ada': nrm(ks[4], (DEPTH, D_MODEL, 9 * D_MODEL), F32) * (0.1 * D_MODEL ** -0.5),
        'b_ada': nrm(ks[5], (DEPTH, 9 * D_MODEL), F32) * 0.01,
        'ln_g': 1.0 + 0.05 * nrm(ks[6], (DEPTH, 3, D_MODEL), F32),
        'ln_b': 0.02 * nrm(ks[7], (DEPTH, 3, D_MODEL), F32),
        'ffn_w_in': nrm(ks[8], (DEPTH, 2, D_MODEL, 2 * D_FF), F32) * D_MODEL ** -0.5,
        'ffn_w_out': nrm(ks[9], (DEPTH, 2, D_FF, D_MODEL), F32) * (BETA * D_FF ** -0.5),
        'w_mix_in': nrm(ks[10], (DEPTH, D_MODEL, D_IN_TOTAL), F32) * D_MODEL ** -0.5,
        'w_mix_out': nrm(ks[11], (DEPTH, D_MIX, D_MODEL), F32) * (BETA * D_MIX ** -0.5),
        'rel_bias': 0.2 * nrm(ks[12], (N_BUCKETS, ATT_HEADS), F32),
        'conv_w': nrm(ks[13], (DEPTH, CONV_K, CONV_W), F32) * CONV_K ** -0.5,
        'rwkv_mu': jax.random.uniform(ks[14], (DEPTH, RWKV_COLS), F32),
        'rwkv_w0': jax.random.uniform(ks[15], (DEPTH, 2, RW), F32, minval=-6.0, maxval=0.0),
        'rwkv_w_up': nrm(ks[16], (DEPTH, 2, DECAY_RANK, RW), F32) * (0.1 * DECAY_RANK ** -0.5),
        'rwkv_a0': 0.3 * nrm(ks[17], (DEPTH, 2, RW), F32),
        'rwkv_a_up': nrm(ks[18], (DEPTH, 2, A_RANK, RW), F32) * (0.1 * A_RANK ** -0.5),
        'rwkv_g_up': nrm(ks[19], (DEPTH, G_RANK, RW), F32) * G_RANK ** -0.5,
        'rwkv_k_k': 0.85 + 0.05 * nrm(ks[20], (DEPTH, RW), F32),
        'rwkv_k_a': 1.0 + 0.05 * nrm(ks[21], (DEPTH, RW), F32),
        'rwkv_r_k': 0.1 * nrm(ks[22], (DEPTH, RW_HEADS, RW_HD), F32),
        'rwkv_gn_g': 1.0 + 0.05 * nrm(ks[23], (DEPTH, RW), F32),
        'rwkv_gn_b': 0.02 * nrm(ks[24], (DEPTH, RW), F32),
    }


def reference(x_prompt, x_sample, c_prompt, c_sample, w_ada, b_ada, ln_g, ln_b, ffn_w_in, ffn_w_out,
              w_mix_in, w_mix_out, rel_bias, conv_w, rwkv_mu, rwkv_w0, rwkv_w_up, rwkv_a0, rwkv_a_up,
              rwkv_g_up, rwkv_k_k, rwkv_k_a, rwkv_r_k, rwkv_gn_g, rwkv_gn_b):
    params = (w_ada, b_ada, ln_g, ln_b, ffn_w_in, ffn_w_out, w_mix_in, w_mix_out, rel_bias, conv_w,
              rwkv_mu, rwkv_w0, rwkv_w_up, rwkv_a0, rwkv_a_up, rwkv_g_up, rwkv_k_k, rwkv_k_a, rwkv_r_k,
              rwkv_gn_g, rwkv_gn_b)
    y_prompt = _trunk(x_prompt, c_prompt, params)
    y_sample = _trunk(x_sample, c_sample, params)
    return (y_prompt, y_sample)
```

```python
import numpy as np
from contextlib import ExitStack, contextmanager
import concourse.bass as bass
import concourse.mybir as mybir
from concourse.bass_utils import run_bass_kernel_spmd

F32 = mybir.dt.float32
BF16 = mybir.dt.bfloat16
AF = mybir.ActivationFunctionType
ALU = mybir.AluOpType

D = 1024
DFF = 2816
DEPTH = 2
ALPHA = (2 * DEPTH) ** 0.25
LN_EPS = 1e-5
EPOCH = 16000


class Buf:
    __slots__ = ("w", "r", "name")

    def __init__(self, name=""):
        self.w = None
        self.r = []
        self.name = name


class Eng:
    def __init__(self, fw, name, eng, order_free=False):
        self.fw = fw
        self.name = name
        self.eng = eng
        self.n = 0
        self.sems = []
        self.seen = {}
        self.order_free = order_free

    def sem_val(self, n):
        ep = (n - 1) // EPOCH
        while len(self.sems) <= ep:
            self.sems.append(self.fw.new_sem(f"{self.name}_e{len(self.sems)}"))
        return self.sems[ep], n - ep * EPOCH


class DmaQ:
    def __init__(self, fw, name, eng_wrapper, nslots):
        self.name = name
        self.E = eng_wrapper
        self.nslots = nslots
        self.sems = [fw.new_sem(f"dq_{name}_{i}") for i in range(nslots)]
        self.cnt = [0] * nslots
        self.next = 0


class FW:
    def __init__(self, nc, stack):
        self.nc = nc
        self.stack = stack
        self.pe = Eng(self, "pe", nc.tensor, order_free=True)
        self.act = Eng(self, "act", nc.scalar)
        self.dve = Eng(self, "dve", nc.vector)
        self.pool = Eng(self, "pool", nc.gpsimd)
        self.sp = Eng(self, "sp", nc.sync)
        self.engs = {e.name: e for e in (self.pe, self.act, self.dve, self.pool, self.sp)}
        self.q_sp = DmaQ(self, "sp", self.sp, 24)
        self.q_pool = DmaQ(self, "pool", self.pool, 12)
        self.ninstr = 0

    def new_sem(self, name):
        return self.stack.enter_context(self.nc.semaphore(name))

    def _wait(self, E, tok):
        if tok[0] == "e":
            src = self.engs[tok[1]]
            n = tok[2]
            if src is E and E.order_free:
                return
            key = tok[1]
            if E.seen.get(key, 0) >= n:
                return
            sem, val = src.sem_val(n)
            E.eng.wait_ge(sem, val)
            E.seen[key] = n
        else:
            q, slot, n = tok[1], tok[2], tok[3]
            key = (q.name, slot)
            if E.seen.get(key, 0) >= n:
                return
            E.eng.wait_ge(q.sems[slot], 16 * n)
            E.seen[key] = n

    def _deps(self, E, reads, writes):
        toks = []
        for b in reads:
            if b.w is not None:
                toks.append(b.w)
        for b in writes:
            if b.w is not None:
                toks.append(b.w)
            toks.extend(b.r)
        best = {}
        for t in toks:
            key = t[:-1]
            if key not in best or best[key][-1] < t[-1]:
                best[key] = t
        for t in best.values():
            self._wait(E, t)

    def _commit(self, tok, reads, writes):
        for b in reads:
            b.r.append(tok)
            if len(b.r) > 48:
                best = {}
                for t in b.r:
                    k = t[:-1]
                    if k not in best or best[k][-1] < t[-1]:
                        best[k] = t
                b.r = list(best.values())
        for b in writes:
            b.w = tok
            b.r = []

    def op(self, E, fn, reads=(), writes=()):
        self._deps(E, reads, writes)
        ins = fn()
        E.n += 1
        sem, _ = E.sem_val(E.n)
        ins.then_inc(sem, 1)
        tok = ("e", E.name, E.n)
        self._commit(tok, reads, writes)
        self.ninstr += 1
        return tok

    def dma(self, q, out, in_, reads=(), writes=(), **kw):
        E = q.E
        slot = q.next
        q.next = (q.next + 1) % q.nslots
        if q.cnt[slot] > 0:
            self._wait(E, ("d", q, slot, q.cnt[slot]))
        self._deps(E, reads, writes)
        ins = E.eng.dma_start(out=out, in_=in_, **kw)
        q.cnt[slot] += 1
        ins.then_inc(q.sems[slot], 16)
        tok = ("d", q, slot, q.cnt[slot])
        self._commit(tok, reads, writes)
        self.ninstr += 1
        return tok

    def barrier(self):
        for E in self.engs.values():
            for q in (self.q_sp, self.q_pool):
                for sl in range(q.nslots):
                    if q.cnt[sl] > 0:
                        self._wait(E, ("d", q, sl, q.cnt[sl]))
            for e in self.engs.values():
                if e is not E and e.n > 0:
                    self._wait(E, ("e", e.name, e.n))

    def finish(self):
        for q in (self.q_sp, self.q_pool):
            for s in range(q.nslots):
                if q.cnt[s] > 0:
                    self._wait(self.sp, ("d", q, s, q.cnt[s]))
        for e in self.engs.values():
            if e is not self.sp and e.n > 0:
                self._wait(self.sp, ("e", e.name, e.n))


def pp_layout():
    off = {}
    n = 0

    def add(name, cols):
        nonlocal n
        off[name] = n
        n += cols

    add("b_ada", DEPTH * 72)
    add("ln_g", DEPTH * 3 * 8)
    add("ln_b", DEPTH * 3 * 8)
    add("conv_w", DEPTH * 3 * 2)
    add("mu", DEPTH * 12)
    add("w0", DEPTH * 2 * 3)
    add("a0", DEPTH * 2 * 3)
    add("k_k", DEPTH * 3)
    add("k_a", DEPTH * 3)
    add("r_k", DEPTH * 3)
    add("gn_g", DEPTH * 3)
    add("gn_b", DEPTH * 3)
    return off, n


def pack_pp(inp):
    off, n = pp_layout()
    pp = np.zeros((128, n), np.float32)

    def put(name, arr2d):
        a = np.asarray(arr2d, np.float32).reshape(-1, 128).T
        pp[:, off[name]:off[name] + a.shape[1]] = a

    put("b_ada", inp["b_ada"])
    put("ln_g", inp["ln_g"])
    put("ln_b", inp["ln_b"])
    put("conv_w", inp["conv_w"])
    put("mu", inp["rwkv_mu"])
    put("w0", inp["rwkv_w0"])
    put("a0", inp["rwkv_a0"])
    put("k_k", inp["rwkv_k_k"])
    put("k_a", inp["rwkv_k_a"])
    put("r_k", inp["rwkv_r_k"])
    put("gn_g", inp["rwkv_gn_g"])
    put("gn_b", inp["rwkv_gn_b"])
    return pp


N_BUCKETS = 32
REL_MAX_DIST = 1024
DILS = (1, 4, 16)


def _t5_bucket(rel):
    half = N_BUCKETS // 2
    exact = half // 2
    n = np.abs(rel)
    large = exact + (np.log(np.maximum(n, exact) / exact) / np.log(REL_MAX_DIST / exact) * (half - exact)).astype(np.int64)
    large = np.minimum(large, half - 1)
    return (np.where(rel > 0, half, 0) + np.where(n < exact, n, large)).astype(np.int32)


def static_consts():
    oh = np.zeros((32, 3 * 512), np.float32)
    rel = np.arange(-64, 65)
    for bi, d in enumerate(DILS):
        b = _t5_bucket(rel * d)
        oh[b, bi * 512 + 256 + rel] = 1.0
    jx = np.ascontiguousarray(np.eye(128, dtype=np.float32)[::-1])
    return oh, jx


def static_cst():
    c = np.zeros((128, 1280), np.float32)
    i = np.arange(128)[:, None]
    t = np.arange(128)[None, :]
    c[:, 0:128] = (i < t)
    c[:, 128:256] = (i <= t)
    c[:, 256:384] = (i > t)
    c[:, 384:512] = (i >= t)
    c[:, 512:640] = np.eye(128)
    c[:, 640:768] = ((i // 64) == (t // 64))
    r = np.ones(512, np.float32)
    r[0::128] = 0.0
    c[:, 768:1280] = r[None, :]
    return c


EXTRA_INPUTS = ("rwkv_w_up", "rwkv_a_up", "rwkv_g_up", "rwkv_gn_g", "rwkv_gn_b")


class K:
    pass


def build(SEG=4096, depth=DEPTH, stop_after=None, T=256, no_rwkv=False):
    NT = 2 * SEG
    nc = bass.Bass("TRN2", target_bir_lowering=False)
    ppoff, NPP = pp_layout()
    x_in = nc.dram_tensor("x", [128, 8, NT], F32, kind="ExternalInput").ap()
    cT_in = nc.dram_tensor("cT", [128, 8, 2], F32, kind="ExternalInput").ap()
    pp_in = nc.dram_tensor("pp", [128, NPP], F32, kind="ExternalInput").ap()
    w_ada = nc.dram_tensor("w_ada", [DEPTH, D, 9 * D], F32, kind="ExternalInput").ap()
    ffn_w_in = nc.dram_tensor("ffn_w_in", [DEPTH, 2, D, 2 * DFF], F32, kind="ExternalInput").ap()
    ffn_w_out = nc.dram_tensor("ffn_w_out", [DEPTH, 2, DFF, D], F32, kind="ExternalInput").ap()
    y_out = nc.dram_tensor("y", [128, 8, NT], F32, kind="ExternalOutput").ap()
    xs = [nc.dram_tensor(f"xs{i}", [128, 8, NT], F32, kind="Internal").ap() for i in range(3)]
    PADV = 1024
    PADK = 1024
    lam_in = nc.dram_tensor("lam", [128, 1], F32, kind="ExternalInput").ap()
    w_mix_in = nc.dram_tensor("w_mix_in", [DEPTH, D, 3456], F32, kind="ExternalInput").ap()
    w_mix_out = nc.dram_tensor("w_mix_out", [DEPTH, D, D], F32, kind="ExternalInput").ap()
    rel_bias_in = nc.dram_tensor("rel_bias", [32, 6], F32, kind="ExternalInput").ap()
    oh_in = nc.dram_tensor("oh", [32, 1536], F32, kind="ExternalInput").ap()
    jx_in = nc.dram_tensor("jx", [128, 128], F32, kind="ExternalInput").ap()
    qT_d = nc.dram_tensor("qT_d", [128, 3, NT], BF16, kind="Internal").ap()
    kT_d = nc.dram_tensor("kT_d", [128, 3, NT], BF16, kind="Internal").ap()
    V_t = nc.dram_tensor("V_d", [NT + 2 * PADV, 384], BF16, kind="Internal")
    V_d = V_t.ap()
    gb_d = nc.dram_tensor("gb_d", [128, 2, NT], F32, kind="Internal").ap()
    uc_d = nc.dram_tensor("uc_d", [128, 2, NT], F32, kind="Internal").ap()
    z_d = nc.dram_tensor("z_d", [128, 12, NT], F32, kind="Internal").ap()
    mix_d = nc.dram_tensor("mix_d", [128, 8, NT], BF16, kind="Internal").ap()
    rw_w_up = nc.dram_tensor("rwkv_w_up", [DEPTH, 2, 64, 384], F32, kind="ExternalInput").ap()
    rw_a_up = nc.dram_tensor("rwkv_a_up", [DEPTH, 2, 64, 384], F32, kind="ExternalInput").ap()
    rw_g_up = nc.dram_tensor("rwkv_g_up", [DEPTH, 128, 384], F32, kind="ExternalInput").ap()
    rw_gn_g_t = nc.dram_tensor("rwkv_gn_g", [DEPTH, 384], F32, kind="ExternalInput")
    rw_gn_b_t = nc.dram_tensor("rwkv_gn_b", [DEPTH, 384], F32, kind="ExternalInput")
    cst_in = nc.dram_tensor("cst", [128, 1280], F32, kind="ExternalInput").ap()
    NCH_ = NT // 128
    rv_d = nc.dram_tensor("rv_d", [128, 3, NT], BF16, kind="Internal").ap()
    rg_d = nc.dram_tensor("rg_d", [128, 3, NT], F32, kind="Internal").ap()
    rbon_d = nc.dram_tensor("rbon_d", [128, 3, NT], F32, kind="Internal").ap()
    rpend_d = [nc.dram_tensor(f"rpend_d{i}", [128, 3, NCH_], F32, kind="Internal").ap() for i in range(2)]
    rk_d = {(dr, k): nc.dram_tensor(f"rk_d{dr}{k}", [128, 3, NT], BF16, kind="Internal").ap()
            for dr in range(2) for k in ("A", "B", "K", "R", "Bh", "Kh")}
    wd_t = nc.dram_tensor("wd_d", [6, 1536], F32, kind="Internal")
    wd_d = wd_t.ap()

    NTILE = NT // T
    with ExitStack() as st:
        fw = FW(nc, st)

        stk = [st]
        uniq = [0]

        def sb(name, shape, dt):
            uniq[0] += 1
            return stk[-1].enter_context(nc.sbuf_tensor(f"s_{name}_{uniq[0]}", shape, dt))

        @contextmanager
        def phase():
            fw.barrier()
            with ExitStack() as ph:
                stk.append(ph)
                yield
                stk.pop()
                fw.barrier()

        def psum(name, shape, dt=F32):
            return st.enter_context(nc.psum_tensor("p_" + name, shape, dt))

        wl_cnt = [0]

        def wload(dst, src_ap, stg, b_stg, b_dst):
            k = wl_cnt[0]
            wl_cnt[0] += 1
            si = k % len(stg)
            fw.dma(fw.q_sp, stg[si], src_ap, writes=[b_stg[si]])
            if k % 2 == 0:
                fw.op(fw.pool, lambda: nc.gpsimd.tensor_copy(out=dst, in_=stg[si]), reads=[b_stg[si]], writes=[b_dst])
            else:
                fw.op(fw.act, lambda: nc.scalar.activation(out=dst, in_=stg[si], func=AF.Copy), reads=[b_stg[si]],
                      writes=[b_dst])

        pp = sb("pp", [128, NPP], F32)
        b_pp = Buf("pp")
        fw.dma(fw.q_sp, pp[:], pp_in[:, :], writes=[b_pp])
        cT = sb("cT", [128, 8, 2], F32)
        b_cT = Buf("cT")
        fw.dma(fw.q_sp, cT[:], cT_in[:, :, :], writes=[b_cT])
        cTb = sb("cTb", [128, 8, 2], BF16)
        fw.op(fw.dve, lambda: nc.vector.tensor_copy(out=cTb[:], in_=cT[:]), reads=[b_cT], writes=[b_cT])
        ones_bf = sb("ones_bf", [128, 128], BF16)
        b_ones = Buf("ones")
        fw.op(fw.dve, lambda: nc.vector.memset(ones_bf[:], 1.0), writes=[b_ones])

        modv = sb("modv", [128, DEPTH, 72, 2], F32)
        b_mod = Buf("mod")
        pall = psum("all", [128, 4096], F32)
        banks = [pall[:, i * 512:(i + 1) * 512] for i in range(8)]
        bbank = [Buf(f"bank{i}") for i in range(8)]

        WA_COLS = 1152
        ada_ph = phase()
        ada_ph.__enter__()
        wa = sb("wa", [128, 8, WA_COLS], BF16)
        b_wa = Buf("wa")
        wast = [sb(f"wast{i}", [128, WA_COLS], F32) for i in range(2)]
        b_wast = [Buf(), Buf()]
        for l in range(depth):
            for piece in range(9 * D // WA_COLS):
                for kc in range(8):
                    wload(wa[:, kc, :], w_ada[l, kc * 128:(kc + 1) * 128, piece * WA_COLS:(piece + 1) * WA_COLS],
                          [w_[:] for w_ in wast], b_wast, b_wa)
                pb = banks[piece % 2]
                bpb = bbank[piece % 2]
                nj = WA_COLS // 128
                for j in range(nj):
                    for kc in range(8):
                        fw.op(fw.pe, lambda j=j, kc=kc: nc.tensor.matmul(
                            pb[:, j * 2:(j + 1) * 2], wa[:, kc, j * 128:(j + 1) * 128], cTb[:, kc, :],
                            start=(kc == 0), stop=(kc == 7)), reads=[b_wa, b_cT], writes=[bpb])
                for j in range(nj):
                    jj = piece * nj + j
                    col = ppoff["b_ada"] + l * 72 + jj
                    fw.op(fw.dve, lambda j=j, jj=jj, col=col: nc.vector.tensor_scalar(
                        out=modv[:, l, jj, :], in0=pb[:, j * 2:(j + 1) * 2], scalar1=pp[:, col:col + 1],
                        scalar2=None, op0=ALU.add), reads=[bpb, b_pp], writes=[b_mod])
        ada_ph.__exit__(None, None, None)
        sc1 = sb("sc1", [128, DEPTH, 3, 8, 2], F32)
        cg = sb("cg", [128, DEPTH, 3, 8, 2], F32)
        for l in range(depth):
            for i in range(3):
                coef = (0.5 if i != 1 else 1.0) / ALPHA
                j1 = (i * 3 + 1) * 8
                j2 = (i * 3 + 2) * 8
                fw.op(fw.dve, lambda l=l, i=i, j1=j1: nc.vector.tensor_scalar(
                    out=sc1[:, l, i, :, :], in0=modv[:, l, j1:j1 + 8, :], scalar1=1.0, scalar2=None, op0=ALU.add),
                    reads=[b_mod], writes=[b_mod])
                fw.op(fw.dve, lambda l=l, i=i, j2=j2, coef=coef: nc.vector.tensor_scalar(
                    out=cg[:, l, i, :, :], in0=modv[:, l, j2:j2 + 8, :], scalar1=1.0, scalar2=coef,
                    op0=ALU.add, op1=ALU.mult), reads=[b_mod], writes=[b_mod])

        def shift_ap(l, i, c, s):
            j = (i * 3 + 0) * 8 + c
            return modv[:, l, j, s:s + 1]

        class TB:
            pass
        tb = TB()

        def alloc_tb(small=False):
            tb.xt = [sb(f"xt{i}", [128, 8, T], F32) for i in range(2)]
            tb.b_xt = [Buf(f"xt{i}") for i in range(2)]
            tb.ht = [sb(f"ht{i}", [128, 8, T], BF16) for i in range(2)]
            tb.b_ht = [Buf(f"ht{i}") for i in range(2)]
            tb.vt = sb("vt", [128, 8, T], F32)
            tb.b_vt = Buf("vt")
            if not small:
                tb.vb = sb("vb", [128, 8, T], BF16)
                tb.b_vb = Buf("vb")
                tb.sqb = sb("sqb", [128, 8, T], BF16)
            tb.st_m = sb("st_m", [128, T], F32)
            tb.st_r = sb("st_r", [128, T], F32)
            tb.st_q = sb("st_q", [128, T], F32)
            tb.b_st = Buf("st")

        def load_x(src, ti, par):
            fw.dma(fw.q_sp, tb.xt[par][:], src[:, :, ti * T:(ti + 1) * T], reads=[src_buf(src, ti)],
                   writes=[tb.b_xt[par]])

        dram_bufs = {}

        def src_buf(ap, ti):
            key = (ap.name, ti)
            if key not in dram_bufs:
                dram_bufs[key] = Buf(str(key))
            return dram_bufs[key]

        def modulate(l, i, ti, par):
            s = (ti * T) // SEG
            for c in range(8):
                E = fw.act if c % 2 == 0 else fw.dve
                if E is fw.act:
                    fw.op(E, lambda c=c: nc.scalar.activation(
                        out=tb.ht[par][:, c, :], in_=tb.xt[par][:, c, :], func=AF.Identity,
                        scale=sc1[:, l, i, c, s:s + 1], bias=shift_ap(l, i, c, s)),
                        reads=[tb.b_xt[par], b_mod], writes=[tb.b_ht[par]])
                else:
                    fw.op(E, lambda c=c: nc.vector.tensor_scalar(
                        out=tb.ht[par][:, c, :], in0=tb.xt[par][:, c, :], scalar1=sc1[:, l, i, c, s:s + 1],
                        scalar2=shift_ap(l, i, c, s), op0=ALU.mult, op1=ALU.add),
                        reads=[tb.b_xt[par], b_mod], writes=[tb.b_ht[par]])

        def drain(gens):
            live = list(gens)
            while live:
                for g in list(live):
                    try:
                        next(g)
                    except StopIteration:
                        live.remove(g)

        def post_norm_gen(l, i, ti, dst, vbt, sqt, b_vs, vt_=None, b_vt_=None, fast=False):
            eps = LN_EPS / (ALPHA * ALPHA)
            if vt_ is None:
                vt_, b_vt_ = tb.vt, tb.b_vt
            if fast:
                fw.op(fw.act, lambda: nc.scalar.activation(out=vbt, in_=vt_[:], func=AF.Copy), reads=[b_vt_],
                      writes=[b_vs])
                fw.op(fw.dve, lambda: nc.vector.tensor_tensor(out=sqt, in0=vt_[:], in1=vt_[:], op=ALU.mult),
                      reads=[b_vt_], writes=[b_vs])
                yield
            else:
                fw.op(fw.pool, lambda: nc.gpsimd.tensor_copy(out=vbt, in_=vt_[:]), reads=[b_vt_], writes=[b_vs])
                fw.op(fw.pool, lambda: nc.gpsimd.tensor_tensor(out=sqt, in0=vt_[:], in1=vt_[:], op=ALU.mult),
                      reads=[b_vt_], writes=[b_vs])
                for _ in range(9):
                    yield
            sbk = banks[7]
            bs = bbank[7]
            for c in range(8):
                fw.op(fw.pe, lambda: nc.tensor.matmul(sbk[:, 0:T], ones_bf[:], vbt[:, c, :],
                                                       start=(c == 0), stop=(c == 7)),
                      reads=[b_ones, b_vs], writes=[bs])
            for c in range(8):
                fw.op(fw.pe, lambda: nc.tensor.matmul(sbk[:, T:2 * T], ones_bf[:], sqt[:, c, :],
                                                       start=(c == 0), stop=(c == 7)),
                      reads=[b_ones, b_vs], writes=[bs])
            yield
            fw.op(fw.act, lambda: nc.scalar.activation(out=tb.st_m[:], in_=sbk[:, 0:T], func=AF.Copy, scale=1.0 / D),
                  reads=[bs], writes=[tb.b_st])
            fw.op(fw.dve, lambda: nc.vector.tensor_tensor(out=tb.st_q[:], in0=tb.st_m[:], in1=tb.st_m[:], op=ALU.mult),
                  reads=[tb.b_st], writes=[tb.b_st])
            yield
            fw.op(fw.dve, lambda: nc.vector.scalar_tensor_tensor(
                out=tb.st_q[:], in0=sbk[:, T:2 * T], scalar=1.0 / D, in1=tb.st_q[:], op0=ALU.mult, op1=ALU.subtract),
                reads=[bs, tb.b_st], writes=[tb.b_st])
            fw.op(fw.dve, lambda: nc.vector.tensor_scalar(out=tb.st_q[:], in0=tb.st_q[:], scalar1=eps, scalar2=None,
                                                           op0=ALU.add), reads=[tb.b_st], writes=[tb.b_st])
            yield
            fw.op(fw.act, lambda: nc.scalar.activation(out=tb.st_q[:], in_=tb.st_q[:], func=AF.Sqrt),
                  reads=[tb.b_st], writes=[tb.b_st])
            yield
            fw.op(fw.dve, lambda: nc.vector.reciprocal(out=tb.st_r[:], in_=tb.st_q[:]), reads=[tb.b_st], writes=[tb.b_st])
            fw.op(fw.dve, lambda: nc.vector.tensor_tensor(
                out=vt_[:], in0=vt_[:], in1=tb.st_m[:].unsqueeze(1).to_broadcast([128, 8, T]), op=ALU.subtract),
                reads=[b_vt_, tb.b_st], writes=[b_vt_])
            yield
            if fast:
                fw.op(fw.dve, lambda: nc.vector.tensor_tensor(
                    out=vt_[:], in0=vt_[:], in1=tb.st_r[:].unsqueeze(1).to_broadcast([128, 8, T]), op=ALU.mult),
                    reads=[b_vt_, tb.b_st], writes=[b_vt_])
            else:
                fw.op(fw.pool, lambda: nc.gpsimd.tensor_tensor(
                    out=vt_[:], in0=vt_[:], in1=tb.st_r[:].unsqueeze(1).to_broadcast([128, 8, T]), op=ALU.mult),
                    reads=[b_vt_, tb.b_st], writes=[b_vt_])
            yield
            gcol = ppoff["ln_g"] + (l * 3 + i) * 8
            bcol = ppoff["ln_b"] + (l * 3 + i) * 8
            for c in range(8):
                if c % 2 == 0 and not fast:
                    fw.op(fw.pool, lambda: nc.gpsimd.tensor_scalar(
                        out=vt_[:, c, :], in0=vt_[:, c, :], scalar1=pp[:, gcol + c:gcol + c + 1],
                        scalar2=pp[:, bcol + c:bcol + c + 1], op0=ALU.mult, op1=ALU.add),
                        reads=[b_vt_, b_pp], writes=[b_vt_])
                else:
                    fw.op(fw.act, lambda: nc.scalar.activation(
                        out=vt_[:, c, :], in_=vt_[:, c, :], func=AF.Identity,
                        scale=pp[:, gcol + c:gcol + c + 1], bias=pp[:, bcol + c:bcol + c + 1]),
                        reads=[b_vt_, b_pp], writes=[b_vt_])
                    yield
            fw.dma(fw.q_sp, dst[:, :, ti * T:(ti + 1) * T], vt_[:], reads=[b_vt_],
                   writes=[src_buf(dst, ti)])

        def post_norm(l, i, ti, par, dst):
            drain([post_norm_gen(l, i, ti, dst, tb.vb[:], tb.sqb[:], tb.b_vb)])

        def ffn_pass(l, j, src, dst):
            with phase():
                _ffn_pass(l, j, src, dst)

        def _ffn_pass(l, j, src, dst):
            alloc_tb(small=True)
            w_in_sb = sb("w_in_sb", [128, 8, 2 * DFF], BF16)
            w_out_sb = sb("w_out_sb", [128, 22, D], BF16)
            b_win = Buf("win")
            b_wout = Buf("wout")
            hid = [sb(f"hid{i}", [128, 22, T], BF16) for i in range(2)]
            b_hid = [Buf(f"hid{i}") for i in range(2)]
            sg = [sb(f"sg{i}", [128, T], F32) for i in range(2)]
            b_sg = [Buf(f"sg{i}") for i in range(2)]
            i = 0 if j == 0 else 2
            stg_in = [h_[:].rearrange("p a b -> p (a b)").bitcast(F32) for h_ in hid]
            for kc in range(8):
                for hf in range(2):
                    wload(w_in_sb[:, kc, hf * DFF:(hf + 1) * DFF],
                          ffn_w_in[l, j, kc * 128:(kc + 1) * 128, hf * DFF:(hf + 1) * DFF], stg_in, b_hid, b_win)
            stg_out = [v_[:, 0:2048].rearrange("p (a b) -> p a b", a=2) for v_ in stg_in]
            for f2 in range(11):
                wload(w_out_sb[:, 2 * f2:2 * f2 + 2, :],
                      ffn_w_out[l, j, f2 * 256:(f2 + 1) * 256, :].rearrange("(a p) d -> p a d", p=128),
                      stg_out, b_hid, b_wout)

            def S1a(ti):
                load_x(src, ti, ti % 2)
                modulate(l, i, ti, ti % 2)

            def S1b(ti):
                par = ti % 2
                for fc in range(22):
                    bk = banks[fc % 5]
                    bbk = bbank[fc % 5]
                    for kc in range(8):
                        fw.op(fw.pe, lambda: nc.tensor.matmul(
                            bk[:, 0:T], w_in_sb[:, kc, fc * 128:(fc + 1) * 128], tb.ht[par][:, kc, :],
                            start=(kc == 0), stop=(kc == 7)), reads=[b_win, tb.b_ht[par]], writes=[bbk])
                    for kc in range(8):
                        fw.op(fw.pe, lambda: nc.tensor.matmul(
                            bk[:, T:2 * T], w_in_sb[:, kc, DFF + fc * 128:DFF + (fc + 1) * 128], tb.ht[par][:, kc, :],
                            start=(kc == 0), stop=(kc == 7)), reads=[b_win, tb.b_ht[par]], writes=[bbk])
                    sp_ = fc % 2
                    fw.op(fw.act, lambda: nc.scalar.activation(out=sg[sp_][:], in_=bk[:, 0:T], func=AF.Silu),
                          reads=[bbk], writes=[b_sg[sp_]])
                    fw.op(fw.dve, lambda: nc.vector.tensor_tensor(out=hid[par][:, fc, :], in0=bk[:, T:2 * T],
                                                                   in1=sg[sp_][:], op=ALU.mult),
                          reads=[bbk, b_sg[sp_]], writes=[b_hid[par]])
                    yield

            def S2a(ti):
                par = ti % 2
                s = (ti * T) // SEG
                for dc in range(8):
                    bk = banks[5 + dc % 2]
                    bbk = bbank[5 + dc % 2]
                    for fc in range(22):
                        fw.op(fw.pe, lambda: nc.tensor.matmul(
                            bk[:, 0:T], w_out_sb[:, fc, dc * 128:(dc + 1) * 128], hid[par][:, fc, :],
                            start=(fc == 0), stop=(fc == 21)), reads=[b_wout, b_hid[par]], writes=[bbk])
                    fw.op(fw.dve, lambda: nc.vector.scalar_tensor_tensor(
                        out=tb.vt[:, dc, :], in0=bk[:, 0:T], scalar=cg[:, l, i, dc, s:s + 1], in1=tb.xt[par][:, dc, :],
                        op0=ALU.mult, op1=ALU.add), reads=[bbk, b_mod, tb.b_xt[par]], writes=[tb.b_vt])

            def S2b(ti):
                par = ti % 2
                return post_norm_gen(l, i, ti, dst, hid[par][:, 0:8, :], hid[par][:, 8:16, :], b_hid[par])

            def S1m(ti):
                for _ in range(8):
                    yield
                modulate(l, i, ti, ti % 2)

            S1a(0)
            if NTILE > 1:
                S1a(1)
            drain([S1b(0)])
            for ti in range(NTILE):
                S2a(ti)
                gens = [S2b(ti)]
                if ti + 1 < NTILE:
                    gens.append(S1b(ti + 1))
                if ti + 2 < NTILE:
                    load_x(src, ti + 2, ti % 2)
                    gens.append(S1m(ti + 2))
                drain(gens)

        lam = sb("lam", [128, 1], F32)
        lm1 = sb("lm1", [128, 1], F32)
        b_lam = Buf("lam")
        fw.dma(fw.q_sp, lam[:], lam_in[:, :], writes=[b_lam])
        fw.op(fw.dve, lambda: nc.vector.tensor_scalar(out=lm1[:], in0=lam[:], scalar1=-1.0, scalar2=None, op0=ALU.add),
              reads=[b_lam], writes=[b_lam])
        jx = sb("jx", [128, 128], F32)
        b_jx = Buf("jx")
        fw.dma(fw.q_sp, jx[:], jx_in[:, :], writes=[b_jx])
        cstb = sb("cstb", [128, 768], BF16)
        identf = sb("identf", [128, 128], F32)
        rstm = sb("rstm", [128, 512], F32)
        b_cst = Buf("cst")
        fw.dma(fw.q_pool, cstb[:], cst_in[:, 0:768], writes=[b_cst])
        fw.dma(fw.q_sp, identf[:], cst_in[:, 512:640], writes=[b_cst])
        fw.dma(fw.q_sp, rstm[:], cst_in[:, 768:1280], writes=[b_cst])
        tri = cstb[:, 0:512].rearrange("p (a x) -> p a x", a=4)
        identb = cstb[:, 512:640]
        blk1 = cstb[:, 640:768]
        with phase():
            zt = sb("zt", [128, 8, 384], BF16)
            b_zt = Buf("zt")
            fw.op(fw.dve, lambda: nc.vector.memset(zt[:], 0.0), writes=[b_zt])
            fw.dma(fw.q_sp, V_d[0:PADV, :].rearrange("(b p) f -> p b f", p=128), zt[:], reads=[b_zt])
            fw.dma(fw.q_sp, V_d[PADV + NT:PADV + NT + PADV, :].rearrange("(b p) f -> p b f", p=128), zt[:],
                   reads=[b_zt])
            ztm = sb("ztm", [128, 3, NT], BF16)
            b_ztm = Buf("ztm")
            fw.op(fw.pool, lambda: nc.gpsimd.memset(ztm[:], 0.0), writes=[b_ztm])
            fw.dma(fw.q_sp, mix_d[:, 5:8, :], ztm[:], reads=[b_ztm])
            rb = sb("rb", [32, 6], F32)
            oh = sb("oh", [32, 1536], F32)
            on6 = sb("on6", [32, 6], F32)
            wv = sb("wv", [6, 1536], F32)
            wm = sb("wm", [6, 1536], F32)
            b_rb = Buf("rb")
            b_wv = Buf("wv")
            fw.dma(fw.q_sp, rb[:], rel_bias_in[:, :], writes=[b_rb])
            fw.dma(fw.q_sp, oh[:], oh_in[:, :], writes=[b_rb])
            fw.op(fw.dve, lambda: nc.vector.memset(on6[:], 1.0), writes=[b_rb])
            for bi in range(3):
                fw.op(fw.pe, lambda: nc.tensor.matmul(banks[0][0:6, :], rb[:], oh[:, bi * 512:(bi + 1) * 512],
                                                       start=True, stop=True), reads=[b_rb], writes=[bbank[0]])
                fw.op(fw.pe, lambda: nc.tensor.matmul(banks[1][0:6, :], on6[:], oh[:, bi * 512:(bi + 1) * 512],
                                                       start=True, stop=True), reads=[b_rb], writes=[bbank[1]])
                fw.op(fw.act, lambda: nc.scalar.activation(out=wv[:, bi * 512:(bi + 1) * 512], in_=banks[0][0:6, :],
                                                            func=AF.Exp), reads=[bbank[0]], writes=[b_wv])
                fw.op(fw.dve, lambda: nc.vector.tensor_tensor(out=wm[:, bi * 512:(bi + 1) * 512],
                                                               in0=banks[1][0:6, :], in1=wv[:, bi * 512:(bi + 1) * 512],
                                                               op=ALU.mult), reads=[bbank[1], b_wv], writes=[b_wv])
            fw.dma(fw.q_sp, wd_d[:, :], wm[:], reads=[b_wv])

        def mixin_pass(l, src):
            with phase():
                alloc_tb()
                wmi = sb("wmi", [128, 8, 3456], BF16)
                b_wmi = Buf("wmi")
                wst = [sb(f"wst{i}", [128, 1728], F32) for i in range(2)]
                b_wst = [Buf(), Buf()]
                for kc in range(8):
                    for hf in range(2):
                        wload(wmi[:, kc, hf * 1728:(hf + 1) * 1728],
                              w_mix_in[l, kc * 128:(kc + 1) * 128, hf * 1728:(hf + 1) * 1728], [w_[:] for w_ in wst],
                              b_wst, b_wmi)
                NB = T // 128
                qs = [sb(f"qs{i}", [128, 3, T], BF16) for i in range(2)]
                ks = [sb(f"ks{i}", [128, 3, T], BF16) for i in range(2)]
                vs = [sb(f"vs{i}", [128, NB, 384], BF16) for i in range(2)]
                gbs = [sb(f"gbs{i}", [128, 2, T], F32) for i in range(2)]
                gcs = [sb(f"gcs{i}", [128, 2, T], F32) for i in range(2)]
                ucs = [sb(f"ucs{i}", [128, 2, T], F32) for i in range(2)]
                zs = [sb(f"zs{i}", [128, 12, T], F32) for i in range(2)]
                bq = [Buf() for _ in range(2)]
                bk_ = [Buf() for _ in range(2)]
                bv = [Buf() for _ in range(2)]
                bgb = [Buf() for _ in range(2)]
                bgc = [Buf() for _ in range(2)]
                buc = [Buf() for _ in range(2)]
                bz = [Buf() for _ in range(2)]
                ev = 0
                load_x(src, 0, 0)
                modulate(l, 1, 0, 0)
                for ti in range(NTILE):
                    par = ti % 2
                    if ti + 1 < NTILE:
                        load_x(src, ti + 1, 1 - par)
                    tsl = slice(ti * T, (ti + 1) * T)
                    for ch in range(27):
                        if ch == 16 and ti + 1 < NTILE:
                            modulate(l, 1, ti + 1, 1 - par)
                        if 6 <= ch < 9:
                            continue
                        bk = banks[ch % 4]
                        bbk = bbank[ch % 4]
                        for kc in range(8):
                            fw.op(fw.pe, lambda: nc.tensor.matmul(
                                bk[:, 0:T], wmi[:, kc, ch * 128:(ch + 1) * 128], tb.ht[par][:, kc, :],
                                start=(kc == 0), stop=(kc == 7)), reads=[b_wmi, tb.b_ht[par]], writes=[bbk])
                        if ch < 3:
                            dst_, bd = qs[par][:, ch, :], bq[par]
                        elif ch < 6:
                            dst_, bd = ks[par][:, ch - 3, :], bk_[par]
                        elif ch < 11:
                            dst_, bd = gbs[par][:, ch - 9, :], bgb[par]
                        elif ch < 13:
                            dst_, bd = gcs[par][:, ch - 11, :], bgc[par]
                        elif ch < 15:
                            fw.op(fw.dve, lambda: nc.vector.tensor_tensor(
                                out=ucs[par][:, ch - 13, :], in0=bk[:, 0:T], in1=gcs[par][:, ch - 13, :], op=ALU.mult),
                                reads=[bbk, bgc[par]], writes=[buc[par]])
                            continue
                        else:
                            dst_, bd = zs[par][:, ch - 15, :], bz[par]
                        ev += 1
                        if ev % 2 == 0:
                            fw.op(fw.act, lambda: nc.scalar.activation(out=dst_, in_=bk[:, 0:T], func=AF.Copy),
                                  reads=[bbk], writes=[bd])
                        else:
                            fw.op(fw.dve, lambda: nc.vector.tensor_copy(out=dst_, in_=bk[:, 0:T]),
                                  reads=[bbk], writes=[bd])
                    for blk in range(NB):
                        bk = banks[4 + blk % 2]
                        bbk = bbank[4 + blk % 2]
                        for kc in range(8):
                            fw.op(fw.pe, lambda: nc.tensor.matmul(
                                bk[:, 0:384], tb.ht[par][:, kc, blk * 128:(blk + 1) * 128], wmi[:, kc, 768:1152],
                                start=(kc == 0), stop=(kc == 7)), reads=[b_wmi, tb.b_ht[par]], writes=[bbk])
                        fw.op(fw.act, lambda: nc.scalar.activation(out=vs[par][:, blk, :], in_=bk[:, 0:384],
                                                                    func=AF.Copy), reads=[bbk], writes=[bv[par]])
                    fw.dma(fw.q_sp, qT_d[:, :, tsl], qs[par][:], reads=[bq[par]])
                    fw.dma(fw.q_sp, kT_d[:, :, tsl], ks[par][:], reads=[bk_[par]])
                    fw.dma(fw.q_sp, V_d[PADV + ti * T:PADV + (ti + 1) * T, :].rearrange("(b p) f -> p b f", p=128),
                           vs[par][:], reads=[bv[par]])
                    fw.dma(fw.q_sp, gb_d[:, :, tsl], gbs[par][:], reads=[bgb[par]])
                    fw.dma(fw.q_sp, uc_d[:, :, tsl], ucs[par][:], reads=[buc[par]])
                    fw.dma(fw.q_sp, z_d[:, :, tsl], zs[par][:], reads=[bz[par]])

        def conv_pass(l):
            with phase():
                u = sb("cu", [128, NT + 2], F32)
                gbt = sb("cgb", [128, NT], F32)
                y = sb("cy", [128, NT], F32)
                yb = sb("cyb", [128, NT], BF16)
                tmp = sb("ctmp", [128, 2], F32)
                b_u, b_gb, b_y, b_yb, b_tmp = Buf(), Buf(), Buf(), Buf(), Buf()
                for c in range(2):
                    def cw(tap):
                        col = ppoff["conv_w"] + (l * 3 + tap) * 2 + c
                        return pp[:, col:col + 1]
                    fw.op(fw.dve, lambda: nc.vector.memset(u[:, 0:1], 0.0), writes=[b_u])
                    fw.op(fw.dve, lambda: nc.vector.memset(u[:, NT + 1:NT + 2], 0.0), writes=[b_u])
                    fw.dma(fw.q_sp, u[:, 1:NT + 1], uc_d[:, c, :], writes=[b_u])
                    fw.dma(fw.q_sp, gbt[:], gb_d[:, c, :], writes=[b_gb])
                    fw.op(fw.dve, lambda: nc.vector.tensor_scalar(out=y[:], in0=u[:, 1:NT + 1], scalar1=cw(1),
                                                                   scalar2=None, op0=ALU.mult),
                          reads=[b_u, b_pp], writes=[b_y])
                    fw.op(fw.dve, lambda: nc.vector.scalar_tensor_tensor(out=y[:], in0=u[:, 0:NT], scalar=cw(0),
                                                                          in1=y[:], op0=ALU.mult, op1=ALU.add),
                          reads=[b_u, b_pp], writes=[b_y])
                    fw.op(fw.dve, lambda: nc.vector.scalar_tensor_tensor(out=y[:], in0=u[:, 2:NT + 2], scalar=cw(2),
                                                                          in1=y[:], op0=ALU.mult, op1=ALU.add),
                          reads=[b_u, b_pp], writes=[b_y])
                    fw.op(fw.dve, lambda: nc.vector.tensor_scalar(out=tmp[:, 0:1], in0=u[:, SEG + 1:SEG + 2],
                                                                   scalar1=cw(2), scalar2=lm1[:, 0:1],
                                                                   op0=ALU.mult, op1=ALU.mult),
                          reads=[b_u, b_pp, b_lam], writes=[b_tmp])
                    fw.op(fw.dve, lambda: nc.vector.tensor_scalar(out=tmp[:, 1:2], in0=u[:, SEG:SEG + 1],
                                                                   scalar1=cw(0), scalar2=lm1[:, 0:1],
                                                                   op0=ALU.mult, op1=ALU.mult),
                          reads=[b_u, b_pp, b_lam], writes=[b_tmp])
                    fw.op(fw.dve, lambda: nc.vector.tensor_tensor(out=y[:, SEG - 1:SEG + 1], in0=y[:, SEG - 1:SEG + 1],
                                                                   in1=tmp[:, 0:2], op=ALU.add),
                          reads=[b_tmp], writes=[b_y])
                    fw.op(fw.dve, lambda: nc.vector.tensor_tensor(out=yb[:], in0=y[:], in1=gbt[:], op=ALU.mult),
                          reads=[b_y, b_gb], writes=[b_yb])
                    fw.dma(fw.q_sp, mix_d[:, 3 + c, :], yb[:], reads=[b_yb])

        def attn_pass(l):
            with phase():
                qc = sb("aq", [128, NT], BF16)
                kc_ = sb("ak", [128, NT + 2 * PADK], BF16)
                NVT = NT // 128 + 16
                Vt = sb("aV", [128, NVT, 128], BF16)
                acc = sb("aacc", [128, 2, NT], F32)
                Eb = sb("aEb", [128, 2, 256], BF16)
                Gh = [sb(f"aG{i}", [128, 128], F32) for i in range(2)]
                NVAR = 5
                Ev = [sb(f"aEv{i}", [128, 2, 256], BF16) for i in range(NVAR)]
                ND = 3
                sexp = [sb(f"asx{i}", [128, 2, 256], BF16) for i in range(ND)]
                sT = [sb(f"asT{i}", [128, 2, 256], BF16) for i in range(ND)]
                nd_regions = [banks[4][:, 0:256], banks[5][:, 0:256]]
                b_nd = [bbank[4], bbank[5]]
                eb_bank = banks[6][:, 0:128]
                b_eb = bbank[6]
                b_q, b_k, b_V, b_acc, b_Eb = Buf(), Buf(), Buf(), Buf(), Buf()
                b_G = [Buf(), Buf()]
                b_Ev = [Buf() for _ in range(NVAR)]
                b_sx = [Buf() for _ in range(ND)]
                b_sT = [Buf() for _ in range(ND)]
                fw.op(fw.pool, lambda: nc.gpsimd.memset(kc_[:, 0:PADK], 0.0), writes=[b_k])
                fw.op(fw.pool, lambda: nc.gpsimd.memset(kc_[:, PADK + NT:PADK + NT + PADK], 0.0), writes=[b_k])
                wcount = 0
                for pair in range(3):
                    fw.dma(fw.q_sp, qc[:], qT_d[:, pair, :], writes=[b_q])
                    fw.dma(fw.q_sp, kc_[:, PADK:PADK + NT], kT_d[:, pair, :], writes=[b_k])
                    for bi, d in enumerate(DILS):
                        nw = NT // (128 * d)
                        assert nw % 2 == 0
                        gi = 0
                        for hh in range(2):
                            h = pair * 2 + hh
                            for ab in range(2):
                                base = 65 if ab == 0 else 193
                                g = Gh[gi % 2]
                                bg = b_G[gi % 2]
                                gi += 1
                                src_ap = bass.AP(wd_t, h * 1536 + bi * 512 + base, [[1, 128], [1, 128]])
                                fw.dma(fw.q_sp, g[:], src_ap, writes=[bg])
                                fw.op(fw.pe, lambda: nc.tensor.matmul(eb_bank, g[:], jx[:], start=True,
                                                                       stop=True),
                                      reads=[bg, b_jx], writes=[b_eb])
                                fw.op(fw.act, lambda: nc.scalar.activation(
                                    out=Eb[:, hh, ab * 128:(ab + 1) * 128], in_=eb_bank, func=AF.Copy),
                                    reads=[b_eb], writes=[b_Eb])
                        def cat(c):
                            sa = "0" if c == 0 else ("l" if c == nw // 2 else "1")
                            sb_ = "0" if c == nw - 1 else ("l" if c == nw // 2 - 1 else "1")
                            return sa, sb_
                        var_idx = {}
                        for c in range(nw):
                            kk_ = cat(c)
                            if kk_ in var_idx:
                                continue
                            vi = len(var_idx)
                            assert vi < NVAR
                            var_idx[kk_] = vi
                            fw.op(fw.dve, lambda: nc.vector.tensor_copy(out=Ev[vi][:], in_=Eb[:]), reads=[b_Eb],
                                  writes=[b_Ev[vi]])
                            for which, (rows, cols) in enumerate(((slice(0, 64), slice(0, 128)),
                                                                  (slice(64, 128), slice(128, 256)))):
                                mode = kk_[which]
                                if mode == "1":
                                    continue
                                blkap = Ev[vi][rows, :, cols]
                                if mode == "0":
                                    fw.op(fw.dve, lambda: nc.vector.memset(blkap, 0.0), writes=[b_Ev[vi]])
                                else:
                                    fw.op(fw.dve, lambda: nc.vector.tensor_scalar(
                                        out=blkap, in0=blkap, scalar1=lam[rows, 0:1], scalar2=None, op0=ALU.mult),
                                        reads=[b_lam], writes=[b_Ev[vi]])
                        for r in range(d):
                            off = (PADV + r - 64 * d) * 384 + pair * 128
                            src_ap = bass.AP(V_t, off, [[d * 384, 128], [d * 128 * 384, nw + 1], [1, 128]])
                            fw.dma(fw.q_sp, Vt[:, r * (nw + 1):(r + 1) * (nw + 1), :], src_ap, writes=[b_V])
                        def window(r, c, wpar, w2):
                            sb2 = pall[:, w2 * 1024:(w2 + 1) * 1024]
                            bsb = bbank[w2 * 2]
                            q0 = r + d * 128 * c
                            qsl = slice(q0, q0 + 127 * d + 1, d)
                            for hh in range(2):
                                p0 = 64 * hh
                                for ab in range(2):
                                    k0 = PADK + r + d * (128 * c - 64 + 128 * ab)
                                    ksl = slice(k0, k0 + 127 * d + 1, d)
                                    fw.op(fw.pe, lambda: nc.tensor.matmul(
                                        sb2[:, hh * 512 + ab * 128: hh * 512 + (ab + 1) * 128],
                                        kc_[p0:p0 + 64, ksl], qc[p0:p0 + 64, qsl], start=True, stop=True),
                                        reads=[b_k, b_q], writes=[bsb])
                            sview = sb2.rearrange("p (h x) -> p h x", h=2)[:, :, 0:256]
                            fw.op(fw.act, lambda: nc.scalar.activation(out=sexp[wpar][:], in_=sview, func=AF.Exp,
                                                                        scale=0.125),
                                  reads=[bsb], writes=[b_sx[wpar]])
                            vi = var_idx[cat(c)]
                            fw.op(fw.dve, lambda: nc.vector.tensor_tensor(out=sT[wpar][:], in0=sexp[wpar][:],
                                                                           in1=Ev[vi][:], op=ALU.mult),
                                  reads=[b_sx[wpar], b_Ev[vi]], writes=[b_sT[wpar]])
                            yield
                            nd = nd_regions[w2]
                            bnd = b_nd[w2]
                            tA = r * (nw + 1) + c
                            for hh in range(2):
                                p0 = 64 * hh
                                for ab in range(2):
                                    fw.op(fw.pe, lambda: nc.tensor.matmul(
                                        nd[p0:p0 + 64, 0:128], Vt[:, tA + ab, p0:p0 + 64],
                                        sT[wpar][:, hh, ab * 128:(ab + 1) * 128], start=(ab == 0), stop=(ab == 1)),
                                        reads=[b_V, b_sT[wpar]], writes=[bnd])
                                for ab in range(2):
                                    fw.op(fw.pe, lambda: nc.tensor.matmul(
                                        nd[p0:p0 + 64, 128:256], ones_bf[:, 0:64],
                                        sT[wpar][:, hh, ab * 128:(ab + 1) * 128], start=(ab == 0), stop=(ab == 1)),
                                        reads=[b_ones, b_sT[wpar]], writes=[bnd])
                            ndv = nd.rearrange("p (a x) -> p a x", a=2)
                            accv = acc[:, :, qsl]
                            if bi == 0:
                                fw.op(fw.dve, lambda: nc.vector.tensor_copy(out=accv, in_=ndv), reads=[bnd],
                                      writes=[b_acc])
                            else:
                                fw.op(fw.dve, lambda: nc.vector.tensor_tensor(out=accv, in0=ndv, in1=accv,
                                                                               op=ALU.add),
                                      reads=[bnd], writes=[b_acc])
                        pending = None
                        for r in range(d):
                            for c in range(nw):
                                g_ = window(r, c, wcount % ND, wcount % 2)
                                wcount += 1
                                next(g_)
                                if pending is not None:
                                    for _ in pending:
                                        pass
                                pending = g_
                        for _ in pending:
                            pass
                    fw.op(fw.dve, lambda: nc.vector.reciprocal(out=acc[:, 1, :], in_=acc[:, 1, :]), writes=[b_acc])
                    fw.op(fw.dve, lambda: nc.vector.tensor_tensor(out=qc[:], in0=acc[:, 0, :], in1=acc[:, 1, :],
                                                                   op=ALU.mult), reads=[b_acc], writes=[b_q])
                    fw.dma(fw.q_sp, mix_d[:, pair, :], qc[:], reads=[b_q])

        def mixout_pass(l, src, dst):
            with phase():
                alloc_tb()
                wmo = sb("wmo", [128, 8, D], BF16)
                b_wmo = Buf()
                wst = [sb(f"wsto{i}", [128, D], F32) for i in range(2)]
                b_wst = [Buf(), Buf()]
                for kc in range(8):
                    wload(wmo[:, kc, :], w_mix_out[l, kc * 128:(kc + 1) * 128, :], [w_[:] for w_ in wst], b_wst, b_wmo)
                mt = [sb(f"mt{i}", [128, 8, T], BF16) for i in range(2)]
                b_mt = [Buf(), Buf()]
                vt2 = [tb.vt, sb("vt2", [128, 8, T], F32)]
                b_vt2 = [tb.b_vt, Buf()]
                vb2 = [tb.vb, sb("vb2", [128, 8, T], BF16)]
                sq2 = [tb.sqb, sb("sq2", [128, 8, T], BF16)]
                b_vb2 = [tb.b_vb, Buf()]

                def M0(ti):
                    par = ti % 2
                    load_x(src, ti, par)
                    fw.dma(fw.q_sp, mt[par][:], mix_d[:, :, ti * T:(ti + 1) * T], writes=[b_mt[par]])

                def M1(ti):
                    par = ti % 2
                    s_ = (ti * T) // SEG
                    for dc in range(8):
                        bk = banks[dc % 4]
                        bbk = bbank[dc % 4]
                        for mc in range(8):
                            fw.op(fw.pe, lambda: nc.tensor.matmul(
                                bk[:, 0:T], wmo[:, mc, dc * 128:(dc + 1) * 128], mt[par][:, mc, :],
                                start=(mc == 0), stop=(mc == 7)), reads=[b_wmo, b_mt[par]], writes=[bbk])
                        fw.op(fw.dve, lambda: nc.vector.scalar_tensor_tensor(
                            out=vt2[par][:, dc, :], in0=bk[:, 0:T], scalar=cg[:, l, 1, dc, s_:s_ + 1],
                            in1=tb.xt[par][:, dc, :], op0=ALU.mult, op1=ALU.add),
                            reads=[bbk, b_mod, tb.b_xt[par]], writes=[b_vt2[par]])
                        yield

                def M2(ti):
                    par = ti % 2
                    return post_norm_gen(l, 1, ti, dst, vb2[par][:], sq2[par][:], b_vb2[par], vt2[par], b_vt2[par], fast=True)

                M0(0)
                if NTILE > 1:
                    M0(1)
                drain([M1(0)])
                for ti in range(NTILE):
                    gens = [M2(ti)]
                    if ti + 1 < NTILE:
                        gens.append(M1(ti + 1))
                    drain(gens)
                    if ti + 2 < NTILE:
                        M0(ti + 2)

        NCH = NT // 128
        KINDS = ("A", "B", "K", "R", "Bh", "Kh")

        def rwkv_prep(l):
            with phase():
                TR = 512
                NCT = TR // 128
                wup = sb("wup", [128, 384], BF16)
                aup = sb("aup", [128, 384], BF16)
                gup = sb("gup", [128, 384], BF16)
                b_w = Buf()
                for dr in range(2):
                    fw.dma(fw.q_pool, wup[64 * dr:64 * dr + 64, :], rw_w_up[l, dr, :, :], writes=[b_w])
                    fw.dma(fw.q_pool, aup[64 * dr:64 * dr + 64, :], rw_a_up[l, dr, :, :], writes=[b_w])
                fw.dma(fw.q_pool, gup[:], rw_g_up[l, :, :], writes=[b_w])
                omka = sb("omka", [128, 3], F32)
                kacol = ppoff["k_a"] + l * 3
                fw.op(fw.dve, lambda: nc.vector.tensor_scalar(out=omka[:], in0=pp[:, kacol:kacol + 3], scalar1=-1.0,
                                                               scalar2=1.0, op0=ALU.mult, op1=ALU.add),
                      reads=[b_pp], writes=[b_w])
                zt = sb("zt", [128, 12, TR + 2], F32)
                t12 = sb("t12", [128, 12, TR], F32)
                zs = sb("zs", [128, 12, TR], F32)
                b_zt, b_t12, b_zs = Buf(), Buf(), Buf()
                twd = sb("twd", [128, TR], BF16)
                adb = sb("adb", [128, TR], BF16)
                sgd = sb("sgd", [128, TR], BF16)
                b_tw = Buf()
                vst = sb("vst", [128, 3, TR], BF16)
                gst = sb("gst", [128, 3, TR], F32)
                bon = sb("bon", [128, 3, TR], F32)
                b_vst, b_gst, b_bon = Buf(), Buf(), Buf()
                stg = {(dr, k): sb(f"stg{dr}{k}", [128, 3, TR], BF16) for dr in range(2) for k in KINDS}
                b_stg = {key: Buf() for key in stg}
                pst = [sb(f"pst{dr}", [128, 3, NCT], F32) for dr in range(2)]
                b_pst = [Buf(), Buf()]
                names = ["kks", "sqk", "nrm", "kk", "rkr", "sgw", "lw", "a", "kd", "bb", "cs", "tmp", "P", "iP", "Pex"]
                tt = {n_: sb("r1" + n_, [128, TR], BF16 if n_ in ("sqk", "rkr") else F32) for n_ in names}
                bt = {n_: Buf() for n_ in names}
                drn = ("sgw", "lw", "a", "kd", "bb", "cs", "tmp", "P", "iP", "Pex")
                ttd, btd = [], []
                for dr_ in range(2):
                    t_ = dict(tt)
                    b_ = dict(bt)
                    if dr_ == 1:
                        for n_ in drn:
                            t_[n_] = sb("r1b" + n_, [128, TR], F32)
                            b_[n_] = Buf()
                    ttd.append(t_)
                    btd.append(b_)
                for ti in range(NT // TR):
                    t0 = ti * TR
                    lo = t0 - 1 if t0 > 0 else t0
                    hi = t0 + TR + 1 if t0 + TR < NT else t0 + TR
                    if t0 == 0:
                        fw.op(fw.dve, lambda: nc.vector.memset(zt[:, :, 0:1], 0.0), writes=[b_zt])
                    if t0 + TR == NT:
                        fw.op(fw.dve, lambda: nc.vector.memset(zt[:, :, TR + 1:TR + 2], 0.0), writes=[b_zt])
                    fw.dma(fw.q_sp, zt[:, :, 1 - (t0 - lo):1 + TR + (hi - t0 - TR)], z_d[:, :, lo:hi], writes=[b_zt])
                    if t0 == SEG:
                        fw.op(fw.dve, lambda: nc.vector.tensor_scalar(out=zt[:, :, 0:1], in0=zt[:, :, 0:1],
                                                                       scalar1=lam[:, 0:1], scalar2=None, op0=ALU.mult),
                              reads=[b_lam], writes=[b_zt])
                    if t0 + TR == SEG:
                        fw.op(fw.dve, lambda: nc.vector.tensor_scalar(out=zt[:, :, TR + 1:TR + 2],
                                                                       in0=zt[:, :, TR + 1:TR + 2],
                                                                       scalar1=lam[:, 0:1], scalar2=None, op0=ALU.mult),
                              reads=[b_lam], writes=[b_zt])
                    zc = zt[:, :, 1:TR + 1]
                    fw.op(fw.dve, lambda: nc.vector.tensor_tensor(out=t12[:], in0=zt[:, :, 0:TR], in1=zt[:, :, 2:TR + 2],
                                                                   op=ALU.add), reads=[b_zt], writes=[b_t12])
                    fw.op(fw.dve, lambda: nc.vector.scalar_tensor_tensor(out=t12[:], in0=t12[:], scalar=0.5, in1=zc,
                                                                          op0=ALU.mult, op1=ALU.subtract),
                          reads=[b_zt], writes=[b_t12])
                    for ch in range(12):
                        mcol = ppoff["mu"] + l * 12 + ch
                        fw.op(fw.dve, lambda: nc.vector.scalar_tensor_tensor(
                            out=zs[:, ch, :], in0=t12[:, ch, :], scalar=pp[:, mcol:mcol + 1], in1=zt[:, ch, 1:TR + 1],
                            op0=ALU.mult, op1=ALU.add), reads=[b_t12, b_zt, b_pp], writes=[b_zs])
                    fw.op(fw.act, lambda: nc.scalar.activation(out=twd[:], in_=zs[:, 9, :], func=AF.Tanh),
                          reads=[b_zs], writes=[b_tw])
                    fw.op(fw.act, lambda: nc.scalar.activation(out=sgd[:], in_=zs[:, 11, :], func=AF.Sigmoid),
                          reads=[b_zs], writes=[b_tw])
                    fw.op(fw.act, lambda: nc.scalar.activation(out=adb[:], in_=zs[:, 10, :], func=AF.Copy),
                          reads=[b_zs], writes=[b_tw])
                    fw.op(fw.pool, lambda: nc.gpsimd.tensor_copy(out=vst[:], in_=zs[:, 6:9, :]), reads=[b_zs],
                          writes=[b_vst])
                    for fc in range(3):
                        fsl = slice(fc * 128, (fc + 1) * 128)
                        r_ = zs[:, fc, :]
                        k_ = zs[:, 3 + fc, :]
                        v_ = zs[:, 6 + fc, :]

                        def col(nm, extra=0):
                            c_ = ppoff[nm] + l * (6 if nm in ("w0", "a0") else 3) + extra + fc
                            return pp[:, c_:c_ + 1]
                        V_ = lambda fn, r, w: fw.op(fw.dve, fn, reads=r, writes=w)
                        A_ = lambda fn, r, w: fw.op(fw.act, fn, reads=r, writes=w)
                        A_(lambda: nc.scalar.activation(out=tt["kks"][:], in_=k_, func=AF.Copy, scale=col("k_k")),
                           [b_zs, b_pp], [bt["kks"]])
                        A_(lambda: nc.scalar.activation(out=tt["sqk"][:], in_=tt["kks"][:], func=AF.Square),
                           [bt["kks"]], [bt["sqk"]])
                        fw.op(fw.pe, lambda: nc.tensor.matmul(banks[0][:, 0:TR], blk1[:], tt["sqk"][:], start=True,
                                                               stop=True), reads=[b_cst, bt["sqk"]], writes=[bbank[0]])
                        A_(lambda: nc.scalar.activation(out=tt["nrm"][:], in_=banks[0][:, 0:TR], func=AF.Sqrt),
                           [bbank[0]], [bt["nrm"]])
                        V_(lambda: nc.vector.tensor_scalar(out=tt["nrm"][:], in0=tt["nrm"][:], scalar1=1e-12,
                                                           scalar2=None, op0=ALU.max), [], [bt["nrm"]])
                        V_(lambda: nc.vector.reciprocal(out=tt["nrm"][:], in_=tt["nrm"][:]), [], [bt["nrm"]])
                        V_(lambda: nc.vector.tensor_tensor(out=tt["kk"][:], in0=tt["kks"][:], in1=tt["nrm"][:],
                                                           op=ALU.mult), [bt["kks"], bt["nrm"]], [bt["kk"]])
                        V_(lambda: nc.vector.scalar_tensor_tensor(out=tt["rkr"][:], in0=r_, scalar=col("r_k"), in1=k_,
                                                                  op0=ALU.mult, op1=ALU.mult), [b_zs, b_pp], [bt["rkr"]])
                        fw.op(fw.pe, lambda: nc.tensor.matmul(banks[1][:, 0:TR], blk1[:], tt["rkr"][:], start=True,
                                                               stop=True), reads=[b_cst, bt["rkr"]], writes=[bbank[1]])
                        V_(lambda: nc.vector.tensor_tensor(out=bon[:, fc, :], in0=banks[1][:, 0:TR], in1=v_,
                                                           op=ALU.mult), [bbank[1], b_zs], [b_bon])
                        fw.op(fw.pe, lambda: nc.tensor.matmul(banks[2][:, 0:TR], gup[:, fsl], sgd[:], start=True,
                                                               stop=True), reads=[b_w, b_tw], writes=[bbank[2]])
                        A_(lambda: nc.scalar.activation(out=gst[:, fc, :], in_=banks[2][:, 0:TR], func=AF.Copy),
                           [bbank[2]], [b_gst])
                        def dr_stream(dr):
                            tt = ttd[dr]
                            bt = btd[dr]
                            rows = slice(64 * dr, 64 * dr + 64)
                            bw_ = banks[3 + dr]
                            ba_ = banks[5 + dr]
                            fw.op(fw.pe, lambda: nc.tensor.matmul(bw_[:, 0:TR], wup[rows, fsl], twd[rows, :],
                                                                   start=True, stop=True),
                                  reads=[b_w, b_tw], writes=[bbank[3 + dr]])
                            yield
                            fw.op(fw.pe, lambda: nc.tensor.matmul(ba_[:, 0:TR], aup[rows, fsl], adb[rows, :],
                                                                   start=True, stop=True),
                                  reads=[b_w, b_tw], writes=[bbank[5 + dr]])
                            yield
                            A_(lambda: nc.scalar.activation(out=tt["sgw"][:], in_=bw_[:, 0:TR], func=AF.Sigmoid,
                                                            bias=col("w0", 3 * dr)), [bbank[3 + dr], b_pp], [bt["sgw"]])
                            yield
                            A_(lambda: nc.scalar.activation(out=tt["a"][:], in_=ba_[:, 0:TR], func=AF.Sigmoid,
                                                            bias=col("a0", 3 * dr)), [bbank[5 + dr], b_pp], [bt["a"]])
                            yield
                            V_(lambda: nc.vector.tensor_scalar(out=tt["lw"][:], in0=tt["sgw"][:],
                                                               scalar1=-float(np.exp(-0.5)), scalar2=None, op0=ALU.mult),
                               [bt["sgw"]], [bt["lw"]])
                            yield
                            A_(lambda: nc.scalar.activation(out=tt["kd"][:], in_=tt["a"][:], func=AF.Identity,
                                                            scale=col("k_a"), bias=omka[:, fc:fc + 1]),
                               [bt["a"], b_pp, b_w], [bt["kd"]])
                            yield
                            V_(lambda: nc.vector.tensor_tensor(out=tt["kd"][:], in0=tt["kd"][:], in1=k_, op=ALU.mult),
                               [b_zs], [bt["kd"]])
                            yield
                            V_(lambda: nc.vector.tensor_tensor(out=tt["bb"][:], in0=tt["kk"][:], in1=tt["a"][:],
                                                               op=ALU.mult), [bt["kk"], bt["a"]], [bt["bb"]])
                            yield
                            V_(lambda: nc.vector.tensor_tensor_scan(out=tt["cs"][:], data0=rstm[:, 0:TR],
                                                                    data1=tt["lw"][:], initial=0.0, op0=ALU.mult,
                                                                    op1=ALU.add), [b_cst, bt["lw"]], [bt["cs"]])
                            yield
                            cs3 = tt["cs"][:].rearrange("p (c x) -> p c x", x=128)
                            lw3 = tt["lw"][:].rearrange("p (c x) -> p c x", x=128)
                            if dr == 1:
                                V_(lambda: nc.vector.tensor_tensor(
                                    out=cs3, in0=cs3[:, :, 127:128].to_broadcast([128, NCT, 128]), in1=cs3,
                                    op=ALU.subtract), [], [bt["cs"]])
                                V_(lambda: nc.vector.tensor_tensor(out=tt["cs"][:], in0=tt["cs"][:], in1=tt["lw"][:],
                                                                   op=ALU.add), [bt["lw"]], [bt["cs"]])
                            A_(lambda: nc.scalar.activation(out=tt["P"][:], in_=tt["cs"][:], func=AF.Exp),
                               [bt["cs"]], [bt["P"]])
                            yield
                            A_(lambda: nc.scalar.activation(out=tt["iP"][:], in_=tt["cs"][:], func=AF.Exp, scale=-1.0),
                               [bt["cs"]], [bt["iP"]])
                            yield
                            V_(lambda: nc.vector.tensor_tensor(out=tt["tmp"][:], in0=tt["cs"][:], in1=tt["lw"][:],
                                                               op=ALU.subtract), [bt["cs"], bt["lw"]], [bt["tmp"]])
                            yield
                            A_(lambda: nc.scalar.activation(out=tt["Pex"][:], in_=tt["tmp"][:], func=AF.Exp),
                               [bt["tmp"]], [bt["Pex"]])
                            yield
                            P3 = tt["P"][:].rearrange("p (c x) -> p c x", x=128)
                            pe_idx = 127 if dr == 0 else 0
                            pend_v = P3[:, :, pe_idx:pe_idx + 1]
                            V_(lambda: nc.vector.tensor_copy(out=pst[dr][:, fc, :].unsqueeze(2), in_=pend_v),
                               [bt["P"]], [b_pst[dr]])
                            yield
                            S = lambda k: stg[(dr, k)][:, fc, :]
                            V_(lambda: nc.vector.scalar_tensor_tensor(out=S("A"), in0=tt["kk"][:], scalar=-1.0,
                                                                      in1=tt["Pex"][:], op0=ALU.mult, op1=ALU.mult),
                               [bt["kk"], bt["Pex"]], [b_stg[(dr, "A")]])
                            yield
                            V_(lambda: nc.vector.tensor_tensor(out=tt["bb"][:], in0=tt["bb"][:], in1=tt["iP"][:],
                                                               op=ALU.mult), [bt["iP"]], [bt["bb"]])
                            yield
                            V_(lambda: nc.vector.tensor_tensor(out=tt["kd"][:], in0=tt["kd"][:], in1=tt["iP"][:],
                                                               op=ALU.mult), [bt["iP"]], [bt["kd"]])
                            yield
                            A_(lambda: nc.scalar.activation(out=S("B"), in_=tt["bb"][:], func=AF.Copy),
                               [bt["bb"]], [b_stg[(dr, "B")]])
                            yield
                            A_(lambda: nc.scalar.activation(out=S("K"), in_=tt["kd"][:], func=AF.Copy),
                               [bt["kd"]], [b_stg[(dr, "K")]])
                            yield
                            V_(lambda: nc.vector.tensor_tensor(out=S("R"), in0=r_, in1=tt["P"][:], op=ALU.mult),
                               [b_zs, bt["P"]], [b_stg[(dr, "R")]])
                            yield
                            pbc = pend_v.to_broadcast([128, NCT, 128])
                            V_(lambda: nc.vector.tensor_tensor(out=S("Bh").rearrange("p (c x) -> p c x", x=128),
                                                               in0=tt["bb"][:].rearrange("p (c x) -> p c x", x=128),
                                                               in1=pbc, op=ALU.mult),
                               [bt["bb"], bt["P"]], [b_stg[(dr, "Bh")]])
                            yield
                            V_(lambda: nc.vector.tensor_tensor(out=S("Kh").rearrange("p (c x) -> p c x", x=128),
                                                               in0=tt["kd"][:].rearrange("p (c x) -> p c x", x=128),
                                                               in1=pbc, op=ALU.mult),
                               [bt["kd"], bt["P"]], [b_stg[(dr, "Kh")]])
                            yield
                        drain([dr_stream(0), dr_stream(1)])
                    tsl = slice(t0, t0 + TR)
                    fw.dma(fw.q_sp, rv_d[:, :, tsl], vst[:], reads=[b_vst])
                    fw.dma(fw.q_sp, rg_d[:, :, tsl], gst[:], reads=[b_gst])
                    fw.dma(fw.q_sp, rbon_d[:, :, tsl], bon[:], reads=[b_bon])
                    for dr in range(2):
                        fw.dma(fw.q_sp, rpend_d[dr][:, :, ti * NCT:(ti + 1) * NCT], pst[dr][:], reads=[b_pst[dr]])
                        for k in KINDS:
                            fw.dma(fw.q_sp, rk_d[(dr, k)][:, :, tsl], stg[(dr, k)][:], reads=[b_stg[(dr, k)]])

        def rwkv_chunks(l):
            with phase():
                big = sb("rbig", [128, 6 * NT], BF16)
                arr = {k: big[:, i_ * NT:(i_ + 1) * NT] for i_, k in enumerate(KINDS)}
                b_arr = {k: Buf() for k in KINDS}
                vv = sb("rvv", [128, NT], BF16)
                b_vv = Buf()
                ysq = big[:, 0:2 * NT].bitcast(F32).rearrange("p (g x) -> p g x", x=64)
                gsb = big[:, 2 * NT:4 * NT].bitcast(F32)
                bsb_ = big[:, 4 * NT:6 * NT].bitcast(F32)
                rwo = vv
                b_ysq = [b_arr["A"], b_arr["B"]]
                b_gsb = [b_arr["K"], b_arr["R"], b_arr["Bh"], b_arr["Kh"]]
                b_rwo = [b_vv]
                st1 = sb("rst1", [128, NCH * 2], F32)
                st2 = sb("rst2", [128, NCH * 2], F32)
                st3 = sb("rst3", [128, NCH * 2], F32)
                gnrow = sb("rgn", [128, 2, 128], F32)
                b_st12 = [Buf()]
                b_gn = [Buf()]
                b_Yl = None
                pend = sb("rpend", [128, NCH], F32)
                b_pend = Buf()
                Yacc = sb("rY", [128, NCH, 128], F32)
                b_Y = Buf()
                S32 = sb("rS32", [128, 64], F32)
                Sbf = sb("rSbf", [128, 64], BF16)
                b_S = Buf()
                def dbl(name, shape, dt):
                    return [sb(f"{name}{i}", shape, dt) for i in range(2)], [Buf() for _ in range(2)]
                Mn, b_Mn = dbl("rMn", [128, 2, 256], BF16)
                Mk, b_Mk = dbl("rMk", [128, 2, 256], BF16)
                NTt, b_NTt = dbl("rNT", [128, 2, 128], BF16)
                Tm, b_Tm = dbl("rT", [128, 2, 128], BF16)
                VT, b_VT = dbl("rVT", [128, 128], BF16)
                BhT, b_BhT = dbl("rBhT", [128, 128], BF16)
                KhT, b_KhT = dbl("rKhT", [128, 128], BF16)
                XX, b_XX = dbl("rXX", [128, 2, 256], BF16)
                Tp, b_Tp = dbl("rTp", [128, 2, 128], BF16)
                XTt = sb("rXTt", [128, 2, 64], BF16)
                UT = sb("rUT", [128, 2, 64], BF16)
                b_XTt, b_UT = Buf(), Buf()
                pall_bf = pall.bitcast(BF16)

                def bank2(i):
                    return pall[:, i * 512:(i + 2) * 512].rearrange("p (h x) -> p h x", h=2)

                def stageA(fc, dr, n, par):
                    csl = slice(n * 128, (n + 1) * 128)
                    msk = tri[:, 0:2, :] if dr == 0 else tri[:, 2:4, :]
                    mskT = tri[:, 2, :] if dr == 0 else tri[:, 0, :]
                    for hh in range(2):
                        rows = slice(64 * hh, 64 * hh + 64)
                        for (bidx, lhs) in ((0, "B"), (2, "K")):
                            bk = banks[bidx + hh]
                            fw.op(fw.pe, lambda: nc.tensor.matmul(bk[:, 0:128], arr[lhs][rows, csl], arr["A"][rows, csl],
                                                                   start=True, stop=True),
                                  reads=[b_arr[lhs], b_arr["A"]], writes=[bbank[bidx + hh]])
                            fw.op(fw.pe, lambda: nc.tensor.matmul(bk[:, 128:256], arr[lhs][rows, csl],
                                                                   arr["R"][rows, csl], start=True, stop=True),
                                  reads=[b_arr[lhs], b_arr["R"]], writes=[bbank[bidx + hh]])
                        fw.op(fw.pe, lambda: nc.tensor.matmul(banks[4 + hh][:, 0:128], arr["A"][rows, csl],
                                                               arr["B"][rows, csl], start=True, stop=True),
                              reads=[b_arr["A"], b_arr["B"]], writes=[bbank[4 + hh]])
                    mb = msk.rearrange("p a x -> p (a x)").unsqueeze(1).to_broadcast([128, 2, 256])
                    fw.op(fw.dve, lambda: nc.vector.tensor_tensor(out=Mn[par][:], in0=bank2(0)[:, :, 0:256], in1=mb,
                                                                   op=ALU.mult),
                          reads=[bbank[0], bbank[1], b_cst], writes=[b_Mn[par]])
                    fw.op(fw.dve, lambda: nc.vector.tensor_tensor(out=Mk[par][:], in0=bank2(2)[:, :, 0:256], in1=mb,
                                                                   op=ALU.mult),
                          reads=[bbank[2], bbank[3], b_cst], writes=[b_Mk[par]])
                    mTb = mskT.unsqueeze(1).to_broadcast([128, 2, 128])
                    fw.op(fw.dve, lambda: nc.vector.tensor_tensor(out=NTt[par][:], in0=bank2(4)[:, :, 0:128], in1=mTb,
                                                                   op=ALU.mult),
                          reads=[bbank[4], bbank[5], b_cst], writes=[b_NTt[par]])
                    yield
                    for (src_t, b_src, dstT, b_dst, col0) in ((vv, b_vv, VT, b_VT, 0), (arr["Bh"], b_arr["Bh"], BhT, b_BhT, 256),
                                                              (arr["Kh"], b_arr["Kh"], KhT, b_KhT, 512)):
                        pv = pall_bf[:, 6 * 1024 + col0: 6 * 1024 + col0 + 128]
                        fw.op(fw.pe, lambda: nc.tensor.transpose(pv, src_t[:, csl], identb[:]),
                              reads=[b_src, b_cst], writes=[bbank[6]])
                        fw.op(fw.act, lambda: nc.scalar.activation(out=dstT[par][:], in_=pv, func=AF.Copy),
                              reads=[bbank[6]], writes=[b_dst[par]])
                    yield
                    identb2 = identb[:].unsqueeze(1).to_broadcast([128, 2, 128])
                    fw.op(fw.dve, lambda: nc.vector.tensor_tensor(out=Tp[0][:], in0=Mn[par][:, :, 0:128], in1=identb2,
                                                                   op=ALU.add),
                          reads=[b_Mn[par], b_cst], writes=[b_Tp[0]])
                    Xc, bXc = (lambda hh: Mn[par][:, hh, 0:128]), b_Mn[par]
                    XTc, bXTc = (lambda hh: NTt[par][:, hh, :]), b_NTt[par]
                    tcur = 0
                    for j in range(1, 7):
                        pp_ = j % 2
                        last = j == 6
                        for hh in range(2):
                            fw.op(fw.pe, lambda: nc.tensor.matmul(banks[0 + hh][:, 0:128], Xc(hh), XTc(hh), start=True,
                                                                   stop=True), reads=[bXc, bXTc], writes=[bbank[hh]])
                            if not last:
                                fw.op(fw.pe, lambda: nc.tensor.matmul(banks[0 + hh][:, 128:256], XTc(hh), Xc(hh),
                                                                       start=True, stop=True),
                                      reads=[bXc, bXTc], writes=[bbank[hh]])
                        wdt = 128 if last else 256
                        fw.op(fw.act, lambda: nc.scalar.activation(out=XX[pp_][:, :, 0:wdt], in_=bank2(0)[:, :, 0:wdt],
                                                                    func=AF.Copy),
                              reads=[bbank[0], bbank[1]], writes=[b_XX[pp_]])
                        dstT_, bdst = (Tm[par], b_Tm[par]) if last else (Tp[1 - tcur], b_Tp[1 - tcur])
                        for hh in range(2):
                            fw.op(fw.pe, lambda: nc.tensor.matmul(banks[4 + hh][:, 0:128], XX[pp_][:, hh, 0:128],
                                                                   Tp[tcur][:, hh, :], start=True, stop=True),
                                  reads=[b_XX[pp_], b_Tp[tcur]], writes=[bbank[4 + hh]])
                        fw.op(fw.dve, lambda: nc.vector.tensor_tensor(out=dstT_[:], in0=bank2(4)[:, :, 0:128],
                                                                       in1=Tp[tcur][:], op=ALU.add),
                              reads=[bbank[4], bbank[5], b_Tp[tcur]], writes=[bdst])
                        tcur = 1 - tcur
                        Xc, bXc = (lambda hh, pp_=pp_: XX[pp_][:, hh, 128:256]), b_XX[pp_]
                        XTc, bXTc = (lambda hh, pp_=pp_: XX[pp_][:, hh, 0:128]), b_XX[pp_]
                        if j % 2 == 0:
                            yield

                def stageB(fc, dr, n, par, first_dir):
                    csl = slice(n * 128, (n + 1) * 128)
                    b7 = banks[7]
                    for hh in range(2):
                        rows = slice(64 * hh, 64 * hh + 64)
                        o = b7[:, hh * 64:(hh + 1) * 64]
                        fw.op(fw.pe, lambda: nc.tensor.matmul(o, arr["A"][rows, csl], Sbf[rows, :], start=True,
                                                               stop=False), reads=[b_arr["A"], b_S], writes=[bbank[7]])
                        fw.op(fw.pe, lambda: nc.tensor.matmul(o, Mk[par][:, hh, 0:128], VT[par][:, hh * 64:(hh + 1) * 64],
                                                               start=False, stop=True),
                              reads=[b_Mk[par], b_VT[par]], writes=[bbank[7]])
                    fw.op(fw.dve, lambda: nc.vector.tensor_copy(out=XTt[:].rearrange("p h x -> p (h x)"),
                                                                 in_=b7[:, 0:128]), reads=[bbank[7]], writes=[b_XTt])
                    yield
                    for hh in range(2):
                        fw.op(fw.pe, lambda: nc.tensor.matmul(b7[:, 128 + hh * 64:128 + (hh + 1) * 64], Tm[par][:, hh, :],
                                                               XTt[:, hh, :], start=True, stop=True),
                              reads=[b_Tm[par], b_XTt], writes=[bbank[7]])
                    fw.op(fw.dve, lambda: nc.vector.tensor_copy(out=UT[:].rearrange("p h x -> p (h x)"),
                                                                 in_=b7[:, 128:256]), reads=[bbank[7]], writes=[b_UT])
                    yield
                    for hh in range(2):
                        rows = slice(64 * hh, 64 * hh + 64)
                        o = b7[rows, 256:320]
                        fw.op(fw.pe, lambda: nc.tensor.matmul(o, BhT[par][:, hh * 64:(hh + 1) * 64], UT[:, hh, :],
                                                               start=True, stop=False),
                              reads=[b_BhT[par], b_UT], writes=[bbank[7]])
                        fw.op(fw.pe, lambda: nc.tensor.matmul(o, KhT[par][:, hh * 64:(hh + 1) * 64],
                                                               VT[par][:, hh * 64:(hh + 1) * 64], start=False, stop=True),
                              reads=[b_KhT[par], b_VT[par]], writes=[bbank[7]])
                    for hh in range(2):
                        rows = slice(64 * hh, 64 * hh + 64)
                        o = b7[:, 320 + hh * 64:320 + (hh + 1) * 64]
                        fw.op(fw.pe, lambda: nc.tensor.matmul(o, arr["R"][rows, csl], Sbf[rows, :], start=True,
                                                               stop=False), reads=[b_arr["R"], b_S], writes=[bbank[7]])
                        fw.op(fw.pe, lambda: nc.tensor.matmul(o, Mn[par][:, hh, 128:256], UT[:, hh, :], start=False,
                                                               stop=False), reads=[b_Mn[par], b_UT], writes=[bbank[7]])
                        fw.op(fw.pe, lambda: nc.tensor.matmul(o, Mk[par][:, hh, 128:256],
                                                               VT[par][:, hh * 64:(hh + 1) * 64], start=False, stop=True),
                              reads=[b_Mk[par], b_VT[par]], writes=[bbank[7]])
                    fw.op(fw.dve, lambda: nc.vector.scalar_tensor_tensor(out=S32[:], in0=S32[:], scalar=pend[:, n:n + 1],
                                                                          in1=b7[:, 256:320], op0=ALU.mult, op1=ALU.add),
                          reads=[bbank[7], b_pend], writes=[b_S])
                    fw.op(fw.act, lambda: nc.scalar.activation(out=Sbf[:], in_=S32[:], func=AF.Copy), reads=[],
                          writes=[b_S])
                    if first_dir:
                        fw.op(fw.act, lambda: nc.scalar.activation(out=Yacc[:, n, :], in_=b7[:, 320:448], func=AF.Copy),
                              reads=[bbank[7]], writes=[b_Y])
                    else:
                        fw.op(fw.dve, lambda: nc.vector.tensor_tensor(out=Yacc[:, n, :], in0=b7[:, 320:448],
                                                                       in1=Yacc[:, n, :], op=ALU.add),
                              reads=[bbank[7]], writes=[b_Y])
                    yield

                def drain(gens):
                    live = list(gens)
                    while live:
                        for g in list(live):
                            try:
                                next(g)
                            except StopIteration:
                                live.remove(g)

                for fc in range(3):
                    fw.dma(fw.q_sp, vv[:], rv_d[:, fc, :], writes=[b_vv])
                    for dr in range(2):
                        for k in KINDS:
                            fw.dma(fw.q_sp, arr[k][:], rk_d[(dr, k)][:, fc, :], writes=[b_arr[k]])
                        fw.dma(fw.q_sp, pend[:], rpend_d[dr][:, fc, :], writes=[b_pend])
                        order = list(range(NCH)) if dr == 0 else list(range(NCH - 1, -1, -1))
                        segstart = {0, SEG // 128} if dr == 0 else {NCH - 1, SEG // 128 - 1}
                        drain([stageA(fc, dr, order[0], 0)])
                        for idx, n in enumerate(order):
                            par = idx % 2
                            if idx == 0:
                                fw.op(fw.dve, lambda: nc.vector.memset(S32[:], 0.0), writes=[b_S])
                                fw.op(fw.dve, lambda: nc.vector.memset(Sbf[:], 0.0), writes=[b_S])
                            elif n in segstart:
                                fw.op(fw.dve, lambda: nc.vector.tensor_scalar(out=S32[:], in0=S32[:], scalar1=lam[:, 0:1],
                                                                               scalar2=None, op0=ALU.mult),
                                      reads=[b_lam], writes=[b_S])
                                fw.op(fw.act, lambda: nc.scalar.activation(out=Sbf[:], in_=S32[:], func=AF.Copy),
                                      reads=[], writes=[b_S])
                            gens = [stageB(fc, dr, n, par, dr == 0)]
                            if idx + 1 < NCH:
                                gens.append(stageA(fc, dr, order[idx + 1], 1 - par))
                            drain(gens)
                    Y2 = Yacc[:].rearrange("p c (h x) -> p (c h) x", h=2)
                    AX = mybir.AxisListType.X
                    fw.dma(fw.q_sp, gnrow[:, 0, :], bass.AP(rw_gn_g_t, l * 384 + fc * 128, [[0, 128], [1, 128]]),
                           writes=b_gn)
                    fw.dma(fw.q_sp, gnrow[:, 1, :], bass.AP(rw_gn_b_t, l * 384 + fc * 128, [[0, 128], [1, 128]]),
                           writes=b_gn)
                    fw.dma(fw.q_sp, gsb[:], rg_d[:, fc, :], writes=b_gsb)
                    fw.dma(fw.q_sp, bsb_[:], rbon_d[:, fc, :], writes=b_gsb)
                    V_ = lambda fn, r, w: fw.op(fw.dve, fn, reads=r, writes=w)
                    V_(lambda: nc.vector.tensor_reduce(out=st1[:], in_=Y2, axis=AX, op=ALU.add), [b_Y], b_st12)
                    V_(lambda: nc.vector.tensor_tensor(out=ysq[:], in0=Y2, in1=Y2, op=ALU.mult), [b_Y], b_ysq)
                    V_(lambda: nc.vector.tensor_reduce(out=st2[:], in_=ysq[:], axis=AX, op=ALU.add), b_ysq, b_st12)
                    V_(lambda: nc.vector.tensor_scalar(out=st1[:], in0=st1[:], scalar1=1.0 / 64, scalar2=None,
                                                       op0=ALU.mult), [], b_st12)
                    V_(lambda: nc.vector.tensor_tensor(out=st3[:], in0=st1[:], in1=st1[:], op=ALU.mult),
                       b_st12, b_st12)
                    V_(lambda: nc.vector.scalar_tensor_tensor(out=st2[:], in0=st2[:], scalar=1.0 / 64, in1=st3[:],
                                                              op0=ALU.mult, op1=ALU.subtract), b_ysq, b_st12)
                    V_(lambda: nc.vector.tensor_scalar(out=st2[:], in0=st2[:], scalar1=64e-5, scalar2=None, op0=ALU.add),
                       [], b_st12)
                    fw.op(fw.act, lambda: nc.scalar.activation(out=st2[:], in_=st2[:], func=AF.Sqrt), reads=[], writes=b_st12)
                    V_(lambda: nc.vector.reciprocal(out=st2[:], in_=st2[:]), [], b_st12)
                    V_(lambda: nc.vector.tensor_tensor(out=Y2, in0=Y2, in1=st1[:].unsqueeze(2).to_broadcast([128, NCH * 2, 64]),
                                                       op=ALU.subtract), b_st12, [b_Y])
                    V_(lambda: nc.vector.tensor_tensor(out=Y2, in0=Y2, in1=st2[:].unsqueeze(2).to_broadcast([128, NCH * 2, 64]),
                                                       op=ALU.mult), b_st12, [b_Y])
                    V_(lambda: nc.vector.tensor_tensor(out=Yacc[:], in0=Yacc[:],
                                                       in1=gnrow[:, 0, :].unsqueeze(1).to_broadcast([128, NCH, 128]),
                                                       op=ALU.mult), b_gn, [b_Y])
                    V_(lambda: nc.vector.tensor_tensor(out=Yacc[:], in0=Yacc[:],
                                                       in1=gnrow[:, 1, :].unsqueeze(1).to_broadcast([128, NCH, 128]),
                                                       op=ALU.add), b_gn, [b_Y])
                    for n in range(NCH):
                        bk = banks[n % 4]
                        fw.op(fw.pe, lambda: nc.tensor.transpose(bk[:, 0:128], Yacc[:, n, :], identf[:]),
                              reads=[b_Y, b_cst], writes=[bbank[n % 4]])
                        csl = slice(n * 128, (n + 1) * 128)
                        V_(lambda: nc.vector.tensor_tensor(out=bsb_[:, csl], in0=bk[:, 0:128], in1=bsb_[:, csl],
                                                           op=ALU.add), [bbank[n % 4]] + b_gsb, b_gsb)
                    V_(lambda: nc.vector.tensor_tensor(out=rwo[:], in0=bsb_[:], in1=gsb[:], op=ALU.mult), b_gsb, b_rwo)
                    fw.dma(fw.q_sp, mix_d[:, 5 + fc, :], rwo[:], reads=b_rwo)

        cur = x_in
        free = [0, 1, 2]
        stages = []
        for l in range(depth):
            stages.append(("ffn", l, 0))
            stages.append(("mix", l))
            stages.append(("ffn", l, 1))
        if stop_after is not None:
            stages = stages[:stop_after]
        for si, stg in enumerate(stages):
            last = si == len(stages) - 1
            if last:
                dst = y_out
            else:
                di = free.pop(0)
                dst = xs[di]
            if stg[0] == "ffn":
                ffn_pass(stg[1], stg[2], cur, dst)
            else:
                l = stg[1]
                mixin_pass(l, cur)
                conv_pass(l)
                attn_pass(l)
                if not no_rwkv:
                    rwkv_prep(l)
                    rwkv_chunks(l)
                mixout_pass(l, cur, dst)
            for k_, a_ in enumerate(xs):
                if a_ is cur:
                    free.append(k_)
            cur = dst
        fw.finish()
        print("instructions:", fw.ninstr)
    return nc


def to_fm(x2d):
    nt = x2d.shape[0]
    return np.ascontiguousarray(x2d.reshape(nt, 8, 128).transpose(2, 1, 0))


def from_fm(y):
    nt = y.shape[2]
    return np.ascontiguousarray(y.transpose(2, 1, 0).reshape(nt, 1024))


_NC_CACHE = {}


def kernel(x_prompt, x_sample, c_prompt, c_sample, w_ada, b_ada, ln_g, ln_b, ffn_w_in, ffn_w_out,
           w_mix_in, w_mix_out, rel_bias, conv_w, rwkv_mu, rwkv_w0, rwkv_w_up, rwkv_a0, rwkv_a_up,
           rwkv_g_up, rwkv_k_k, rwkv_k_a, rwkv_r_k, rwkv_gn_g, rwkv_gn_b):
    f32 = lambda a: np.ascontiguousarray(np.asarray(a, dtype=np.float32))
    inp = dict(b_ada=b_ada, ln_g=ln_g, ln_b=ln_b, conv_w=conv_w, rwkv_mu=rwkv_mu, rwkv_w0=rwkv_w0, rwkv_a0=rwkv_a0,
               rwkv_k_k=rwkv_k_k, rwkv_k_a=rwkv_k_a, rwkv_r_k=rwkv_r_k, rwkv_gn_g=rwkv_gn_g, rwkv_gn_b=rwkv_gn_b)
    x_prompt = np.asarray(x_prompt, np.float32)
    x_sample = np.asarray(x_sample, np.float32)
    c_prompt = np.asarray(c_prompt, np.float32)
    c_sample = np.asarray(c_sample, np.float32)
    SEG = 4096
    if "nc" not in _NC_CACHE:
        _NC_CACHE["nc"] = build(SEG=SEG)
    nc = _NC_CACHE["nc"]
    oh, jx = static_consts()
    shared = {"pp": pack_pp(inp), "w_ada": f32(w_ada), "ffn_w_in": f32(ffn_w_in), "ffn_w_out": f32(ffn_w_out),
              "w_mix_in": f32(w_mix_in), "w_mix_out": f32(w_mix_out), "rel_bias": f32(rel_bias), "oh": oh, "jx": jx,
              "cst": static_cst(), "rwkv_w_up": f32(rwkv_w_up), "rwkv_a_up": f32(rwkv_a_up),
              "rwkv_g_up": f32(rwkv_g_up), "rwkv_gn_g": f32(rwkv_gn_g), "rwkv_gn_b": f32(rwkv_gn_b)}

    def cT_of(c0, c1):
        return np.ascontiguousarray(np.stack([c0, c1], -1).reshape(8, 128, 2).transpose(1, 0, 2))

    per_core = []
    for b in range(2):
        per_core.append({"x": to_fm(x_prompt[b]), "cT": cT_of(c_prompt[b], c_prompt[b]),
                         "lam": np.ones((128, 1), np.float32)})
    for i in range(2):
        xx = np.concatenate([x_sample[2 * i], x_sample[2 * i + 1]], 0)
        per_core.append({"x": to_fm(xx), "cT": cT_of(c_sample[2 * i], c_sample[2 * i + 1]),
                         "lam": np.zeros((128, 1), np.float32)})
    in_maps = []
    for core in range(8):
        m = dict(shared)
        m.update(per_core[core % 4])
        in_maps.append(m)
    res = run_bass_kernel_spmd(nc, in_maps, core_ids=list(range(8)))
    outs = [from_fm(res.results[c]["y"]) for c in range(4)]
    y_prompt = np.stack([outs[0], outs[1]], 0).astype(np.float32)
    y_sample = np.stack([outs[2][:SEG], outs[2][SEG:], outs[3][:SEG], outs[3][SEG:]], 0).astype(np.float32)
    return (y_prompt, y_sample)
```

```python
import numpy as np
from contextlib import ExitStack, contextmanager
import concourse.bass as bass
import concourse.mybir as mybir
from concourse.bass_utils import run_bass_kernel_spmd

F32 = mybir.dt.float32
BF16 = mybir.dt.bfloat16
AF = mybir.ActivationFunctionType
ALU = mybir.AluOpType

D = 1024
DFF = 2816
DEPTH = 2
ALPHA = (2 * DEPTH) ** 0.25
LN_EPS = 1e-5
EPOCH = 16000


class Buf:
    __slots__ = ("w", "r", "name")

    def __init__(self, name=""):
        self.w = None
        self.r = []
        self.name = name


class Eng:
    def __init__(self, fw, name, eng, order_free=False):
        self.fw = fw
        self.name = name
        self.eng = eng
        self.n = 0
        self.sems = []
        self.seen = {}
        self.order_free = order_free

    def sem_val(self, n):
        ep = (n - 1) // EPOCH
        while len(self.sems) <= ep:
            self.sems.append(self.fw.new_sem(f"{self.name}_e{len(self.sems)}"))
        return self.sems[ep], n - ep * EPOCH


class DmaQ:
    def __init__(self, fw, name, eng_wrapper, nslots):
        self.name = name
        self.E = eng_wrapper
        self.nslots = nslots
        self.sems = [fw.new_sem(f"dq_{name}_{i}") for i in range(nslots)]
        self.cnt = [0] * nslots
        self.next = 0


class FW:
    def __init__(self, nc, stack):
        self.nc = nc
        self.stack = stack
        self.pe = Eng(self, "pe", nc.tensor, order_free=True)
        self.act = Eng(self, "act", nc.scalar)
        self.dve = Eng(self, "dve", nc.vector)
        self.pool = Eng(self, "pool", nc.gpsimd)
        self.sp = Eng(self, "sp", nc.sync)
        self.engs = {e.name: e for e in (self.pe, self.act, self.dve, self.pool, self.sp)}
        self.q_sp = DmaQ(self, "sp", self.sp, 24)
        self.q_pool = DmaQ(self, "pool", self.pool, 12)
        self.ninstr = 0

    def new_sem(self, name):
        return self.stack.enter_context(self.nc.semaphore(name))

    def _wait(self, E, tok):
        if tok[0] == "e":
            src = self.engs[tok[1]]
            n = tok[2]
            if src is E and E.order_free:
                return
            key = tok[1]
            if E.seen.get(key, 0) >= n:
                return
            sem, val = src.sem_val(n)
            E.eng.wait_ge(sem, val)
            E.seen[key] = n
        else:
            q, slot, n = tok[1], tok[2], tok[3]
            key = (q.name, slot)
            if E.seen.get(key, 0) >= n:
                return
            E.eng.wait_ge(q.sems[slot], 16 * n)
            E.seen[key] = n

    def _deps(self, E, reads, writes):
        toks = []
        for b in reads:
            if b.w is not None:
                toks.append(b.w)
        for b in writes:
            if b.w is not None:
                toks.append(b.w)
            toks.extend(b.r)
        best = {}
        for t in toks:
            key = t[:-1]
            if key not in best or best[key][-1] < t[-1]:
                best[key] = t
        for t in best.values():
            self._wait(E, t)

    def _commit(self, tok, reads, writes):
        for b in reads:
            b.r.append(tok)
            if len(b.r) > 48:
                best = {}
                for t in b.r:
                    k = t[:-1]
                    if k not in best or best[k][-1] < t[-1]:
                        best[k] = t
                b.r = list(best.values())
        for b in writes:
            b.w = tok
            b.r = []

    def op(self, E, fn, reads=(), writes=()):
        self._deps(E, reads, writes)
        ins = fn()
        E.n += 1
        sem, _ = E.sem_val(E.n)
        ins.then_inc(sem, 1)
        tok = ("e", E.name, E.n)
        self._commit(tok, reads, writes)
        self.ninstr += 1
        return tok

    def dma(self, q, out, in_, reads=(), writes=(), **kw):
        E = q.E
        slot = q.next
        q.next = (q.next + 1) % q.nslots
        if q.cnt[slot] > 0:
            self._wait(E, ("d", q, slot, q.cnt[slot]))
        self._deps(E, reads, writes)
        ins = E.eng.dma_start(out=out, in_=in_, **kw)
        q.cnt[slot] += 1
        ins.then_inc(q.sems[slot], 16)
        tok = ("d", q, slot, q.cnt[slot])
        self._commit(tok, reads, writes)
        self.ninstr += 1
        return tok

    def barrier(self):
        for E in self.engs.values():
            for q in (self.q_sp, self.q_pool):
                for sl in range(q.nslots):
                    if q.cnt[sl] > 0:
                        self._wait(E, ("d", q, sl, q.cnt[sl]))
            for e in self.engs.values():
                if e is not E and e.n > 0:
                    self._wait(E, ("e", e.name, e.n))

    def finish(self):
        for q in (self.q_sp, self.q_pool):
            for s in range(q.nslots):
                if q.cnt[s] > 0:
                    self._wait(self.sp, ("d", q, s, q.cnt[s]))
        for e in self.engs.values():
            if e is not self.sp and e.n > 0:
                self._wait(self.sp, ("e", e.name, e.n))


def pp_layout():
    off = {}
    n = 0

    def add(name, cols):
        nonlocal n
        off[name] = n
        n += cols

    add("b_ada", DEPTH * 72)
    add("ln_g", DEPTH * 3 * 8)
    add("ln_b", DEPTH * 3 * 8)
    add("conv_w", DEPTH * 3 * 2)
    add("mu", DEPTH * 12)
    add("w0", DEPTH * 2 * 3)
    add("a0", DEPTH * 2 * 3)
    add("k_k", DEPTH * 3)
    add("k_a", DEPTH * 3)
    add("r_k", DEPTH * 3)
    add("gn_g", DEPTH * 3)
    add("gn_b", DEPTH * 3)
    return off, n


def pack_pp(inp):
    off, n = pp_layout()
    pp = np.zeros((128, n), np.float32)

    def put(name, arr2d):
        a = np.asarray(arr2d, np.float32).reshape(-1, 128).T
        pp[:, off[name]:off[name] + a.shape[1]] = a

    put("b_ada", inp["b_ada"])
    put("ln_g", inp["ln_g"])
    put("ln_b", inp["ln_b"])
    put("conv_w", inp["conv_w"])
    put("mu", inp["rwkv_mu"])
    put("w0", inp["rwkv_w0"])
    put("a0", inp["rwkv_a0"])
    put("k_k", inp["rwkv_k_k"])
    put("k_a", inp["rwkv_k_a"])
    put("r_k", inp["rwkv_r_k"])
    put("gn_g", inp["rwkv_gn_g"])
    put("gn_b", inp["rwkv_gn_b"])
    return pp


N_BUCKETS = 32
REL_MAX_DIST = 1024
DILS = (1, 4, 16)


def _t5_bucket(rel):
    half = N_BUCKETS // 2
    exact = half // 2
    n = np.abs(rel)
    large = exact + (np.log(np.maximum(n, exact) / exact) / np.log(REL_MAX_DIST / exact) * (half - exact)).astype(np.int64)
    large = np.minimum(large, half - 1)
    return (np.where(rel > 0, half, 0) + np.where(n < exact, n, large)).astype(np.int32)


def static_consts():
    oh = np.zeros((32, 3 * 512), np.float32)
    rel = np.arange(-64, 65)
    for bi, d in enumerate(DILS):
        b = _t5_bucket(rel * d)
        oh[b, bi * 512 + 256 + rel] = 1.0
    jx = np.ascontiguousarray(np.eye(128, dtype=np.float32)[::-1])
    return oh, jx


def static_cst():
    c = np.zeros((128, 1280), np.float32)
    i = np.arange(128)[:, None]
    t = np.arange(128)[None, :]
    c[:, 0:128] = (i < t)
    c[:, 128:256] = (i <= t)
    c[:, 256:384] = (i > t)
    c[:, 384:512] = (i >= t)
    c[:, 512:640] = np.eye(128)
    c[:, 640:768] = ((i // 64) == (t // 64))
    r = np.ones(512, np.float32)
    r[0::128] = 0.0
    c[:, 768:1280] = r[None, :]
    return c


EXTRA_INPUTS = ("rwkv_w_up", "rwkv_a_up", "rwkv_g_up", "rwkv_gn_g", "rwkv_gn_b")


class K:
    pass


def build(SEG=4096, depth=DEPTH, stop_after=None, T=256, no_rwkv=False):
    NT = 2 * SEG
    nc = bass.Bass("TRN2", target_bir_lowering=False)
    ppoff, NPP = pp_layout()
    x_in = nc.dram_tensor("x", [128, 8, NT], F32, kind="ExternalInput").ap()
    cT_in = nc.dram_tensor("cT", [128, 8, 2], F32, kind="ExternalInput").ap()
    pp_in = nc.dram_tensor("pp", [128, NPP], F32, kind="ExternalInput").ap()
    w_ada = nc.dram_tensor("w_ada", [DEPTH, D, 9 * D], F32, kind="ExternalInput").ap()
    ffn_w_in = nc.dram_tensor("ffn_w_in", [DEPTH, 2, D, 2 * DFF], F32, kind="ExternalInput").ap()
    ffn_w_out = nc.dram_tensor("ffn_w_out", [DEPTH, 2, DFF, D], F32, kind="ExternalInput").ap()
    y_out = nc.dram_tensor("y", [128, 8, NT], F32, kind="ExternalOutput").ap()
    xs = [nc.dram_tensor(f"xs{i}", [128, 8, NT], F32, kind="Internal").ap() for i in range(3)]
    PADV = 1024
    PADK = 1024
    lam_in = nc.dram_tensor("lam", [128, 1], F32, kind="ExternalInput").ap()
    w_mix_in = nc.dram_tensor("w_mix_in", [DEPTH, D, 3456], F32, kind="ExternalInput").ap()
    w_mix_out = nc.dram_tensor("w_mix_out", [DEPTH, D, D], F32, kind="ExternalInput").ap()
    rel_bias_in = nc.dram_tensor("rel_bias", [32, 6], F32, kind="ExternalInput").ap()
    oh_in = nc.dram_tensor("oh", [32, 1536], F32, kind="ExternalInput").ap()
    jx_in = nc.dram_tensor("jx", [128, 128], F32, kind="ExternalInput").ap()
    qT_d = nc.dram_tensor("qT_d", [128, 3, NT], BF16, kind="Internal").ap()
    kT_d = nc.dram_tensor("kT_d", [128, 3, NT], BF16, kind="Internal").ap()
    V_t = nc.dram_tensor("V_d", [NT + 2 * PADV, 384], BF16, kind="Internal")
    V_d = V_t.ap()
    gb_d = nc.dram_tensor("gb_d", [128, 2, NT], F32, kind="Internal").ap()
    uc_d = nc.dram_tensor("uc_d", [128, 2, NT], F32, kind="Internal").ap()
    z_d = nc.dram_tensor("z_d", [128, 12, NT], F32, kind="Internal").ap()
    mix_d = nc.dram_tensor("mix_d", [128, 8, NT], BF16, kind="Internal").ap()
    rw_w_up = nc.dram_tensor("rwkv_w_up", [DEPTH, 2, 64, 384], F32, kind="ExternalInput").ap()
    rw_a_up = nc.dram_tensor("rwkv_a_up", [DEPTH, 2, 64, 384], F32, kind="ExternalInput").ap()
    rw_g_up = nc.dram_tensor("rwkv_g_up", [DEPTH, 128, 384], F32, kind="ExternalInput").ap()
    rw_gn_g_t = nc.dram_tensor("rwkv_gn_g", [DEPTH, 384], F32, kind="ExternalInput")
    rw_gn_b_t = nc.dram_tensor("rwkv_gn_b", [DEPTH, 384], F32, kind="ExternalInput")
    cst_in = nc.dram_tensor("cst", [128, 1280], F32, kind="ExternalInput").ap()
    NCH_ = NT // 128
    rv_d = nc.dram_tensor("rv_d", [128, 3, NT], BF16, kind="Internal").ap()
    rg_d = nc.dram_tensor("rg_d", [128, 3, NT], F32, kind="Internal").ap()
    rbon_d = nc.dram_tensor("rbon_d", [128, 3, NT], F32, kind="Internal").ap()
    rpend_d = [nc.dram_tensor(f"rpend_d{i}", [128, 3, NCH_], F32, kind="Internal").ap() for i in range(2)]
    rk_d = {(dr, k): nc.dram_tensor(f"rk_d{dr}{k}", [128, 3, NT], BF16, kind="Internal").ap()
            for dr in range(2) for k in ("A", "B", "K", "R", "Bh", "Kh")}
    wd_t = nc.dram_tensor("wd_d", [6, 1536], F32, kind="Internal")
    wd_d = wd_t.ap()

    NTILE = NT // T
    with ExitStack() as st:
        fw = FW(nc, st)

        stk = [st]
        uniq = [0]

        def sb(name, shape, dt):
            uniq[0] += 1
            return stk[-1].enter_context(nc.sbuf_tensor(f"s_{name}_{uniq[0]}", shape, dt))

        @contextmanager
        def phase():
            fw.barrier()
            with ExitStack() as ph:
                stk.append(ph)
                yield
                stk.pop()
                fw.barrier()

        def psum(name, shape, dt=F32):
            return st.enter_context(nc.psum_tensor("p_" + name, shape, dt))

        wl_cnt = [0]

        def wload(dst, src_ap, stg, b_stg, b_dst):
            k = wl_cnt[0]
            wl_cnt[0] += 1
            si = k % len(stg)
            fw.dma(fw.q_sp, stg[si], src_ap, writes=[b_stg[si]])
            if k % 2 == 0:
                fw.op(fw.pool, lambda: nc.gpsimd.tensor_copy(out=dst, in_=stg[si]), reads=[b_stg[si]], writes=[b_dst])
            else:
                fw.op(fw.act, lambda: nc.scalar.activation(out=dst, in_=stg[si], func=AF.Copy), reads=[b_stg[si]],
                      writes=[b_dst])

        pp = sb("pp", [128, NPP], F32)
        b_pp = Buf("pp")
        fw.dma(fw.q_sp, pp[:], pp_in[:, :], writes=[b_pp])
        cT = sb("cT", [128, 8, 2], F32)
        b_cT = Buf("cT")
        fw.dma(fw.q_sp, cT[:], cT_in[:, :, :], writes=[b_cT])
        cTb = sb("cTb", [128, 8, 2], BF16)
        fw.op(fw.dve, lambda: nc.vector.tensor_copy(out=cTb[:], in_=cT[:]), reads=[b_cT], writes=[b_cT])
        ones_bf = sb("ones_bf", [128, 128], BF16)
        b_ones = Buf("ones")
        fw.op(fw.dve, lambda: nc.vector.memset(ones_bf[:], 1.0), writes=[b_ones])

        modv = sb("modv", [128, DEPTH, 72, 2], F32)
        b_mod = Buf("mod")
        pall = psum("all", [128, 4096], F32)
        banks = [pall[:, i * 512:(i + 1) * 512] for i in range(8)]
        bbank = [Buf(f"bank{i}") for i in range(8)]

        WA_COLS = 1152
        ada_ph = phase()
        ada_ph.__enter__()
        wa = sb("wa", [128, 8, WA_COLS], BF16)
        b_wa = Buf("wa")
        wast = [sb(f"wast{i}", [128, WA_COLS], F32) for i in range(2)]
        b_wast = [Buf(), Buf()]
        for l in range(depth):
            for piece in range(9 * D // WA_COLS):
                for kc in range(8):
                    wload(wa[:, kc, :], w_ada[l, kc * 128:(kc + 1) * 128, piece * WA_COLS:(piece + 1) * WA_COLS],
                          [w_[:] for w_ in wast], b_wast, b_wa)
                pb = banks[piece % 2]
                bpb = bbank[piece % 2]
                nj = WA_COLS // 128
                for j in range(nj):
                    for kc in range(8):
                        fw.op(fw.pe, lambda j=j, kc=kc: nc.tensor.matmul(
                            pb[:, j * 2:(j + 1) * 2], wa[:, kc, j * 128:(j + 1) * 128], cTb[:, kc, :],
                            start=(kc == 0), stop=(kc == 7)), reads=[b_wa, b_cT], writes=[bpb])
                for j in range(nj):
                    jj = piece * nj + j
                    col = ppoff["b_ada"] + l * 72 + jj
                    fw.op(fw.dve, lambda j=j, jj=jj, col=col: nc.vector.tensor_scalar(
                        out=modv[:, l, jj, :], in0=pb[:, j * 2:(j + 1) * 2], scalar1=pp[:, col:col + 1],
                        scalar2=None, op0=ALU.add), reads=[bpb, b_pp], writes=[b_mod])
        ada_ph.__exit__(None, None, None)
        sc1 = sb("sc1", [128, DEPTH, 3, 8, 2], F32)
        cg = sb("cg", [128, DEPTH, 3, 8, 2], F32)
        for l in range(depth):
            for i in range(3):
                coef = (0.5 if i != 1 else 1.0) / ALPHA
                j1 = (i * 3 + 1) * 8
                j2 = (i * 3 + 2) * 8
                fw.op(fw.dve, lambda l=l, i=i, j1=j1: nc.vector.tensor_scalar(
                    out=sc1[:, l, i, :, :], in0=modv[:, l, j1:j1 + 8, :], scalar1=1.0, scalar2=None, op0=ALU.add),
                    reads=[b_mod], writes=[b_mod])
                fw.op(fw.dve, lambda l=l, i=i, j2=j2, coef=coef: nc.vector.tensor_scalar(
                    out=cg[:, l, i, :, :], in0=modv[:, l, j2:j2 + 8, :], scalar1=1.0, scalar2=coef,
                    op0=ALU.add, op1=ALU.mult), reads=[b_mod], writes=[b_mod])

        def shift_ap(l, i, c, s):
            j = (i * 3 + 0) * 8 + c
            return modv[:, l, j, s:s + 1]

        class TB:
            pass
        tb = TB()

        def alloc_tb(small=False):
            tb.xt = [sb(f"xt{i}", [128, 8, T], F32) for i in range(2)]
            tb.b_xt = [Buf(f"xt{i}") for i in range(2)]
            tb.ht = [sb(f"ht{i}", [128, 8, T], BF16) for i in range(2)]
            tb.b_ht = [Buf(f"ht{i}") for i in range(2)]
            tb.vt = sb("vt", [128, 8, T], F32)
            tb.b_vt = Buf("vt")
            if not small:
                tb.vb = sb("vb", [128, 8, T], BF16)
                tb.b_vb = Buf("vb")
                tb.sqb = sb("sqb", [128, 8, T], BF16)
            tb.st_m = sb("st_m", [128, T], F32)
            tb.st_r = sb("st_r", [128, T], F32)
            tb.st_q = sb("st_q", [128, T], F32)
            tb.b_st = Buf("st")

        def load_x(src, ti, par):
            fw.dma(fw.q_sp, tb.xt[par][:], src[:, :, ti * T:(ti + 1) * T], reads=[src_buf(src, ti)],
                   writes=[tb.b_xt[par]])

        dram_bufs = {}

        def src_buf(ap, ti):
            key = (ap.name, ti)
            if key not in dram_bufs:
                dram_bufs[key] = Buf(str(key))
            return dram_bufs[key]

        def modulate(l, i, ti, par):
            s = (ti * T) // SEG
            for c in range(8):
                E = fw.act if c % 2 == 0 else fw.dve
                if E is fw.act:
                    fw.op(E, lambda c=c: nc.scalar.activation(
                        out=tb.ht[par][:, c, :], in_=tb.xt[par][:, c, :], func=AF.Identity,
                        scale=sc1[:, l, i, c, s:s + 1], bias=shift_ap(l, i, c, s)),
                        reads=[tb.b_xt[par], b_mod], writes=[tb.b_ht[par]])
                else:
                    fw.op(E, lambda c=c: nc.vector.tensor_scalar(
                        out=tb.ht[par][:, c, :], in0=tb.xt[par][:, c, :], scalar1=sc1[:, l, i, c, s:s + 1],
                        scalar2=shift_ap(l, i, c, s), op0=ALU.mult, op1=ALU.add),
                        reads=[tb.b_xt[par], b_mod], writes=[tb.b_ht[par]])

        def drain(gens):
            live = list(gens)
            while live:
                for g in list(live):
                    try:
                        next(g)
                    except StopIteration:
                        live.remove(g)

        def post_norm_gen(l, i, ti, dst, vbt, sqt, b_vs, vt_=None, b_vt_=None, fast=False):
            eps = LN_EPS / (ALPHA * ALPHA)
            if vt_ is None:
                vt_, b_vt_ = tb.vt, tb.b_vt
            if fast:
                fw.op(fw.act, lambda: nc.scalar.activation(out=vbt, in_=vt_[:], func=AF.Copy), reads=[b_vt_],
                      writes=[b_vs])
                fw.op(fw.dve, lambda: nc.vector.tensor_tensor(out=sqt, in0=vt_[:], in1=vt_[:], op=ALU.mult),
                      reads=[b_vt_], writes=[b_vs])
                yield
            else:
                fw.op(fw.pool, lambda: nc.gpsimd.tensor_copy(out=vbt, in_=vt_[:]), reads=[b_vt_], writes=[b_vs])
                fw.op(fw.pool, lambda: nc.gpsimd.tensor_tensor(out=sqt, in0=vt_[:], in1=vt_[:], op=ALU.mult),
                      reads=[b_vt_], writes=[b_vs])
                for _ in range(9):
                    yield
            sbk = banks[7]
            bs = bbank[7]
            for c in range(8):
                fw.op(fw.pe, lambda: nc.tensor.matmul(sbk[:, 0:T], ones_bf[:], vbt[:, c, :],
                                                       start=(c == 0), stop=(c == 7)),
                      reads=[b_ones, b_vs], writes=[bs])
            for c in range(8):
                fw.op(fw.pe, lambda: nc.tensor.matmul(sbk[:, T:2 * T], ones_bf[:], sqt[:, c, :],
                                                       start=(c == 0), stop=(c == 7)),
                      reads=[b_ones, b_vs], writes=[bs])
            yield
            fw.op(fw.act, lambda: nc.scalar.activation(out=tb.st_m[:], in_=sbk[:, 0:T], func=AF.Copy, scale=1.0 / D),
                  reads=[bs], writes=[tb.b_st])
            fw.op(fw.dve, lambda: nc.vector.tensor_tensor(out=tb.st_q[:], in0=tb.st_m[:], in1=tb.st_m[:], op=ALU.mult),
                  reads=[tb.b_st], writes=[tb.b_st])
            yield
            fw.op(fw.dve, lambda: nc.vector.scalar_tensor_tensor(
                out=tb.st_q[:], in0=sbk[:, T:2 * T], scalar=1.0 / D, in1=tb.st_q[:], op0=ALU.mult, op1=ALU.subtract),
                reads=[bs, tb.b_st], writes=[tb.b_st])
            fw.op(fw.dve, lambda: nc.vector.tensor_scalar(out=tb.st_q[:], in0=tb.st_q[:], scalar1=eps, scalar2=None,
                                                           op0=ALU.add), reads=[tb.b_st], writes=[tb.b_st])
            yield
            fw.op(fw.act, lambda: nc.scalar.activation(out=tb.st_q[:], in_=tb.st_q[:], func=AF.Sqrt),
                  reads=[tb.b_st], writes=[tb.b_st])
            yield
            fw.op(fw.dve, lambda: nc.vector.reciprocal(out=tb.st_r[:], in_=tb.st_q[:]), reads=[tb.b_st], writes=[tb.b_st])
            fw.op(fw.dve, lambda: nc.vector.tensor_tensor(
                out=vt_[:], in0=vt_[:], in1=tb.st_m[:].unsqueeze(1).to_broadcast([128, 8, T]), op=ALU.subtract),
                reads=[b_vt_, tb.b_st], writes=[b_vt_])
            yield
            if fast:
                fw.op(fw.dve, lambda: nc.vector.tensor_tensor(
                    out=vt_[:], in0=vt_[:], in1=tb.st_r[:].unsqueeze(1).to_broadcast([128, 8, T]), op=ALU.mult),
                    reads=[b_vt_, tb.b_st], writes=[b_vt_])
            else:
                fw.op(fw.pool, lambda: nc.gpsimd.tensor_tensor(
                    out=vt_[:], in0=vt_[:], in1=tb.st_r[:].unsqueeze(1).to_broadcast([128, 8, T]), op=ALU.mult),
                    reads=[b_vt_, tb.b_st], writes=[b_vt_])
            yield
            gcol = ppoff["ln_g"] + (l * 3 + i) * 8
            bcol = ppoff["ln_b"] + (l * 3 + i) * 8
            for c in range(8):
                if c % 2 == 0 and not fast:
                    fw.op(fw.pool, lambda: nc.gpsimd.tensor_scalar(
                        out=vt_[:, c, :], in0=vt_[:, c, :], scalar1=pp[:, gcol + c:gcol + c + 1],
                        scalar2=pp[:, bcol + c:bcol + c + 1], op0=ALU.mult, op1=ALU.add),
                        reads=[b_vt_, b_pp], writes=[b_vt_])
                else:
                    fw.op(fw.act, lambda: nc.scalar.activation(
                        out=vt_[:, c, :], in_=vt_[:, c, :], func=AF.Identity,
                        scale=pp[:, gcol + c:gcol + c + 1], bias=pp[:, bcol + c:bcol + c + 1]),
                        reads=[b_vt_, b_pp], writes=[b_vt_])
                    yield
            fw.dma(fw.q_sp, dst[:, :, ti * T:(ti + 1) * T], vt_[:], reads=[b_vt_],
                   writes=[src_buf(dst, ti)])

        def post_norm(l, i, ti, par, dst):
            drain([post_norm_gen(l, i, ti, dst, tb.vb[:], tb.sqb[:], tb.b_vb)])

        def ffn_pass(l, j, src, dst):
            with phase():
                _ffn_pass(l, j, src, dst)

        def _ffn_pass(l, j, src, dst):
            alloc_tb(small=True)
            w_in_sb = sb("w_in_sb", [128, 8, 2 * DFF], BF16)
            w_out_sb = sb("w_out_sb", [128, 22, D], BF16)
            b_win = Buf("win")
            b_wout = Buf("wout")
            hid = [sb(f"hid{i}", [128, 22, T], BF16) for i in range(2)]
            b_hid = [Buf(f"hid{i}") for i in range(2)]
            sg = [sb(f"sg{i}", [128, T], F32) for i in range(2)]
            b_sg = [Buf(f"sg{i}") for i in range(2)]
            i = 0 if j == 0 else 2
            stg_in = [h_[:].rearrange("p a b -> p (a b)").bitcast(F32) for h_ in hid]
            for kc in range(8):
                for hf in range(2):
                    wload(w_in_sb[:, kc, hf * DFF:(hf + 1) * DFF],
                          ffn_w_in[l, j, kc * 128:(kc + 1) * 128, hf * DFF:(hf + 1) * DFF], stg_in, b_hid, b_win)
            stg_out = [v_[:, 0:2048].rearrange("p (a b) -> p a b", a=2) for v_ in stg_in]
            for f2 in range(11):
                wload(w_out_sb[:, 2 * f2:2 * f2 + 2, :],
                      ffn_w_out[l, j, f2 * 256:(f2 + 1) * 256, :].rearrange("(a p) d -> p a d", p=128),
                      stg_out, b_hid, b_wout)

            def S1a(ti):
                load_x(src, ti, ti % 2)
                modulate(l, i, ti, ti % 2)

            def S1b(ti):
                par = ti % 2
                for fc in range(22):
                    bk = banks[fc % 5]
                    bbk = bbank[fc % 5]
                    for kc in range(8):
                        fw.op(fw.pe, lambda: nc.tensor.matmul(
                            bk[:, 0:T], w_in_sb[:, kc, fc * 128:(fc + 1) * 128], tb.ht[par][:, kc, :],
                            start=(kc == 0), stop=(kc == 7)), reads=[b_win, tb.b_ht[par]], writes=[bbk])
                    for kc in range(8):
                        fw.op(fw.pe, lambda: nc.tensor.matmul(
                            bk[:, T:2 * T], w_in_sb[:, kc, DFF + fc * 128:DFF + (fc + 1) * 128], tb.ht[par][:, kc, :],
                            start=(kc == 0), stop=(kc == 7)), reads=[b_win, tb.b_ht[par]], writes=[bbk])
                    sp_ = fc % 2
                    fw.op(fw.act, lambda: nc.scalar.activation(out=sg[sp_][:], in_=bk[:, 0:T], func=AF.Silu),
                          reads=[bbk], writes=[b_sg[sp_]])
                    fw.op(fw.dve, lambda: nc.vector.tensor_tensor(out=hid[par][:, fc, :], in0=bk[:, T:2 * T],
                                                                   in1=sg[sp_][:], op=ALU.mult),
                          reads=[bbk, b_sg[sp_]], writes=[b_hid[par]])
                    yield

            def S2a(ti):
                par = ti % 2
                s = (ti * T) // SEG
                for dc in range(8):
                    bk = banks[5 + dc % 2]
                    bbk = bbank[5 + dc % 2]
                    for fc in range(22):
                        fw.op(fw.pe, lambda: nc.tensor.matmul(
                            bk[:, 0:T], w_out_sb[:, fc, dc * 128:(dc + 1) * 128], hid[par][:, fc, :],
                            start=(fc == 0), stop=(fc == 21)), reads=[b_wout, b_hid[par]], writes=[bbk])
                    fw.op(fw.dve, lambda: nc.vector.scalar_tensor_tensor(
                        out=tb.vt[:, dc, :], in0=bk[:, 0:T], scalar=cg[:, l, i, dc, s:s + 1], in1=tb.xt[par][:, dc, :],
                        op0=ALU.mult, op1=ALU.add), reads=[bbk, b_mod, tb.b_xt[par]], writes=[tb.b_vt])

            def S2b(ti):
                par = ti % 2
                return post_norm_gen(l, i, ti, dst, hid[par][:, 0:8, :], hid[par][:, 8:16, :], b_hid[par])

            def S1m(ti):
                for _ in range(8):
                    yield
                modulate(l, i, ti, ti % 2)

            S1a(0)
            if NTILE > 1:
                S1a(1)
            drain([S1b(0)])
            for ti in range(NTILE):
                S2a(ti)
                gens = [S2b(ti)]
                if ti + 1 < NTILE:
                    gens.append(S1b(ti + 1))
                if ti + 2 < NTILE:
                    load_x(src, ti + 2, ti % 2)
                    gens.append(S1m(ti + 2))
                drain(gens)

        lam = sb("lam", [128, 1], F32)
        lm1 = sb("lm1", [128, 1], F32)
        b_lam = Buf("lam")
        fw.dma(fw.q_sp, lam[:], lam_in[:, :], writes=[b_lam])
        fw.op(fw.dve, lambda: nc.vector.tensor_scalar(out=lm1[:], in0=lam[:], scalar1=-1.0, scalar2=None, op0=ALU.add),
              reads=[b_lam], writes=[b_lam])
        jx = sb("jx", [128, 128], F32)
        b_jx = Buf("jx")
        fw.dma(fw.q_sp, jx[:], jx_in[:, :], writes=[b_jx])
        cstb = sb("cstb", [128, 768], BF16)
        identf = sb("identf", [128, 128], F32)
        rstm = sb("rstm", [128, 512], F32)
        b_cst = Buf("cst")
        fw.dma(fw.q_pool, cstb[:], cst_in[:, 0:768], writes=[b_cst])
        fw.dma(fw.q_sp, identf[:], cst_in[:, 512:640], writes=[b_cst])
        fw.dma(fw.q_sp, rstm[:], cst_in[:, 768:1280], writes=[b_cst])
        tri = cstb[:, 0:512].rearrange("p (a x) -> p a x", a=4)
        identb = cstb[:, 512:640]
        blk1 = cstb[:, 640:768]
        with phase():
            zt = sb("zt", [128, 8, 384], BF16)
            b_zt = Buf("zt")
            fw.op(fw.dve, lambda: nc.vector.memset(zt[:], 0.0), writes=[b_zt])
            fw.dma(fw.q_sp, V_d[0:PADV, :].rearrange("(b p) f -> p b f", p=128), zt[:], reads=[b_zt])
            fw.dma(fw.q_sp, V_d[PADV + NT:PADV + NT + PADV, :].rearrange("(b p) f -> p b f", p=128), zt[:],
                   reads=[b_zt])
            ztm = sb("ztm", [128, 3, NT], BF16)
            b_ztm = Buf("ztm")
            fw.op(fw.pool, lambda: nc.gpsimd.memset(ztm[:], 0.0), writes=[b_ztm])
            fw.dma(fw.q_sp, mix_d[:, 5:8, :], ztm[:], reads=[b_ztm])
            rb = sb("rb", [32, 6], F32)
            oh = sb("oh", [32, 1536], F32)
            on6 = sb("on6", [32, 6], F32)
            wv = sb("wv", [6, 1536], F32)
            wm = sb("wm", [6, 1536], F32)
            b_rb = Buf("rb")
            b_wv = Buf("wv")
            fw.dma(fw.q_sp, rb[:], rel_bias_in[:, :], writes=[b_rb])
            fw.dma(fw.q_sp, oh[:], oh_in[:, :], writes=[b_rb])
            fw.op(fw.dve, lambda: nc.vector.memset(on6[:], 1.0), writes=[b_rb])
            for bi in range(3):
                fw.op(fw.pe, lambda: nc.tensor.matmul(banks[0][0:6, :], rb[:], oh[:, bi * 512:(bi + 1) * 512],
                                                       start=True, stop=True), reads=[b_rb], writes=[bbank[0]])
                fw.op(fw.pe, lambda: nc.tensor.matmul(banks[1][0:6, :], on6[:], oh[:, bi * 512:(bi + 1) * 512],
                                                       start=True, stop=True), reads=[b_rb], writes=[bbank[1]])
                fw.op(fw.act, lambda: nc.scalar.activation(out=wv[:, bi * 512:(bi + 1) * 512], in_=banks[0][0:6, :],
                                                            func=AF.Exp), reads=[bbank[0]], writes=[b_wv])
                fw.op(fw.dve, lambda: nc.vector.tensor_tensor(out=wm[:, bi * 512:(bi + 1) * 512],
                                                               in0=banks[1][0:6, :], in1=wv[:, bi * 512:(bi + 1) * 512],
                                                               op=ALU.mult), reads=[bbank[1], b_wv], writes=[b_wv])
            fw.dma(fw.q_sp, wd_d[:, :], wm[:], reads=[b_wv])

        def mixin_pass(l, src):
            with phase():
                alloc_tb()
                wmi = sb("wmi", [128, 8, 3456], BF16)
                b_wmi = Buf("wmi")
                wst = [sb(f"wst{i}", [128, 1728], F32) for i in range(2)]
                b_wst = [Buf(), Buf()]
                for kc in range(8):
                    for hf in range(2):
                        wload(wmi[:, kc, hf * 1728:(hf + 1) * 1728],
                              w_mix_in[l, kc * 128:(kc + 1) * 128, hf * 1728:(hf + 1) * 1728], [w_[:] for w_ in wst],
                              b_wst, b_wmi)
                NB = T // 128
                qs = [sb(f"qs{i}", [128, 3, T], BF16) for i in range(2)]
                ks = [sb(f"ks{i}", [128, 3, T], BF16) for i in range(2)]
                vs = [sb(f"vs{i}", [128, NB, 384], BF16) for i in range(2)]
                gbs = [sb(f"gbs{i}", [128, 2, T], F32) for i in range(2)]
                gcs = [sb(f"gcs{i}", [128, 2, T], F32) for i in range(2)]
                ucs = [sb(f"ucs{i}", [128, 2, T], F32) for i in range(2)]
                zs = [sb(f"zs{i}", [128, 12, T], F32) for i in range(2)]
                bq = [Buf() for _ in range(2)]
                bk_ = [Buf() for _ in range(2)]
                bv = [Buf() for _ in range(2)]
                bgb = [Buf() for _ in range(2)]
                bgc = [Buf() for _ in range(2)]
                buc = [Buf() for _ in range(2)]
                bz = [Buf() for _ in range(2)]
                ev = 0
                load_x(src, 0, 0)
                modulate(l, 1, 0, 0)
                for ti in range(NTILE):
                    par = ti % 2
                    if ti + 1 < NTILE:
                        load_x(src, ti + 1, 1 - par)
                    tsl = slice(ti * T, (ti + 1) * T)
                    for ch in range(27):
                        if ch == 16 and ti + 1 < NTILE:
                            modulate(l, 1, ti + 1, 1 - par)
                        if 6 <= ch < 9:
                            continue
                        bk = banks[ch % 4]
                        bbk = bbank[ch % 4]
                        for kc in range(8):
                            fw.op(fw.pe, lambda: nc.tensor.matmul(
                                bk[:, 0:T], wmi[:, kc, ch * 128:(ch + 1) * 128], tb.ht[par][:, kc, :],
                                start=(kc == 0), stop=(kc == 7)), reads=[b_wmi, tb.b_ht[par]], writes=[bbk])
                        if ch < 3:
                            dst_, bd = qs[par][:, ch, :], bq[par]
                        elif ch < 6:
                            dst_, bd = ks[par][:, ch - 3, :], bk_[par]
                        elif ch < 11:
                            dst_, bd = gbs[par][:, ch - 9, :], bgb[par]
                        elif ch < 13:
                            dst_, bd = gcs[par][:, ch - 11, :], bgc[par]
                        elif ch < 15:
                            fw.op(fw.dve, lambda: nc.vector.tensor_tensor(
                                out=ucs[par][:, ch - 13, :], in0=bk[:, 0:T], in1=gcs[par][:, ch - 13, :], op=ALU.mult),
                                reads=[bbk, bgc[par]], writes=[buc[par]])
                            continue
                        else:
                            dst_, bd = zs[par][:, ch - 15, :], bz[par]
                        ev += 1
                        if ev % 2 == 0:
                            fw.op(fw.act, lambda: nc.scalar.activation(out=dst_, in_=bk[:, 0:T], func=AF.Copy),
                                  reads=[bbk], writes=[bd])
                        else:
                            fw.op(fw.dve, lambda: nc.vector.tensor_copy(out=dst_, in_=bk[:, 0:T]),
                                  reads=[bbk], writes=[bd])
                    for blk in range(NB):
                        bk = banks[4 + blk % 2]
                        bbk = bbank[4 + blk % 2]
                        for kc in range(8):
                            fw.op(fw.pe, lambda: nc.tensor.matmul(
                                bk[:, 0:384], tb.ht[par][:, kc, blk * 128:(blk + 1) * 128], wmi[:, kc, 768:1152],
                                start=(kc == 0), stop=(kc == 7)), reads=[b_wmi, tb.b_ht[par]], writes=[bbk])
                        fw.op(fw.act, lambda: nc.scalar.activation(out=vs[par][:, blk, :], in_=bk[:, 0:384],
                                                                    func=AF.Copy), reads=[bbk], writes=[bv[par]])
                    fw.dma(fw.q_sp, qT_d[:, :, tsl], qs[par][:], reads=[bq[par]])
                    fw.dma(fw.q_sp, kT_d[:, :, tsl], ks[par][:], reads=[bk_[par]])
                    fw.dma(fw.q_sp, V_d[PADV + ti * T:PADV + (ti + 1) * T, :].rearrange("(b p) f -> p b f", p=128),
                           vs[par][:], reads=[bv[par]])
                    fw.dma(fw.q_sp, gb_d[:, :, tsl], gbs[par][:], reads=[bgb[par]])
                    fw.dma(fw.q_sp, uc_d[:, :, tsl], ucs[par][:], reads=[buc[par]])
                    fw.dma(fw.q_sp, z_d[:, :, tsl], zs[par][:], reads=[bz[par]])

        def conv_pass(l):
            with phase():
                u = sb("cu", [128, NT + 2], F32)
                gbt = sb("cgb", [128, NT], F32)
                y = sb("cy", [128, NT], F32)
                yb = sb("cyb", [128, NT], BF16)
                tmp = sb("ctmp", [128, 2], F32)
                b_u, b_gb, b_y, b_yb, b_tmp = Buf(), Buf(), Buf(), Buf(), Buf()
                for c in range(2):
                    def cw(tap):
                        col = ppoff["conv_w"] + (l * 3 + tap) * 2 + c
                        return pp[:, col:col + 1]
                    fw.op(fw.dve, lambda: nc.vector.memset(u[:, 0:1], 0.0), writes=[b_u])
                    fw.op(fw.dve, lambda: nc.vector.memset(u[:, NT + 1:NT + 2], 0.0), writes=[b_u])
                    fw.dma(fw.q_sp, u[:, 1:NT + 1], uc_d[:, c, :], writes=[b_u])
                    fw.dma(fw.q_sp, gbt[:], gb_d[:, c, :], writes=[b_gb])
                    fw.op(fw.dve, lambda: nc.vector.tensor_scalar(out=y[:], in0=u[:, 1:NT + 1], scalar1=cw(1),
                                                                   scalar2=None, op0=ALU.mult),
                          reads=[b_u, b_pp], writes=[b_y])
                    fw.op(fw.dve, lambda: nc.vector.scalar_tensor_tensor(out=y[:], in0=u[:, 0:NT], scalar=cw(0),
                                                                          in1=y[:], op0=ALU.mult, op1=ALU.add),
                          reads=[b_u, b_pp], writes=[b_y])
                    fw.op(fw.dve, lambda: nc.vector.scalar_tensor_tensor(out=y[:], in0=u[:, 2:NT + 2], scalar=cw(2),
                                                                          in1=y[:], op0=ALU.mult, op1=ALU.add),
                          reads=[b_u, b_pp], writes=[b_y])
                    fw.op(fw.dve, lambda: nc.vector.tensor_scalar(out=tmp[:, 0:1], in0=u[:, SEG + 1:SEG + 2],
                                                                   scalar1=cw(2), scalar2=lm1[:, 0:1],
                                                                   op0=ALU.mult, op1=ALU.mult),
                          reads=[b_u, b_pp, b_lam], writes=[b_tmp])
                    fw.op(fw.dve, lambda: nc.vector.tensor_scalar(out=tmp[:, 1:2], in0=u[:, SEG:SEG + 1],
                                                                   scalar1=cw(0), scalar2=lm1[:, 0:1],
                                                                   op0=ALU.mult, op1=ALU.mult),
                          reads=[b_u, b_pp, b_lam], writes=[b_tmp])
                    fw.op(fw.dve, lambda: nc.vector.tensor_tensor(out=y[:, SEG - 1:SEG + 1], in0=y[:, SEG - 1:SEG + 1],
                                                                   in1=tmp[:, 0:2], op=ALU.add),
                          reads=[b_tmp], writes=[b_y])
                    fw.op(fw.dve, lambda: nc.vector.tensor_tensor(out=yb[:], in0=y[:], in1=gbt[:], op=ALU.mult),
                          reads=[b_y, b_gb], writes=[b_yb])
                    fw.dma(fw.q_sp, mix_d[:, 3 + c, :], yb[:], reads=[b_yb])

        def attn_pass(l):
            with phase():
                qc = sb("aq", [128, NT], BF16)
                kc_ = sb("ak", [128, NT + 2 * PADK], BF16)
                NVT = NT // 128 + 16
                Vt = sb("aV", [128, NVT, 128], BF16)
                acc = sb("aacc", [128, 2, NT], F32)
                Eb = sb("aEb", [128, 2, 256], BF16)
                Gh = [sb(f"aG{i}", [128, 128], F32) for i in range(2)]
                NVAR = 5
                Ev = [sb(f"aEv{i}", [128, 2, 256], BF16) for i in range(NVAR)]
                ND = 3
                sexp = [sb(f"asx{i}", [128, 2, 256], BF16) for i in range(ND)]
                sT = [sb(f"asT{i}", [128, 2, 256], BF16) for i in range(ND)]
                nd_regions = [banks[4][:, 0:256], banks[5][:, 0:256]]
                b_nd = [bbank[4], bbank[5]]
                eb_bank = banks[6][:, 0:128]
                b_eb = bbank[6]
                b_q, b_k, b_V, b_acc, b_Eb = Buf(), Buf(), Buf(), Buf(), Buf()
                b_G = [Buf(), Buf()]
                b_Ev = [Buf() for _ in range(NVAR)]
                b_sx = [Buf() for _ in range(ND)]
                b_sT = [Buf() for _ in range(ND)]
                fw.op(fw.pool, lambda: nc.gpsimd.memset(kc_[:, 0:PADK], 0.0), writes=[b_k])
                fw.op(fw.pool, lambda: nc.gpsimd.memset(kc_[:, PADK + NT:PADK + NT + PADK], 0.0), writes=[b_k])
                wcount = 0
                for pair in range(3):
                    fw.dma(fw.q_sp, qc[:], qT_d[:, pair, :], writes=[b_q])
                    fw.dma(fw.q_sp, kc_[:, PADK:PADK + NT], kT_d[:, pair, :], writes=[b_k])
                    for bi, d in enumerate(DILS):
                        nw = NT // (128 * d)
                        assert nw % 2 == 0
                        gi = 0
                        for hh in range(2):
                            h = pair * 2 + hh
                            for ab in range(2):
                                base = 65 if ab == 0 else 193
                                g = Gh[gi % 2]
                                bg = b_G[gi % 2]
                                gi += 1
                                src_ap = bass.AP(wd_t, h * 1536 + bi * 512 + base, [[1, 128], [1, 128]])
                                fw.dma(fw.q_sp, g[:], src_ap, writes=[bg])
                                fw.op(fw.pe, lambda: nc.tensor.matmul(eb_bank, g[:], jx[:], start=True,
                                                                       stop=True),
                                      reads=[bg, b_jx], writes=[b_eb])
                                fw.op(fw.act, lambda: nc.scalar.activation(
                                    out=Eb[:, hh, ab * 128:(ab + 1) * 128], in_=eb_bank, func=AF.Copy),
                                    reads=[b_eb], writes=[b_Eb])
                        def cat(c):
                            sa = "0" if c == 0 else ("l" if c == nw // 2 else "1")
                            sb_ = "0" if c == nw - 1 else ("l" if c == nw // 2 - 1 else "1")
                            return sa, sb_
                        var_idx = {}
                        for c in range(nw):
                            kk_ = cat(c)
                            if kk_ in var_idx:
                                continue
                            vi = len(var_idx)
                            assert vi < NVAR
                            var_idx[kk_] = vi
                            fw.op(fw.dve, lambda: nc.vector.tensor_copy(out=Ev[vi][:], in_=Eb[:]), reads=[b_Eb],
                                  writes=[b_Ev[vi]])
                            for which, (rows, cols) in enumerate(((slice(0, 64), slice(0, 128)),
                                                                  (slice(64, 128), slice(128, 256)))):
                                mode = kk_[which]
                                if mode == "1":
                                    continue
                                blkap = Ev[vi][rows, :, cols]
                                if mode == "0":
                                    fw.op(fw.dve, lambda: nc.vector.memset(blkap, 0.0), writes=[b_Ev[vi]])
                                else:
                                    fw.op(fw.dve, lambda: nc.vector.tensor_scalar(
                                        out=blkap, in0=blkap, scalar1=lam[rows, 0:1], scalar2=None, op0=ALU.mult),
                                        reads=[b_lam], writes=[b_Ev[vi]])
                        for r in range(d):
                            off = (PADV + r - 64 * d) * 384 + pair * 128
                            src_ap = bass.AP(V_t, off, [[d * 384, 128], [d * 128 * 384, nw + 1], [1, 128]])
                            fw.dma(fw.q_sp, Vt[:, r * (nw + 1):(r + 1) * (nw + 1), :], src_ap, writes=[b_V])
                        def window(r, c, wpar, w2):
                            sb2 = pall[:, w2 * 1024:(w2 + 1) * 1024]
                            bsb = bbank[w2 * 2]
                            q0 = r + d * 128 * c
                            qsl = slice(q0, q0 + 127 * d + 1, d)
                            for hh in range(2):
                                p0 = 64 * hh
                                for ab in range(2):
                                    k0 = PADK + r + d * (128 * c - 64 + 128 * ab)
                                    ksl = slice(k0, k0 + 127 * d + 1, d)
                                    fw.op(fw.pe, lambda: nc.tensor.matmul(
                                        sb2[:, hh * 512 + ab * 128: hh * 512 + (ab + 1) * 128],
                                        kc_[p0:p0 + 64, ksl], qc[p0:p0 + 64, qsl], start=True, stop=True),
                                        reads=[b_k, b_q], writes=[bsb])
                            sview = sb2.rearrange("p (h x) -> p h x", h=2)[:, :, 0:256]
                            fw.op(fw.act, lambda: nc.scalar.activation(out=sexp[wpar][:], in_=sview, func=AF.Exp,
                                                                        scale=0.125),
                                  reads=[bsb], writes=[b_sx[wpar]])
                            vi = var_idx[cat(c)]
                            fw.op(fw.dve, lambda: nc.vector.tensor_tensor(out=sT[wpar][:], in0=sexp[wpar][:],
                                                                           in1=Ev[vi][:], op=ALU.mult),
                                  reads=[b_sx[wpar], b_Ev[vi]], writes=[b_sT[wpar]])
                            yield
                            nd = nd_regions[w2]
                            bnd = b_nd[w2]
                            tA = r * (nw + 1) + c
                            for hh in range(2):
                                p0 = 64 * hh
                                for ab in range(2):
                                    fw.op(fw.pe, lambda: nc.tensor.matmul(
                                        nd[p0:p0 + 64, 0:128], Vt[:, tA + ab, p0:p0 + 64],
                                        sT[wpar][:, hh, ab * 128:(ab + 1) * 128], start=(ab == 0), stop=(ab == 1)),
                                        reads=[b_V, b_sT[wpar]], writes=[bnd])
                                for ab in range(2):
                                    fw.op(fw.pe, lambda: nc.tensor.matmul(
                                        nd[p0:p0 + 64, 128:256], ones_bf[:, 0:64],
                                        sT[wpar][:, hh, ab * 128:(ab + 1) * 128], start=(ab == 0), stop=(ab == 1)),
                                        reads=[b_ones, b_sT[wpar]], writes=[bnd])
                            ndv = nd.rearrange("p (a x) -> p a x", a=2)
                            accv = acc[:, :, qsl]
                            if bi == 0:
                                fw.op(fw.dve, lambda: nc.vector.tensor_copy(out=accv, in_=ndv), reads=[bnd],
                                      writes=[b_acc])
                            else:
                                fw.op(fw.dve, lambda: nc.vector.tensor_tensor(out=accv, in0=ndv, in1=accv,
                                                                               op=ALU.add),
                                      reads=[bnd], writes=[b_acc])
                        pending = None
                        for r in range(d):
                            for c in range(nw):
                                g_ = window(r, c, wcount % ND, wcount % 2)
                                wcount += 1
                                next(g_)
                                if pending is not None:
                                    for _ in pending:
                                        pass
                                pending = g_
                        for _ in pending:
                            pass
                    fw.op(fw.dve, lambda: nc.vector.reciprocal(out=acc[:, 1, :], in_=acc[:, 1, :]), writes=[b_acc])
                    fw.op(fw.dve, lambda: nc.vector.tensor_tensor(out=qc[:], in0=acc[:, 0, :], in1=acc[:, 1, :],
                                                                   op=ALU.mult), reads=[b_acc], writes=[b_q])
                    fw.dma(fw.q_sp, mix_d[:, pair, :], qc[:], reads=[b_q])

        def mixout_pass(l, src, dst):
            with phase():
                alloc_tb()
                wmo = sb("wmo", [128, 8, D], BF16)
                b_wmo = Buf()
                wst = [sb(f"wsto{i}", [128, D], F32) for i in range(2)]
                b_wst = [Buf(), Buf()]
                for kc in range(8):
                    wload(wmo[:, kc, :], w_mix_out[l, kc * 128:(kc + 1) * 128, :], [w_[:] for w_ in wst], b_wst, b_wmo)
                mt = [sb(f"mt{i}", [128, 8, T], BF16) for i in range(2)]
                b_mt = [Buf(), Buf()]
                vt2 = [tb.vt, sb("vt2", [128, 8, T], F32)]
                b_vt2 = [tb.b_vt, Buf()]
                vb2 = [tb.vb, sb("vb2", [128, 8, T], BF16)]
                sq2 = [tb.sqb, sb("sq2", [128, 8, T], BF16)]
                b_vb2 = [tb.b_vb, Buf()]

                def M0(ti):
                    par = ti % 2
                    load_x(src, ti, par)
                    fw.dma(fw.q_sp, mt[par][:], mix_d[:, :, ti * T:(ti + 1) * T], writes=[b_mt[par]])

                def M1(ti):
                    par = ti % 2
                    s_ = (ti * T) // SEG
                    for dc in range(8):
                        bk = banks[dc % 4]
                        bbk = bbank[dc % 4]
                        for mc in range(8):
                            fw.op(fw.pe, lambda: nc.tensor.matmul(
                                bk[:, 0:T], wmo[:, mc, dc * 128:(dc + 1) * 128], mt[par][:, mc, :],
                                start=(mc == 0), stop=(mc == 7)), reads=[b_wmo, b_mt[par]], writes=[bbk])
                        fw.op(fw.dve, lambda: nc.vector.scalar_tensor_tensor(
                            out=vt2[par][:, dc, :], in0=bk[:, 0:T], scalar=cg[:, l, 1, dc, s_:s_ + 1],
                            in1=tb.xt[par][:, dc, :], op0=ALU.mult, op1=ALU.add),
                            reads=[bbk, b_mod, tb.b_xt[par]], writes=[b_vt2[par]])
                        yield

                def M2(ti):
                    par = ti % 2
                    return post_norm_gen(l, 1, ti, dst, vb2[par][:], sq2[par][:], b_vb2[par], vt2[par], b_vt2[par], fast=True)

                M0(0)
                if NTILE > 1:
                    M0(1)
                drain([M1(0)])
                for ti in range(NTILE):
                    gens = [M2(ti)]
                    if ti + 1 < NTILE:
                        gens.append(M1(ti + 1))
                    drain(gens)
                    if ti + 2 < NTILE:
                        M0(ti + 2)

        NCH = NT // 128
        KINDS = ("A", "B", "K", "R", "Bh", "Kh")

        def rwkv_prep(l):
            with phase():
                TR = 512
                NCT = TR // 128
                wup = sb("wup", [128, 384], BF16)
                aup = sb("aup", [128, 384], BF16)
                gup = sb("gup", [128, 384], BF16)
                b_w = Buf()
                for dr in range(2):
                    fw.dma(fw.q_pool, wup[64 * dr:64 * dr + 64, :], rw_w_up[l, dr, :, :], writes=[b_w])
                    fw.dma(fw.q_pool, aup[64 * dr:64 * dr + 64, :], rw_a_up[l, dr, :, :], writes=[b_w])
                fw.dma(fw.q_pool, gup[:], rw_g_up[l, :, :], writes=[b_w])
                omka = sb("omka", [128, 3], F32)
                kacol = ppoff["k_a"] + l * 3
                fw.op(fw.dve, lambda: nc.vector.tensor_scalar(out=omka[:], in0=pp[:, kacol:kacol + 3], scalar1=-1.0,
                                                               scalar2=1.0, op0=ALU.mult, op1=ALU.add),
                      reads=[b_pp], writes=[b_w])
                zt = sb("zt", [128, 12, TR + 2], F32)
                t12 = sb("t12", [128, 12, TR], F32)
                zs = sb("zs", [128, 12, TR], F32)
                b_zt, b_t12, b_zs = Buf(), Buf(), Buf()
                twd = sb("twd", [128, TR], BF16)
                adb = sb("adb", [128, TR], BF16)
                sgd = sb("sgd", [128, TR], BF16)
                b_tw = Buf()
                vst = sb("vst", [128, 3, TR], BF16)
                gst = sb("gst", [128, 3, TR], F32)
                bon = sb("bon", [128, 3, TR], F32)
                b_vst, b_gst, b_bon = Buf(), Buf(), Buf()
                stg = {(dr, k): sb(f"stg{dr}{k}", [128, 3, TR], BF16) for dr in range(2) for k in KINDS}
                b_stg = {key: Buf() for key in stg}
                pst = [sb(f"pst{dr}", [128, 3, NCT], F32) for dr in range(2)]
                b_pst = [Buf(), Buf()]
                names = ["kks", "sqk", "nrm", "kk", "rkr", "sgw", "lw", "a", "kd", "bb", "cs", "tmp", "P", "iP", "Pex"]
                tt = {n_: sb("r1" + n_, [128, TR], BF16 if n_ in ("sqk", "rkr") else F32) for n_ in names}
                bt = {n_: Buf() for n_ in names}
                drn = ("sgw", "lw", "a", "kd", "bb", "cs", "tmp", "P", "iP", "Pex")
                ttd, btd = [], []
                for dr_ in range(2):
                    t_ = dict(tt)
                    b_ = dict(bt)
                    if dr_ == 1:
                        for n_ in drn:
                            t_[n_] = sb("r1b" + n_, [128, TR], F32)
                            b_[n_] = Buf()
                    ttd.append(t_)
                    btd.append(b_)
                for ti in range(NT // TR):
                    t0 = ti * TR
                    lo = t0 - 1 if t0 > 0 else t0
                    hi = t0 + TR + 1 if t0 + TR < NT else t0 + TR
                    if t0 == 0:
                        fw.op(fw.dve, lambda: nc.vector.memset(zt[:, :, 0:1], 0.0), writes=[b_zt])
                    if t0 + TR == NT:
                        fw.op(fw.dve, lambda: nc.vector.memset(zt[:, :, TR + 1:TR + 2], 0.0), writes=[b_zt])
                    fw.dma(fw.q_sp, zt[:, :, 1 - (t0 - lo):1 + TR + (hi - t0 - TR)], z_d[:, :, lo:hi], writes=[b_zt])
                    if t0 == SEG:
                        fw.op(fw.dve, lambda: nc.vector.tensor_scalar(out=zt[:, :, 0:1], in0=zt[:, :, 0:1],
                                                                       scalar1=lam[:, 0:1], scalar2=None, op0=ALU.mult),
                              reads=[b_lam], writes=[b_zt])
                    if t0 + TR == SEG:
                        fw.op(fw.dve, lambda: nc.vector.tensor_scalar(out=zt[:, :, TR + 1:TR + 2],
                                                                       in0=zt[:, :, TR + 1:TR + 2],
                                                                       scalar1=lam[:, 0:1], scalar2=None, op0=ALU.mult),
                              reads=[b_lam], writes=[b_zt])
                    zc = zt[:, :, 1:TR + 1]
                    fw.op(fw.dve, lambda: nc.vector.tensor_tensor(out=t12[:], in0=zt[:, :, 0:TR], in1=zt[:, :, 2:TR + 2],
                                                                   op=ALU.add), reads=[b_zt], writes=[b_t12])
                    fw.op(fw.dve, lambda: nc.vector.scalar_tensor_tensor(out=t12[:], in0=t12[:], scalar=0.5, in1=zc,
                                                                          op0=ALU.mult, op1=ALU.subtract),
                          reads=[b_zt], writes=[b_t12])
                    for ch in range(12):
                        mcol = ppoff["mu"] + l * 12 + ch
                        fw.op(fw.dve, lambda: nc.vector.scalar_tensor_tensor(
                            out=zs[:, ch, :], in0=t12[:, ch, :], scalar=pp[:, mcol:mcol + 1], in1=zt[:, ch, 1:TR + 1],
                            op0=ALU.mult, op1=ALU.add), reads=[b_t12, b_zt, b_pp], writes=[b_zs])
                    fw.op(fw.act, lambda: nc.scalar.activation(out=twd[:], in_=zs[:, 9, :], func=AF.Tanh),
                          reads=[b_zs], writes=[b_tw])
                    fw.op(fw.act, lambda: nc.scalar.activation(out=sgd[:], in_=zs[:, 11, :], func=AF.Sigmoid),
                          reads=[b_zs], writes=[b_tw])
                    fw.op(fw.act, lambda: nc.scalar.activation(out=adb[:], in_=zs[:, 10, :], func=AF.Copy),
                          reads=[b_zs], writes=[b_tw])
                    fw.op(fw.pool, lambda: nc.gpsimd.tensor_copy(out=vst[:], in_=zs[:, 6:9, :]), reads=[b_zs],
                          writes=[b_vst])
                    for fc in range(3):
                        fsl = slice(fc * 128, (fc + 1) * 128)
                        r_ = zs[:, fc, :]
                        k_ = zs[:, 3 + fc, :]
                        v_ = zs[:, 6 + fc, :]

                        def col(nm, extra=0):
                            c_ = ppoff[nm] + l * (6 if nm in ("w0", "a0") else 3) + extra + fc
                            return pp[:, c_:c_ + 1]
                        V_ = lambda fn, r, w: fw.op(fw.dve, fn, reads=r, writes=w)
                        A_ = lambda fn, r, w: fw.op(fw.act, fn, reads=r, writes=w)
                        V_(lambda: nc.vector.tensor_scalar(out=tt["kks"][:], in0=k_, scalar1=col("k_k"), scalar2=None,
                                                           op0=ALU.mult), [b_zs, b_pp], [bt["kks"]])
                        A_(lambda: nc.scalar.activation(out=tt["sqk"][:], in_=tt["kks"][:], func=AF.Square),
                           [bt["kks"]], [bt["sqk"]])
                        fw.op(fw.pe, lambda: nc.tensor.matmul(banks[0][:, 0:TR], blk1[:], tt["sqk"][:], start=True,
                                                               stop=True), reads=[b_cst, bt["sqk"]], writes=[bbank[0]])
                        A_(lambda: nc.scalar.activation(out=tt["nrm"][:], in_=banks[0][:, 0:TR], func=AF.Sqrt),
                           [bbank[0]], [bt["nrm"]])
                        V_(lambda: nc.vector.tensor_scalar(out=tt["nrm"][:], in0=tt["nrm"][:], scalar1=1e-12,
                                                           scalar2=None, op0=ALU.max), [], [bt["nrm"]])
                        V_(lambda: nc.vector.reciprocal(out=tt["nrm"][:], in_=tt["nrm"][:]), [], [bt["nrm"]])
                        V_(lambda: nc.vector.tensor_tensor(out=tt["kk"][:], in0=tt["kks"][:], in1=tt["nrm"][:],
                                                           op=ALU.mult), [bt["kks"], bt["nrm"]], [bt["kk"]])
                        V_(lambda: nc.vector.scalar_tensor_tensor(out=tt["rkr"][:], in0=r_, scalar=col("r_k"), in1=k_,
                                                                  op0=ALU.mult, op1=ALU.mult), [b_zs, b_pp], [bt["rkr"]])
                        fw.op(fw.pe, lambda: nc.tensor.matmul(banks[1][:, 0:TR], blk1[:], tt["rkr"][:], start=True,
                                                               stop=True), reads=[b_cst, bt["rkr"]], writes=[bbank[1]])
                        V_(lambda: nc.vector.tensor_tensor(out=bon[:, fc, :], in0=banks[1][:, 0:TR], in1=v_,
                                                           op=ALU.mult), [bbank[1], b_zs], [b_bon])
                        fw.op(fw.pe, lambda: nc.tensor.matmul(banks[2][:, 0:TR], gup[:, fsl], sgd[:], start=True,
                                                               stop=True), reads=[b_w, b_tw], writes=[bbank[2]])
                        A_(lambda: nc.scalar.activation(out=gst[:, fc, :], in_=banks[2][:, 0:TR], func=AF.Copy),
                           [bbank[2]], [b_gst])
                        def dr_stream(dr):
                            tt = ttd[dr]
                            bt = btd[dr]
                            rows = slice(64 * dr, 64 * dr + 64)
                            bw_ = banks[3 + dr]
                            ba_ = banks[5 + dr]
                            fw.op(fw.pe, lambda: nc.tensor.matmul(bw_[:, 0:TR], wup[rows, fsl], twd[rows, :],
                                                                   start=True, stop=True),
                                  reads=[b_w, b_tw], writes=[bbank[3 + dr]])
                            yield
                            fw.op(fw.pe, lambda: nc.tensor.matmul(ba_[:, 0:TR], aup[rows, fsl], adb[rows, :],
                                                                   start=True, stop=True),
                                  reads=[b_w, b_tw], writes=[bbank[5 + dr]])
                            yield
                            A_(lambda: nc.scalar.activation(out=tt["sgw"][:], in_=bw_[:, 0:TR], func=AF.Sigmoid,
                                                            bias=col("w0", 3 * dr)), [bbank[3 + dr], b_pp], [bt["sgw"]])
                            yield
                            A_(lambda: nc.scalar.activation(out=tt["a"][:], in_=ba_[:, 0:TR], func=AF.Sigmoid,
                                                            bias=col("a0", 3 * dr)), [bbank[5 + dr], b_pp], [bt["a"]])
                            yield
                            V_(lambda: nc.vector.tensor_scalar(out=tt["lw"][:], in0=tt["sgw"][:],
                                                               scalar1=-float(np.exp(-0.5)), scalar2=None, op0=ALU.mult),
                               [bt["sgw"]], [bt["lw"]])
                            yield
                            V_(lambda: nc.vector.tensor_scalar(out=tt["kd"][:], in0=tt["a"][:], scalar1=col("k_a"),
                                                               scalar2=omka[:, fc:fc + 1], op0=ALU.mult, op1=ALU.add),
                               [bt["a"], b_pp, b_w], [bt["kd"]])
                            yield
                            V_(lambda: nc.vector.tensor_tensor(out=tt["kd"][:], in0=tt["kd"][:], in1=k_, op=ALU.mult),
                               [b_zs], [bt["kd"]])
                            yield
                            V_(lambda: nc.vector.tensor_tensor(out=tt["bb"][:], in0=tt["kk"][:], in1=tt["a"][:],
                                                               op=ALU.mult), [bt["kk"], bt["a"]], [bt["bb"]])
                            yield
                            V_(lambda: nc.vector.tensor_tensor_scan(out=tt["cs"][:], data0=rstm[:, 0:TR],
                                                                    data1=tt["lw"][:], initial=0.0, op0=ALU.mult,
                                                                    op1=ALU.add), [b_cst, bt["lw"]], [bt["cs"]])
                            yield
                            cs3 = tt["cs"][:].rearrange("p (c x) -> p c x", x=128)
                            lw3 = tt["lw"][:].rearrange("p (c x) -> p c x", x=128)
                            if dr == 1:
                                V_(lambda: nc.vector.tensor_tensor(
                                    out=cs3, in0=cs3[:, :, 127:128].to_broadcast([128, NCT, 128]), in1=cs3,
                                    op=ALU.subtract), [], [bt["cs"]])
                                V_(lambda: nc.vector.tensor_tensor(out=tt["cs"][:], in0=tt["cs"][:], in1=tt["lw"][:],
                                                                   op=ALU.add), [bt["lw"]], [bt["cs"]])
                            A_(lambda: nc.scalar.activation(out=tt["P"][:], in_=tt["cs"][:], func=AF.Exp),
                               [bt["cs"]], [bt["P"]])
                            yield
                            A_(lambda: nc.scalar.activation(out=tt["iP"][:], in_=tt["cs"][:], func=AF.Exp, scale=-1.0),
                               [bt["cs"]], [bt["iP"]])
                            yield
                            V_(lambda: nc.vector.tensor_tensor(out=tt["tmp"][:], in0=tt["cs"][:], in1=tt["lw"][:],
                                                               op=ALU.subtract), [bt["cs"], bt["lw"]], [bt["tmp"]])
                            yield
                            A_(lambda: nc.scalar.activation(out=tt["Pex"][:], in_=tt["tmp"][:], func=AF.Exp),
                               [bt["tmp"]], [bt["Pex"]])
                            yield
                            P3 = tt["P"][:].rearrange("p (c x) -> p c x", x=128)
                            pe_idx = 127 if dr == 0 else 0
                            pend_v = P3[:, :, pe_idx:pe_idx + 1]
                            V_(lambda: nc.vector.tensor_copy(out=pst[dr][:, fc, :].unsqueeze(2), in_=pend_v),
                               [bt["P"]], [b_pst[dr]])
                            yield
                            S = lambda k: stg[(dr, k)][:, fc, :]
                            V_(lambda: nc.vector.scalar_tensor_tensor(out=S("A"), in0=tt["kk"][:], scalar=-1.0,
                                                                      in1=tt["Pex"][:], op0=ALU.mult, op1=ALU.mult),
                               [bt["kk"], bt["Pex"]], [b_stg[(dr, "A")]])
                            yield
                            V_(lambda: nc.vector.tensor_tensor(out=tt["bb"][:], in0=tt["bb"][:], in1=tt["iP"][:],
                                                               op=ALU.mult), [bt["iP"]], [bt["bb"]])
                            yield
                            V_(lambda: nc.vector.tensor_tensor(out=tt["kd"][:], in0=tt["kd"][:], in1=tt["iP"][:],
                                                               op=ALU.mult), [bt["iP"]], [bt["kd"]])
                            yield
                            A_(lambda: nc.scalar.activation(out=S("B"), in_=tt["bb"][:], func=AF.Copy),
                               [bt["bb"]], [b_stg[(dr, "B")]])
                            yield
                            A_(lambda: nc.scalar.activation(out=S("K"), in_=tt["kd"][:], func=AF.Copy),
                               [bt["kd"]], [b_stg[(dr, "K")]])
                            yield
                            V_(lambda: nc.vector.tensor_tensor(out=S("R"), in0=r_, in1=tt["P"][:], op=ALU.mult),
                               [b_zs, bt["P"]], [b_stg[(dr, "R")]])
                            yield
                            pbc = pend_v.to_broadcast([128, NCT, 128])
                            V_(lambda: nc.vector.tensor_tensor(out=S("Bh").rearrange("p (c x) -> p c x", x=128),
                                                               in0=tt["bb"][:].rearrange("p (c x) -> p c x", x=128),
                                                               in1=pbc, op=ALU.mult),
                               [bt["bb"], bt["P"]], [b_stg[(dr, "Bh")]])
                            yield
                            V_(lambda: nc.vector.tensor_tensor(out=S("Kh").rearrange("p (c x) -> p c x", x=128),
                                                               in0=tt["kd"][:].rearrange("p (c x) -> p c x", x=128),
                                                               in1=pbc, op=ALU.mult),
                               [bt["kd"], bt["P"]], [b_stg[(dr, "Kh")]])
                            yield
                        drain([dr_stream(0), dr_stream(1)])
                    tsl = slice(t0, t0 + TR)
                    fw.dma(fw.q_sp, rv_d[:, :, tsl], vst[:], reads=[b_vst])
                    fw.dma(fw.q_sp, rg_d[:, :, tsl], gst[:], reads=[b_gst])
                    fw.dma(fw.q_sp, rbon_d[:, :, tsl], bon[:], reads=[b_bon])
                    for dr in range(2):
                        fw.dma(fw.q_sp, rpend_d[dr][:, :, ti * NCT:(ti + 1) * NCT], pst[dr][:], reads=[b_pst[dr]])
                        for k in KINDS:
                            fw.dma(fw.q_sp, rk_d[(dr, k)][:, :, tsl], stg[(dr, k)][:], reads=[b_stg[(dr, k)]])

        def rwkv_chunks(l):
            with phase():
                HT = NT // 2
                HCH = NCH // 2
                big = sb("rbig", [128, 6 * NT], BF16)
                vv = sb("rvv", [128, NT], BF16)
                b_vv = Buf()
                ysq = big[:, 0:2 * NT].bitcast(F32).rearrange("p (g x) -> p g x", x=64)
                gsb = big[:, 2 * NT:4 * NT].bitcast(F32)
                bsb_ = big[:, 4 * NT:6 * NT].bitcast(F32)
                rwo = vv
                st1 = sb("rst1", [128, NCH * 2], F32)
                st2 = sb("rst2", [128, NCH * 2], F32)
                st3 = sb("rst3", [128, NCH * 2], F32)
                gnrow = sb("rgn", [128, 2, 128], F32)
                b_st12 = [Buf()]
                b_gn = [Buf()]
                Yacc = sb("rY", [128, NCH, 128], F32)
                b_Yc = [Buf() for _ in range(NCH)]
                pall_bf = pall.bitcast(BF16)

                class Lane:
                    pass
                lanes = []
                for li in range(2):
                    L = Lane()
                    L.li = li
                    L.dr = li
                    L.arr = {k: big[:, i_ * NT + li * HT:i_ * NT + (li + 1) * HT] for i_, k in enumerate(KINDS)}
                    L.b_arr = {k: Buf() for k in KINDS}
                    L.pend = sb(f"rpend{li}", [128, NCH], F32)
                    L.b_pend = Buf()
                    L.S32 = sb(f"rS32{li}", [128, 64], F32)
                    L.Sbf = sb(f"rSbf{li}", [128, 64], BF16)
                    L.b_S = Buf()

                    def dbl(name, shape, dt, li=li):
                        return [sb(f"{name}{li}{i}", shape, dt) for i in range(2)], [Buf() for _ in range(2)]
                    L.Mn, L.b_Mn = dbl("rMn", [128, 2, 256], BF16)
                    L.Mk, L.b_Mk = dbl("rMk", [128, 2, 256], BF16)
                    L.NTt, L.b_NTt = dbl("rNT", [128, 2, 128], BF16)
                    L.Tm, L.b_Tm = dbl("rT", [128, 2, 128], BF16)
                    L.TT, L.b_TT = dbl("rTT", [128, 3, 128], BF16)
                    L.XX, L.b_XX = dbl("rXX", [128, 2, 256], BF16)
                    L.Tp, L.b_Tp = dbl("rTp", [128, 2, 128], BF16)
                    L.XTt = sb(f"rXTt{li}", [128, 2, 64], BF16)
                    L.UT = sb(f"rUT{li}", [128, 2, 64], BF16)
                    L.b_XTt, L.b_UT = Buf(), Buf()
                    L.q = [4 * li + i_ for i_ in range(4)]
                    L.pXY = pall[:, L.q[0] * 512:(L.q[0] + 2) * 512].rearrange("p (h x) -> p h x", h=2)
                    L.pZW = pall[:, L.q[2] * 512:(L.q[2] + 2) * 512].rearrange("p (h x) -> p h x", h=2)
                    L.bXY = [bbank[L.q[0]], bbank[L.q[1]]]
                    L.bZW = [bbank[L.q[2]], bbank[L.q[3]]]
                    L.msk = tri[:, 0:2, :] if li == 0 else tri[:, 2:4, :]
                    L.mskT = tri[:, 2, :] if li == 0 else tri[:, 0, :]
                    lanes.append(L)
                b_ysq = [lanes[0].b_arr["A"], lanes[0].b_arr["B"], lanes[1].b_arr["A"], lanes[1].b_arr["B"]]
                b_gsb = [L.b_arr[k] for L in lanes for k in ("K", "R", "Bh", "Kh")]
                b_rwo = [b_vv]

                def stageA(L, n, loc, par):
                    csl = slice(loc * 128, (loc + 1) * 128)
                    vsl = slice(n * 128, (n + 1) * 128)
                    arr, b_arr = L.arr, L.b_arr
                    Mn, Mk, NTt, Tm, Tp = L.Mn, L.Mk, L.NTt, L.Tm, L.Tp
                    for hh in range(2):
                        rows = slice(64 * hh, 64 * hh + 64)
                        bk = banks[L.q[hh]]
                        for (c0, lhs) in ((0, "B"), (256, "K")):
                            fw.op(fw.pe, lambda: nc.tensor.matmul(bk[:, c0:c0 + 128], arr[lhs][rows, csl],
                                                                   arr["A"][rows, csl], start=True, stop=True),
                                  reads=[b_arr[lhs], b_arr["A"]], writes=[bbank[L.q[hh]]])
                            fw.op(fw.pe, lambda: nc.tensor.matmul(bk[:, c0 + 128:c0 + 256], arr[lhs][rows, csl],
                                                                   arr["R"][rows, csl], start=True, stop=True),
                                  reads=[b_arr[lhs], b_arr["R"]], writes=[bbank[L.q[hh]]])
                        fw.op(fw.pe, lambda: nc.tensor.matmul(banks[L.q[2 + hh]][:, 0:128], arr["A"][rows, csl],
                                                               arr["B"][rows, csl], start=True, stop=True),
                              reads=[b_arr["A"], b_arr["B"]], writes=[bbank[L.q[2 + hh]]])
                    mb = L.msk.rearrange("p a x -> p (a x)").unsqueeze(1).to_broadcast([128, 2, 256])
                    fw.op(fw.dve, lambda: nc.vector.tensor_tensor(out=Mn[par][:], in0=L.pXY[:, :, 0:256], in1=mb,
                                                                   op=ALU.mult),
                          reads=L.bXY + [b_cst], writes=[L.b_Mn[par]])
                    fw.op(fw.dve, lambda: nc.vector.tensor_tensor(out=Mk[par][:], in0=L.pXY[:, :, 256:512], in1=mb,
                                                                   op=ALU.mult),
                          reads=L.bXY + [b_cst], writes=[L.b_Mk[par]])
                    mTb = L.mskT.unsqueeze(1).to_broadcast([128, 2, 128])
                    fw.op(fw.dve, lambda: nc.vector.tensor_tensor(out=NTt[par][:], in0=L.pZW[:, :, 0:128], in1=mTb,
                                                                   op=ALU.mult),
                          reads=L.bZW + [b_cst], writes=[L.b_NTt[par]])
                    yield
                    zb = L.q[2] * 1024
                    for (src_ap, b_src, col0) in ((vv[:, vsl], b_vv, 256), (arr["Bh"][:, csl], b_arr["Bh"], 512),
                                                  (arr["Kh"][:, csl], b_arr["Kh"], 768)):
                        pv = pall_bf[:, zb + col0: zb + col0 + 128]
                        fw.op(fw.pe, lambda: nc.tensor.transpose(pv, src_ap, identb[:]),
                              reads=[b_src, b_cst], writes=[bbank[L.q[2]]])
                    pv3 = pall_bf[:, zb + 256: zb + 1024].rearrange("p (a x) -> p a x", a=3)[:, :, 0:128]
                    fw.op(fw.act, lambda: nc.scalar.activation(out=L.TT[par][:], in_=pv3, func=AF.Copy),
                          reads=[bbank[L.q[2]]], writes=[L.b_TT[par]])
                    yield
                    identb2 = identb[:].unsqueeze(1).to_broadcast([128, 2, 128])
                    fw.op(fw.dve, lambda: nc.vector.tensor_tensor(out=Tp[0][:], in0=Mn[par][:, :, 0:128], in1=identb2,
                                                                   op=ALU.add),
                          reads=[L.b_Mn[par], b_cst], writes=[L.b_Tp[0]])
                    Xc, bXc = (lambda hh: Mn[par][:, hh, 0:128]), L.b_Mn[par]
                    XTc, bXTc = (lambda hh: NTt[par][:, hh, :]), L.b_NTt[par]
                    XX, b_XX = L.XX, L.b_XX
                    tcur = 0

                    def tprod(jprev, dstT_, bdst, tcur):
                        pq = jprev % 2
                        for hh in range(2):
                            fw.op(fw.pe, lambda: nc.tensor.matmul(banks[L.q[2 + hh]][:, 0:128], XX[pq][:, hh, 0:128],
                                                                   Tp[tcur][:, hh, :], start=True, stop=True),
                                  reads=[b_XX[pq], L.b_Tp[tcur]], writes=[bbank[L.q[2 + hh]]])

                    def tevac(dstT_, bdst, tcur):
                        fw.op(fw.dve, lambda: nc.vector.tensor_tensor(out=dstT_[:], in0=L.pZW[:, :, 0:128],
                                                                       in1=Tp[tcur][:], op=ALU.add),
                              reads=L.bZW + [L.b_Tp[tcur]], writes=[bdst])

                    for j in range(1, 7):
                        pp_ = j % 2
                        last = j == 6
                        for hh in range(2):
                            bk = banks[L.q[hh]]
                            fw.op(fw.pe, lambda: nc.tensor.matmul(bk[:, 0:128], Xc(hh), XTc(hh), start=True,
                                                                   stop=True), reads=[bXc, bXTc], writes=[bbank[L.q[hh]]])
                            if not last:
                                fw.op(fw.pe, lambda: nc.tensor.matmul(bk[:, 128:256], XTc(hh), Xc(hh),
                                                                       start=True, stop=True),
                                      reads=[bXc, bXTc], writes=[bbank[L.q[hh]]])
                        if j >= 2:
                            tprod(j - 1, Tp[1 - tcur], L.b_Tp[1 - tcur], tcur)
                        wdt = 128 if last else 256
                        fw.op(fw.act, lambda: nc.scalar.activation(out=XX[pp_][:, :, 0:wdt], in_=L.pXY[:, :, 0:wdt],
                                                                    func=AF.Copy),
                              reads=L.bXY, writes=[b_XX[pp_]])
                        if j >= 2:
                            tevac(Tp[1 - tcur], L.b_Tp[1 - tcur], tcur)
                            tcur = 1 - tcur
                        Xc, bXc = (lambda hh, pp_=pp_: XX[pp_][:, hh, 128:256]), b_XX[pp_]
                        XTc, bXTc = (lambda hh, pp_=pp_: XX[pp_][:, hh, 0:128]), b_XX[pp_]
                        yield
                    tprod(6, Tm[par], L.b_Tm[par], tcur)
                    tevac(Tm[par], L.b_Tm[par], tcur)
                    yield

                def stageB(L, n, loc, par):
                    csl = slice(loc * 128, (loc + 1) * 128)
                    arr, b_arr = L.arr, L.b_arr
                    Mn, Mk, Tm, TT = L.Mn, L.Mk, L.Tm, L.TT
                    bw = banks[L.q[3]]
                    bbw = bbank[L.q[3]]
                    for hh in range(2):
                        rows = slice(64 * hh, 64 * hh + 64)
                        o = bw[:, 128 + hh * 64:128 + (hh + 1) * 64]
                        fw.op(fw.pe, lambda: nc.tensor.matmul(o, arr["A"][rows, csl], L.Sbf[rows, :], start=True,
                                                               stop=False), reads=[b_arr["A"], L.b_S], writes=[bbw])
                        fw.op(fw.pe, lambda: nc.tensor.matmul(o, Mk[par][:, hh, 0:128], TT[par][:, 0, hh * 64:(hh + 1) * 64],
                                                               start=False, stop=True),
                              reads=[L.b_Mk[par], L.b_TT[par]], writes=[bbw])
                    fw.op(fw.dve, lambda: nc.vector.tensor_copy(out=L.XTt[:].rearrange("p h x -> p (h x)"),
                                                                 in_=bw[:, 128:256]), reads=[bbw], writes=[L.b_XTt])
                    yield
                    for hh in range(2):
                        fw.op(fw.pe, lambda: nc.tensor.matmul(bw[:, 128 + hh * 64:128 + (hh + 1) * 64], Tm[par][:, hh, :],
                                                               L.XTt[:, hh, :], start=True, stop=True),
                              reads=[L.b_Tm[par], L.b_XTt], writes=[bbw])
                    fw.op(fw.dve, lambda: nc.vector.tensor_copy(out=L.UT[:].rearrange("p h x -> p (h x)"),
                                                                 in_=bw[:, 128:256]), reads=[bbw], writes=[L.b_UT])
                    yield
                    for hh in range(2):
                        rows = slice(64 * hh, 64 * hh + 64)
                        o = bw[rows, 256:320]
                        fw.op(fw.pe, lambda: nc.tensor.matmul(o, TT[par][:, 1, hh * 64:(hh + 1) * 64], L.UT[:, hh, :],
                                                               start=True, stop=False),
                              reads=[L.b_TT[par], L.b_UT], writes=[bbw])
                        fw.op(fw.pe, lambda: nc.tensor.matmul(o, TT[par][:, 2, hh * 64:(hh + 1) * 64],
                                                               TT[par][:, 0, hh * 64:(hh + 1) * 64], start=False, stop=True),
                              reads=[L.b_TT[par], L.b_TT[par]], writes=[bbw])
                    for hh in range(2):
                        rows = slice(64 * hh, 64 * hh + 64)
                        o = bw[:, 320 + hh * 64:320 + (hh + 1) * 64]
                        fw.op(fw.pe, lambda: nc.tensor.matmul(o, arr["R"][rows, csl], L.Sbf[rows, :], start=True,
                                                               stop=False), reads=[b_arr["R"], L.b_S], writes=[bbw])
                        fw.op(fw.pe, lambda: nc.tensor.matmul(o, Mn[par][:, hh, 128:256], L.UT[:, hh, :], start=False,
                                                               stop=False), reads=[L.b_Mn[par], L.b_UT], writes=[bbw])
                        fw.op(fw.pe, lambda: nc.tensor.matmul(o, Mk[par][:, hh, 128:256],
                                                               TT[par][:, 0, hh * 64:(hh + 1) * 64], start=False, stop=True),
                              reads=[L.b_Mk[par], L.b_TT[par]], writes=[bbw])
                    fw.op(fw.dve, lambda: nc.vector.scalar_tensor_tensor(out=L.S32[:], in0=L.S32[:],
                                                                          scalar=L.pend[:, n:n + 1],
                                                                          in1=bw[:, 256:320], op0=ALU.mult, op1=ALU.add),
                          reads=[bbw, L.b_pend], writes=[L.b_S])
                    fw.op(fw.act, lambda: nc.scalar.activation(out=L.Sbf[:], in_=L.S32[:], func=AF.Copy), reads=[],
                          writes=[L.b_S])
                    fw.op(fw.dve, lambda: nc.vector.tensor_tensor(out=Yacc[:, n, :], in0=bw[:, 320:448],
                                                                   in1=Yacc[:, n, :], op=ALU.add),
                          reads=[bbw], writes=[b_Yc[n]])
                    yield

                for fc in range(3):
                    fw.dma(fw.q_sp, vv[:], rv_d[:, fc, :], writes=[b_vv])
                    fw.op(fw.pool, lambda: nc.gpsimd.memset(Yacc[:], 0.0), writes=b_Yc)
                    for L in lanes:
                        fw.dma(fw.q_sp, L.pend[:], rpend_d[L.dr][:, fc, :], writes=[L.b_pend])
                    for ph_ in range(2):
                        halves = [ph_, 1 - ph_]
                        orders = [list(range(ph_ * HCH, (ph_ + 1) * HCH)),
                                  list(range((2 - ph_) * HCH - 1, (1 - ph_) * HCH - 1, -1))]
                        for L in lanes:
                            hf = halves[L.li]
                            for k in KINDS:
                                fw.dma(fw.q_sp, L.arr[k], rk_d[(L.dr, k)][:, fc, hf * HT:(hf + 1) * HT],
                                       writes=[L.b_arr[k]])
                            if ph_ == 0:
                                fw.op(fw.dve, lambda: nc.vector.memset(L.S32[:], 0.0), writes=[L.b_S])
                                fw.op(fw.dve, lambda: nc.vector.memset(L.Sbf[:], 0.0), writes=[L.b_S])
                            else:
                                fw.op(fw.dve, lambda: nc.vector.tensor_scalar(out=L.S32[:], in0=L.S32[:],
                                                                               scalar1=lam[:, 0:1], scalar2=None,
                                                                               op0=ALU.mult),
                                      reads=[b_lam], writes=[L.b_S])
                                fw.op(fw.act, lambda: nc.scalar.activation(out=L.Sbf[:], in_=L.S32[:], func=AF.Copy),
                                      reads=[], writes=[L.b_S])
                        loc_of = lambda L, n: n - halves[L.li] * HCH
                        drain([stageA(L, orders[L.li][0], loc_of(L, orders[L.li][0]), 0) for L in lanes])
                        for idx in range(HCH):
                            par = idx % 2
                            gens = []
                            for L in lanes:
                                n = orders[L.li][idx]
                                gens.append(stageB(L, n, loc_of(L, n), par))
                            if idx + 1 < HCH:
                                for L in lanes:
                                    n = orders[L.li][idx + 1]
                                    gens.append(stageA(L, n, loc_of(L, n), 1 - par))
                            drain(gens)
                    b_Y = None
                    Y2 = Yacc[:].rearrange("p c (h x) -> p (c h) x", h=2)
                    AX = mybir.AxisListType.X
                    fw.dma(fw.q_sp, gnrow[:, 0, :], bass.AP(rw_gn_g_t, l * 384 + fc * 128, [[0, 128], [1, 128]]),
                           writes=b_gn)
                    fw.dma(fw.q_sp, gnrow[:, 1, :], bass.AP(rw_gn_b_t, l * 384 + fc * 128, [[0, 128], [1, 128]]),
                           writes=b_gn)
                    fw.dma(fw.q_sp, gsb[:], rg_d[:, fc, :], writes=b_gsb)
                    fw.dma(fw.q_sp, bsb_[:], rbon_d[:, fc, :], writes=b_gsb)
                    V_ = lambda fn, r, w: fw.op(fw.dve, fn, reads=r, writes=w)
                    V_(lambda: nc.vector.tensor_reduce(out=st1[:], in_=Y2, axis=AX, op=ALU.add), b_Yc, b_st12)
                    V_(lambda: nc.vector.tensor_tensor(out=ysq[:], in0=Y2, in1=Y2, op=ALU.mult), b_Yc, b_ysq)
                    V_(lambda: nc.vector.tensor_reduce(out=st2[:], in_=ysq[:], axis=AX, op=ALU.add), b_ysq, b_st12)
                    V_(lambda: nc.vector.tensor_scalar(out=st1[:], in0=st1[:], scalar1=1.0 / 64, scalar2=None,
                                                       op0=ALU.mult), [], b_st12)
                    V_(lambda: nc.vector.tensor_tensor(out=st3[:], in0=st1[:], in1=st1[:], op=ALU.mult),
                       b_st12, b_st12)
                    V_(lambda: nc.vector.scalar_tensor_tensor(out=st2[:], in0=st2[:], scalar=1.0 / 64, in1=st3[:],
                                                              op0=ALU.mult, op1=ALU.subtract), b_ysq, b_st12)
                    V_(lambda: nc.vector.tensor_scalar(out=st2[:], in0=st2[:], scalar1=64e-5, scalar2=None, op0=ALU.add),
                       [], b_st12)
                    fw.op(fw.act, lambda: nc.scalar.activation(out=st2[:], in_=st2[:], func=AF.Sqrt), reads=[], writes=b_st12)
                    V_(lambda: nc.vector.reciprocal(out=st2[:], in_=st2[:]), [], b_st12)
                    V_(lambda: nc.vector.tensor_tensor(out=Y2, in0=Y2, in1=st1[:].unsqueeze(2).to_broadcast([128, NCH * 2, 64]),
                                                       op=ALU.subtract), b_st12, b_Yc)
                    V_(lambda: nc.vector.tensor_tensor(out=Y2, in0=Y2, in1=st2[:].unsqueeze(2).to_broadcast([128, NCH * 2, 64]),
                                                       op=ALU.mult), b_st12, b_Yc)
                    V_(lambda: nc.vector.tensor_tensor(out=Yacc[:], in0=Yacc[:],
                                                       in1=gnrow[:, 0, :].unsqueeze(1).to_broadcast([128, NCH, 128]),
                                                       op=ALU.mult), b_gn, b_Yc)
                    V_(lambda: nc.vector.tensor_tensor(out=Yacc[:], in0=Yacc[:],
                                                       in1=gnrow[:, 1, :].unsqueeze(1).to_broadcast([128, NCH, 128]),
                                                       op=ALU.add), b_gn, b_Yc)
                    for n in range(NCH):
                        bk = banks[n % 4]
                        fw.op(fw.pe, lambda: nc.tensor.transpose(bk[:, 0:128], Yacc[:, n, :], identf[:]),
                              reads=[b_Yc[n], b_cst], writes=[bbank[n % 4]])
                        csl = slice(n * 128, (n + 1) * 128)
                        V_(lambda: nc.vector.tensor_tensor(out=bsb_[:, csl], in0=bk[:, 0:128], in1=bsb_[:, csl],
                                                           op=ALU.add), [bbank[n % 4]] + b_gsb, b_gsb)
                    V_(lambda: nc.vector.tensor_tensor(out=rwo[:], in0=bsb_[:], in1=gsb[:], op=ALU.mult), b_gsb, b_rwo)
                    fw.dma(fw.q_sp, mix_d[:, 5 + fc, :], rwo[:], reads=b_rwo)

        cur = x_in
        free = [0, 1, 2]
        stages = []
        for l in range(depth):
            stages.append(("ffn", l, 0))
            stages.append(("mix", l))
            stages.append(("ffn", l, 1))
        if stop_after is not None:
            stages = stages[:stop_after]
        for si, stg in enumerate(stages):
            last = si == len(stages) - 1
            if last:
                dst = y_out
            else:
                di = free.pop(0)
                dst = xs[di]
            if stg[0] == "ffn":
                ffn_pass(stg[1], stg[2], cur, dst)
            else:
                l = stg[1]
                mixin_pass(l, cur)
                conv_pass(l)
                attn_pass(l)
                if not no_rwkv:
                    rwkv_prep(l)
                    rwkv_chunks(l)
                mixout_pass(l, cur, dst)
            for k_, a_ in enumerate(xs):
                if a_ is cur:
                    free.append(k_)
            cur = dst
        fw.finish()
        print("instructions:", fw.ninstr)
    return nc


def to_fm(x2d):
    nt = x2d.shape[0]
    return np.ascontiguousarray(x2d.reshape(nt, 8, 128).transpose(2, 1, 0))


def from_fm(y):
    nt = y.shape[2]
    return np.ascontiguousarray(y.transpose(2, 1, 0).reshape(nt, 1024))


_NC_CACHE = {}


def kernel(x_prompt, x_sample, c_prompt, c_sample, w_ada, b_ada, ln_g, ln_b, ffn_w_in, ffn_w_out,
           w_mix_in, w_mix_out, rel_bias, conv_w, rwkv_mu, rwkv_w0, rwkv_w_up, rwkv_a0, rwkv_a_up,
           rwkv_g_up, rwkv_k_k, rwkv_k_a, rwkv_r_k, rwkv_gn_g, rwkv_gn_b):
    f32 = lambda a: np.ascontiguousarray(np.asarray(a, dtype=np.float32))
    inp = dict(b_ada=b_ada, ln_g=ln_g, ln_b=ln_b, conv_w=conv_w, rwkv_mu=rwkv_mu, rwkv_w0=rwkv_w0, rwkv_a0=rwkv_a0,
               rwkv_k_k=rwkv_k_k, rwkv_k_a=rwkv_k_a, rwkv_r_k=rwkv_r_k, rwkv_gn_g=rwkv_gn_g, rwkv_gn_b=rwkv_gn_b)
    x_prompt = np.asarray(x_prompt, np.float32)
    x_sample = np.asarray(x_sample, np.float32)
    c_prompt = np.asarray(c_prompt, np.float32)
    c_sample = np.asarray(c_sample, np.float32)
    SEG = 4096
    if "nc" not in _NC_CACHE:
        _NC_CACHE["nc"] = build(SEG=SEG)
    nc = _NC_CACHE["nc"]
    oh, jx = static_consts()
    shared = {"pp": pack_pp(inp), "w_ada": f32(w_ada), "ffn_w_in": f32(ffn_w_in), "ffn_w_out": f32(ffn_w_out),
              "w_mix_in": f32(w_mix_in), "w_mix_out": f32(w_mix_out), "rel_bias": f32(rel_bias), "oh": oh, "jx": jx,
              "cst": static_cst(), "rwkv_w_up": f32(rwkv_w_up), "rwkv_a_up": f32(rwkv_a_up),
              "rwkv_g_up": f32(rwkv_g_up), "rwkv_gn_g": f32(rwkv_gn_g), "rwkv_gn_b": f32(rwkv_gn_b)}

    def cT_of(c0, c1):
        return np.ascontiguousarray(np.stack([c0, c1], -1).reshape(8, 128, 2).transpose(1, 0, 2))

    per_core = []
    for b in range(2):
        per_core.append({"x": to_fm(x_prompt[b]), "cT": cT_of(c_prompt[b], c_prompt[b]),
                         "lam": np.ones((128, 1), np.float32)})
    for i in range(2):
        xx = np.concatenate([x_sample[2 * i], x_sample[2 * i + 1]], 0)
        per_core.append({"x": to_fm(xx), "cT": cT_of(c_sample[2 * i], c_sample[2 * i + 1]),
                         "lam": np.zeros((128, 1), np.float32)})
    in_maps = []
    for core in range(8):
        m = dict(shared)
        m.update(per_core[core % 4])
        in_maps.append(m)
    res = run_bass_kernel_spmd(nc, in_maps, core_ids=list(range(8)))
    outs = [from_fm(res.results[c]["y"]) for c in range(4)]
    y_prompt = np.stack([outs[0], outs[1]], 0).astype(np.float32)
    y_sample = np.stack([outs[2][:SEG], outs[2][SEG:], outs[3][:SEG], outs[3][SEG:]], 0).astype(np.float32)
    return (y_prompt, y_sample)
```
